# Optimizing a Trainium2 kernel written in Bass

```python
import jax, jax.numpy as jnp
from jax import lax
import numpy as np

D_MODEL = 2048
BATCH = 1
SEQ = 16384
DEPTH = 2

HGRN_DK = 128
HGRN_DV = 128
HGRN_HEADS = D_MODEL // (2 * HGRN_DV)
GDN_DK = 128
GDN_DV = 128
GDN_HEADS = D_MODEL // (2 * GDN_DV)
CONV_W = 4
CHUNK = 64
HGRN_KW = HGRN_HEADS * HGRN_DK
HGRN_VW = HGRN_HEADS * HGRN_DV
GDN_KW = GDN_HEADS * GDN_DK
GDN_VW = GDN_HEADS * GDN_DV
MIX_W = HGRN_VW + GDN_VW
IN_SPLITS = (HGRN_KW, HGRN_KW, HGRN_VW, HGRN_VW, GDN_KW, GDN_KW, GDN_VW, GDN_HEADS, GDN_HEADS, GDN_VW)
IN_W = sum(IN_SPLITS)
SB_HEADS = 16
SB_DH = D_MODEL // SB_HEADS
Q_BLOCK = 128
D_FF = 4 * D_MODEL
EPS = 1e-6
N_A_LAYERS = (DEPTH + 1) // 2
N_C_LAYERS = DEPTH // 2

kernel_name = "hgrn2_gdn_stickbreaking_hybrid"


def rmsnorm(x, gain):
    x32 = x.astype(jnp.float32)
    y = x32 * lax.rsqrt(jnp.mean(x32 * x32, axis=-1, keepdims=True) + EPS)
    return (y * gain.astype(jnp.float32)).astype(x.dtype)


def gated_rmsnorm(o, gate, weight):
    y = o * lax.rsqrt(jnp.mean(o * o, axis=-1, keepdims=True) + EPS)
    return y * weight.astype(jnp.float32) * jax.nn.silu(gate)


def to_heads(t, n_heads):
    return t.reshape(t.shape[0], t.shape[1], n_heads, -1).transpose(0, 2, 1, 3)


def l2norm(t):
    return t * lax.rsqrt(jnp.sum(t * t, axis=-1, keepdims=True) + EPS)


def causal_conv(u, w):
    T = u.shape[1]
    up = jnp.pad(u, ((0, 0), (CONV_W - 1, 0), (0, 0)))
    return sum(up[:, j:j + T] * w[j] for j in range(CONV_W))


def hgrn2_chunked(q, k, v, log_f):
    B, H, T, dk = q.shape
    dv = v.shape[-1]
    n = T // CHUNK

    def split(t):
        return jnp.moveaxis(t.reshape(B, H, n, CHUNK, t.shape[-1]), 2, 0)

    qc, kc, vc = split(q), split(k), split(v)
    bc = lax.cumsum(split(log_f), axis=3)
    causal = jnp.tril(jnp.ones((CHUNK, CHUNK), dtype=bool))

    def step(S, inp):
        qi, ki, vi, bi = inp
        diff = bi[:, :, :, None, :] - bi[:, :, None, :, :]
        decay = jnp.exp(jnp.where(causal[:, :, None], diff, -jnp.inf))
        attn = jnp.einsum('bhtc,bhsc,bhtsc->bhts', qi, ki, decay)
        o = attn @ vi + jnp.einsum('bhtc,bhcv->bhtv', qi * jnp.exp(bi), S)
        b_last = bi[:, :, -1]
        S = jnp.exp(b_last)[..., None] * S + jnp.einsum(
            'bhsc,bhsv->bhcv', ki * jnp.exp(b_last[:, :, None] - bi), vi)
        return S, o

    S0 = jnp.zeros((B, H, dk, dv), jnp.float32)
    _, o = lax.scan(step, S0, (qc, kc, vc, bc))
    return jnp.moveaxis(o, 0, 2).reshape(B, H, T, dv)


def gated_delta_chunked(q, k, v, g, beta):
    B, H, T, dk = q.shape
    dv = v.shape[-1]
    n = T // CHUNK
    q = q.reshape(B, H, n, CHUNK, dk)
    k = k.reshape(B, H, n, CHUNK, dk)
    v = v.reshape(B, H, n, CHUNK, dv)
    g = g.reshape(B, H, n, CHUNK)
    beta = beta.reshape(B, H, n, CHUNK)
    gc = lax.cumsum(g, axis=3)
    tril = jnp.tril(jnp.ones((CHUNK, CHUNK), dtype=bool))
    strict = jnp.tril(jnp.ones((CHUNK, CHUNK), dtype=bool), -1)
    decay = jnp.exp(jnp.where(tril, gc[..., :, None] - gc[..., None, :], -jnp.inf))
    kb = k * beta[..., None]
    m = jnp.where(strict, jnp.einsum('bhntc,bhnsc->bhnts', kb, k) * decay, 0.0)
    eye = jnp.eye(CHUNK, dtype=m.dtype)
    rhs = jnp.concatenate([v * beta[..., None], kb * jnp.exp(gc)[..., None]], axis=-1)
    sol = lax.linalg.triangular_solve(m + eye, rhs, left_side=True, lower=True, unit_diagonal=True)
    u, w = sol[..., :dv], sol[..., dv:]
    qk = jnp.einsum('bhntc,bhnsc->bhnts', q, k) * decay
    q_dec = q * jnp.exp(gc)[..., None]
    g_last = gc[..., -1]
    k_dec = k * jnp.exp(g_last[..., None] - gc)[..., None]

    def step(S, inp):
        ui, wi, qi, qki, ki, gl = inp
        v_new = ui - wi @ S
        o = qi @ S + qki @ v_new
        S = jnp.exp(gl)[..., None, None] * S + jnp.einsum('bhsc,bhsv->bhcv', ki, v_new)
        return S, o

    xs = (jnp.moveaxis(u, 2, 0), jnp.moveaxis(w, 2, 0), jnp.moveaxis(q_dec, 2, 0),
          jnp.moveaxis(qk, 2, 0), jnp.moveaxis(k_dec, 2, 0), jnp.moveaxis(g_last, 2, 0))
    S0 = jnp.zeros((B, H, dk, dv), jnp.float32)
    _, o = lax.scan(step, S0, xs)
    return jnp.moveaxis(o, 0, 2).reshape(B, H, T, dv)


def hgrn2_gdn_mixer(h, w_in, conv_w, a_log, dt_bias, lb, hgrn_norm, gdn_norm, w_out):
    B, T, _ = h.shape
    proj = (h @ w_in).astype(jnp.float32)
    cuts = np.cumsum(IN_SPLITS)[:-1].tolist()
    hq, hf, hi, hg, gq, gk, gv, ga, gb, gg = jnp.split(proj, cuts, axis=-1)

    lb = lb.astype(jnp.float32)
    f = lb + (1.0 - lb) * jax.nn.sigmoid(hf)
    log_f = jnp.log(f)
    k_a = (1.0 - lb) * jax.nn.sigmoid(-hf)
    q_a = jax.nn.silu(hq)
    o_a = hgrn2_chunked(to_heads(q_a, HGRN_HEADS), to_heads(k_a, HGRN_HEADS),
                        to_heads(hi, HGRN_HEADS), to_heads(log_f, HGRN_HEADS))
    o_a = gated_rmsnorm(o_a.transpose(0, 2, 1, 3), hg.reshape(B, T, HGRN_HEADS, HGRN_DV),
                        hgrn_norm).reshape(B, T, HGRN_VW)

    qkv = jax.nn.silu(causal_conv(jnp.concatenate([gq, gk, gv], axis=-1), conv_w.astype(jnp.float32)))
    cq, ck, cv = jnp.split(qkv, [GDN_KW, 2 * GDN_KW], axis=-1)
    q_b = l2norm(to_heads(cq, GDN_HEADS)) * (GDN_DK ** -0.5)
    k_b = l2norm(to_heads(ck, GDN_HEADS))
    g = -jnp.exp(a_log.astype(jnp.float32)) * jax.nn.softplus(ga + dt_bias.astype(jnp.float32))
    beta = jax.nn.sigmoid(gb)
    o_b = gated_delta_chunked(q_b, k_b, to_heads(cv, GDN_HEADS),
                              g.transpose(0, 2, 1), beta.transpose(0, 2, 1))
    o_b = gated_rmsnorm(o_b.transpose(0, 2, 1, 3), gg.reshape(B, T, GDN_HEADS, GDN_DV),
                        gdn_norm).reshape(B, T, GDN_VW)

    o = jnp.concatenate([o_a, o_b], axis=-1).astype(h.dtype)
    return o @ w_out


def stick_breaking_mixer(h, w_qkv, w_o):
    B, T, _ = h.shape
    qkv = (h @ w_qkv).astype(jnp.float32)
    q, k, v = jnp.split(qkv, 3, axis=-1)
    q, k, v = to_heads(q, SB_HEADS), to_heads(k, SB_HEADS), to_heads(v, SB_HEADS)
    nb = T // Q_BLOCK
    q_blocks = jnp.moveaxis(q.reshape(B, SB_HEADS, nb, Q_BLOCK, SB_DH), 2, 0)
    starts = jnp.arange(nb, dtype=jnp.int32) * Q_BLOCK
    k_pos = jnp.arange(T, dtype=jnp.int32)
    q_off = jnp.arange(Q_BLOCK, dtype=jnp.int32)
    scale = SB_DH ** -0.5

    def block(args):
        qb, start = args
        z = jnp.einsum('bhqd,bhkd->bhqk', qb, k) * scale
        mask = k_pos[None, :] < (start + q_off)[:, None]
        log_1m = jnp.where(mask, -jax.nn.softplus(z), 0.0)
        after = lax.cumsum(log_1m, axis=3, reverse=True) - log_1m
        a = jnp.where(mask, jnp.exp(jax.nn.log_sigmoid(z) + after), 0.0)
        return jnp.einsum('bhqk,bhkd->bhqd', a, v)

    o = lax.map(block, (q_blocks, starts))
    o = jnp.moveaxis(o, 0, 2).reshape(B, SB_HEADS, T, SB_DH).transpose(0, 2, 1, 3)
    return o.reshape(B, T, SB_HEADS * SB_DH).astype(h.dtype) @ w_o


def sqrelu_mlp(h, w1, w2):
    return jnp.square(jax.nn.relu(h @ w1)) @ w2


def setup_inputs(seed: int = 0) -> dict:
    key = jax.random.key(seed)
    ks = jax.random.split(key, 20)

    def dense(k, shape, fan_in):
        return jax.random.normal(k, shape, jnp.float32) * (fan_in ** -0.5)

    def gain(k, shape):
        return 1.0 + 0.02 * jax.random.normal(k, shape, jnp.float32)

    dt = jnp.exp(jax.random.uniform(ks[5], (N_A_LAYERS, GDN_HEADS), jnp.float32,
                                    minval=float(np.log(1e-3)), maxval=float(np.log(1e-1))))
    return {
        "x": jax.random.normal(ks[0], (BATCH, SEQ, D_MODEL), jnp.float32),
        "mix_norm": gain(ks[1], (DEPTH, D_MODEL)),
        "a_w_in": dense(ks[2], (N_A_LAYERS, D_MODEL, IN_W), D_MODEL),
        "a_conv_w": dense(ks[3], (N_A_LAYERS, CONV_W, 2 * GDN_KW + GDN_VW), CONV_W),
        "a_a_log": jnp.log(jax.random.uniform(ks[4], (N_A_LAYERS, GDN_HEADS), jnp.float32,
                                              minval=1.0, maxval=16.0)),
        "a_dt_bias": dt + jnp.log(-jnp.expm1(-dt)),
        "a_lb_logits": 0.1 * jax.random.normal(ks[6], (N_A_LAYERS + 1, HGRN_KW), jnp.float32),
        "a_hgrn_norm": gain(ks[7], (N_A_LAYERS, HGRN_DV)),
        "a_gdn_norm": gain(ks[8], (N_A_LAYERS, GDN_DV)),
        "a_w_out": dense(ks[9], (N_A_LAYERS, MIX_W, D_MODEL), MIX_W),
        "c_w_qkv": dense(ks[10], (N_C_LAYERS, D_MODEL, 3 * SB_HEADS * SB_DH), D_MODEL),
        "c_w_o": dense(ks[11], (N_C_LAYERS, SB_HEADS * SB_DH, D_MODEL), SB_HEADS * SB_DH),
        "mlp_norm": gain(ks[12], (DEPTH, D_MODEL)),
        "mlp_w1": dense(ks[13], (DEPTH, D_MODEL, D_FF), D_MODEL),
        "mlp_w2": dense(ks[14], (DEPTH, D_FF, D_MODEL), D_FF),
        "final_norm": gain(ks[15], (D_MODEL,)),
    }


def reference(x, mix_norm, a_w_in, a_conv_w, a_a_log, a_dt_bias, a_lb_logits, a_hgrn_norm,
              a_gdn_norm, a_w_out, c_w_qkv, c_w_o, mlp_norm, mlp_w1, mlp_w2, final_norm):
    lb_all = lax.cumsum(jax.nn.softmax(a_lb_logits.astype(jnp.float32), axis=0), axis=0)
    for layer in range(DEPTH):
        j = layer // 2
        h = rmsnorm(x, mix_norm[layer])
        if layer % 2 == 0:
            x = x + hgrn2_gdn_mixer(h, a_w_in[j], a_conv_w[j], a_a_log[j], a_dt_bias[j],
                                    lb_all[j], a_hgrn_norm[j], a_gdn_norm[j], a_w_out[j])
        else:
            x = x + stick_breaking_mixer(h, c_w_qkv[j], c_w_o[j])
        h = rmsnorm(x, mlp_norm[layer])
        x = x + sqrelu_mlp(h, mlp_w1[layer], mlp_w2[layer])
    return rmsnorm(x, final_norm)
```

```python
import contextlib
import numpy as np
import concourse.bass as bass
import concourse.mybir as mybir
from concourse.bass_utils import run_bass_kernel_spmd

F32 = mybir.dt.float32
BF16 = mybir.dt.bfloat16
AF = mybir.ActivationFunctionType
ALU = mybir.AluOpType
AX = mybir.AxisListType

ENG_EPOCH = 30000
DMA_EPOCH = 1500


class Prog:
    def __init__(self, name="k"):
        self.nc = bass.Bass("TRN2", target_bir_lowering=False)
        self.stack = contextlib.ExitStack()
        self.lists = {k: [] for k in ("pe", "act", "dve", "pool", "sp")}
        self.count = {k: 0 for k in self.lists}
        self.dcount = {}
        self.dnum = {}
        self.clock = {k: {} for k in self.lists}
        self.lastw = {}
        self.readers = {}
        self.semnames = []
        self.nalloc = 0

    def dram(self, name, shape, dtype, kind):
        return self.nc.dram_tensor(name, list(shape), dtype, kind=kind).ap()

    def sbuf(self, name, shape, dtype):
        return self.stack.enter_context(self.nc.sbuf_tensor(name, list(shape), dtype))

    def psum(self, name, shape, dtype=F32):
        return self.stack.enter_context(self.nc.psum_tensor(name, list(shape), dtype))

    def _esem(self, eng, cnt):
        return "%s_%d" % (eng, (cnt - 1) // ENG_EPOCH)

    def _use(self, sn):
        if sn not in self.semnames:
            self.semnames.append(sn)

    PSUM_KEYS = ("PS", "ZP", "TP", "CP", "OP", "PQ", "PF", "PT")

    def op(self, eng, fn, r=(), w=(), dsem=None, inc=True):
        pr = [k for k in r if isinstance(k, tuple) and k[0] in self.PSUM_KEYS]
        if pr:
            r = [k for k in r if k not in pr]
            w = list(w) + [k for k in pr if k not in w]
        deps = []
        for k in r:
            e = self.lastw.get(k)
            if e is not None:
                deps.append(e)
        for k in w:
            e = self.lastw.get(k)
            if e is not None:
                deps.append(e)
            deps.extend(self.readers.get(k, ()))
        clk = self.clock[eng]
        lst = self.lists[eng]
        for (sn, val, snap, seng) in deps:
            if sn.startswith("d:"):
                val = max(val, self.dcount[sn])
            elif seng == eng and eng == "pe":
                continue
            if clk.get(sn, 0) >= val:
                continue
            lst.append(("w", sn, val))
            clk[sn] = val
            for k2, v2 in snap.items():
                if clk.get(k2, 0) < v2:
                    clk[k2] = v2
        if dsem is not None:
            n = self.dnum.get(dsem, 0)
            self.dnum[dsem] = n + 1
            sn = "d:%s_%d" % (dsem, n // DMA_EPOCH)
            self.dcount[sn] = self.dcount.get(sn, 0) + 16
            val = self.dcount[sn]
            self._use(sn)
            lst.append(("i", fn, sn, 16))
        else:
            if inc:
                self.count[eng] += 1
                c = self.count[eng]
                sn = self._esem(eng, c)
                val = c - ((c - 1) // ENG_EPOCH) * ENG_EPOCH
                self._use(sn)
                lst.append(("i", fn, sn, 1))
            else:
                c = self.count[eng] + 1
                sn = self._esem(eng, c)
                val = c - ((c - 1) // ENG_EPOCH) * ENG_EPOCH
                self._use(sn)
                lst.append(("i", fn, None, 0))
        ev = (sn, val, dict(clk), eng)
        for k in r:
            self.readers.setdefault(k, []).append(ev)
        for k in w:
            self.lastw[k] = ev
            self.readers[k] = []
        return ev

    def finish(self):
        lst = self.lists["sp"]
        for sn, val in self.dcount.items():
            if self.clock["sp"].get(sn, 0) < val:
                lst.append(("w", sn, val))
        nc = self.nc
        sems = {}
        for sn in self.semnames:
            sems[sn] = self.stack.enter_context(nc.semaphore(sn.replace(":", "_")))
        lists = self.lists

        def emit(key, e):
            for it in lists[key]:
                if it[0] == "w":
                    e.wait_ge(sems[it[1]], it[2])
                else:
                    ins = it[1](e)
                    if it[2] is not None:
                        ins.then_inc(sems[it[2]], it[3])

        with nc.Block() as block:
            @block.sync
            def _(e):
                emit("sp", e)

            @block.tensor
            def _(e):
                emit("pe", e)

            @block.scalar
            def _(e):
                emit("act", e)

            @block.vector
            def _(e):
                emit("dve", e)

            @block.gpsimd
            def _(e):
                emit("pool", e)
        self.stack.close()
        return nc

    def stats(self):
        return {k: len(v) for k, v in self.lists.items()}


def _mm(P, out, lhsT, rhs, start, stop, r, w, inc=None):
    P.op("pe", lambda e, out=out, lhsT=lhsT, rhs=rhs, start=start, stop=stop:
         e.matmul(out, lhsT=lhsT, rhs=rhs, start=start, stop=stop),
         r=r, w=w, inc=(stop if inc is None else inc))


def _act(P, out, in_, func, r, w, **kw):
    P.op("act", lambda e, out=out, in_=in_, func=func, kw=kw:
         e.activation(out=out, in_=in_, func=func, **kw), r=r, w=w)


def _dma(P, eng, out, in_, r, w, dsem):
    P.op(eng, lambda e, out=out, in_=in_: e.dma_start(out=out, in_=in_), r=r, w=w, dsem=dsem)


D_MODEL = 2048
KC = 16
EPS = 1e-6
TB = 1024
TT = 512


def build_dense(T, do_oproj, do_mlp, proj_cols, do_final, do_xout):
    P = Prog()
    D = D_MODEL
    xT = P.dram("xT", [D, T], F32, "ExternalInput")
    if do_oproj:
        oT = P.dram("oT", [D, T], F32, "ExternalInput")
        w_o = P.dram("w_o", [D, D], F32, "ExternalInput")
    if do_mlp:
        g_mlp = P.dram("g_mlp", [128, KC], F32, "ExternalInput")
        w1 = P.dram("w1", [D, 4 * D], F32, "ExternalInput")
        w2 = P.dram("w2", [4 * D, D], F32, "ExternalInput")
    if proj_cols:
        g_p = P.dram("g_p", [128, KC], F32, "ExternalInput")
        w_p = P.dram("w_p", [D, proj_cols], F32, "ExternalInput")
        projT = P.dram("projT", [proj_cols, T], F32, "ExternalOutput")
    if do_final:
        g_f = P.dram("g_f", [128, KC], F32, "ExternalInput")
        yT = P.dram("yT", [D, T], F32, "ExternalOutput")
    if do_xout:
        xoT = P.dram("xoT", [D, T], F32, "ExternalOutput")

    X = P.sbuf("X", [128, KC, TB], F32)
    H = P.sbuf("H", [128, KC, TB], BF16)
    WA = [P.sbuf("WA%d" % i, [128, KC, 512], BF16) for i in range(2)]
    if do_mlp:
        WB = [P.sbuf("WB%d" % i, [128, 4, D], BF16) for i in range(2)]
        U = [P.sbuf("U%d" % i, [128, 4, TB], BF16) for i in range(2)]
        TMP = [P.sbuf("TMP%d" % i, [128, TT], F32) for i in range(2)]
    SQ = [P.sbuf("SQ%d" % i, [128, TT], F32) for i in range(2)]
    RS = P.sbuf("RS", [128, TT], F32)
    EV = [P.sbuf("EV%d" % i, [128, TT], F32) for i in range(4)]
    ones = P.sbuf("ones", [128, 128], F32)
    epsT = P.sbuf("epsT", [128, 1], F32)
    G = {}
    PS = [P.psum("ps%d" % i, [128, TT], F32) for i in range(8)]
    st = {"ps": 0, "ev": 0, "tmp": 0}

    def nextps():
        i = st["ps"]
        st["ps"] = (i + 1) % 8
        return PS[i], ("PS", i)

    P.op("pool", lambda e: e.memset(ones[:, :], 1.0), w=["ones"])
    P.op("pool", lambda e: e.memset(epsT[:, :], EPS), w=["eps"])
    for nm, src in (("mlp", g_mlp if do_mlp else None), ("p", g_p if proj_cols else None),
                    ("f", g_f if do_final else None)):
        if src is not None:
            G[nm] = P.sbuf("G" + nm, [128, KC], F32)
            _dma(P, "sp", G[nm][:, :], src[:, :], r=[], w=["G" + nm], dsem="G" + nm)

    def tsl(tt):
        return slice(tt * TT, (tt + 1) * TT)

    def norm_tile(gname, tt, emit):
        ps, pk = nextps()
        for c in range(KC):
            sq = SQ[c % 2]
            _act(P, sq[:, :], X[:, c, tsl(tt)], AF.Square, r=[("X", c, tt)], w=[("SQ", c % 2)])
            _mm(P, ps[:, :], ones[:, :], sq[:, :], c == 0, c == KC - 1,
                r=["ones", ("SQ", c % 2)], w=[pk], inc=True)
        _act(P, RS[:, :], ps[:, :], AF.Sqrt, r=[pk, "eps"], w=["RS"], scale=1.0 / D_MODEL, bias=epsT[:, 0:1])
        P.op("dve", lambda e: e.reciprocal(out=RS[:, :], in_=RS[:, :]), r=["RS"], w=["RS"])
        for c in range(KC):
            emit(c)

    def norm_to_H(gname):
        Gt = G[gname]
        for tt in range(TB // TT):
            def emit(c, tt=tt):
                P.op("dve", lambda e, c=c, tt=tt: e.scalar_tensor_tensor(
                    out=H[:, c, tsl(tt)], in0=X[:, c, tsl(tt)], scalar=Gt[:, c:c + 1], in1=RS[:, :],
                    op0=ALU.mult, op1=ALU.mult),
                    r=[("X", c, tt), "RS", "G" + gname], w=[("H", c, tt)])
            norm_tile(gname, tt, emit)

    wa_i = [0]

    def load_wa(view, c0, wd):
        i = wa_i[0]
        wa_i[0] = (i + 1) % 2
        wa = WA[i]
        _dma(P, "pool", wa[:, :, 0:wd], view[:, :, c0:c0 + wd], r=[], w=[("WA", i)], dsem="WA%d" % i)
        return wa, ("WA", i)

    for blk in range(T // TB):
        t0 = blk * TB
        for c in range(KC):
            _dma(P, "sp", X[:, c, :], xT[c * 128:(c + 1) * 128, t0:t0 + TB], r=[],
                 w=[("X", c, 0), ("X", c, 1)], dsem="X%d" % (c % 4))
        if do_oproj:
            for c in range(KC):
                _dma(P, "pool", H[:, c, :], oT[c * 128:(c + 1) * 128, t0:t0 + TB], r=[],
                     w=[("H", c, 0), ("H", c, 1)], dsem="H%d" % (c % 4))
            wv = w_o.rearrange("(k p) n -> p k n", p=128)
            for g in range(D // 512):
                wa, wk = load_wa(wv, g * 512, 512)
                for j in range(4):
                    dc = g * 4 + j
                    for tt in range(TB // TT):
                        ps, pk = nextps()
                        for k in range(KC):
                            _mm(P, ps[:, :], wa[:, k, j * 128:(j + 1) * 128], H[:, k, tsl(tt)], k == 0, k == KC - 1,
                                r=[wk, ("H", k, tt)], w=[pk])
                        P.op("dve", lambda e, ps=ps, dc=dc, tt=tt: e.tensor_tensor(
                            out=X[:, dc, tsl(tt)], in0=ps[:, :], in1=X[:, dc, tsl(tt)], op=ALU.add),
                            r=[pk, ("X", dc, tt)], w=[("X", dc, tt)])
        if do_mlp:
            norm_to_H("mlp")
            w1v = w1.rearrange("(k p) n -> p k n", p=128)
            w2v = w2.rearrange("(j p) n -> p j n", p=128)
            NG = 4 * D // 512

            def first(g):
                wa, wk = load_wa(w1v, g * 512, 512)
                u = U[g % 2]
                for j in range(4):
                    for tt in range(TB // TT):
                        ps, pk = nextps()
                        for k in range(KC):
                            _mm(P, ps[:, :], wa[:, k, j * 128:(j + 1) * 128], H[:, k, tsl(tt)], k == 0, k == KC - 1,
                                r=[wk, ("H", k, tt)], w=[pk])
                        ti = st["tmp"]
                        st["tmp"] = (ti + 1) % 2
                        _act(P, TMP[ti][:, :], ps[:, :], AF.Relu, r=[pk], w=[("TMP", ti)])
                        _act(P, u[:, j, tsl(tt)], TMP[ti][:, :], AF.Square, r=[("TMP", ti)], w=[("U", g % 2, j, tt)])

            def second(g):
                wb = WB[g % 2]
                u = U[g % 2]
                _dma(P, "pool", wb[:, :, :], w2v[:, g * 4:(g + 1) * 4, :], r=[], w=[("WB", g % 2)], dsem="WB%d" % (g % 2))
                for dc in range(KC):
                    for tt in range(TB // TT):
                        ps, pk = nextps()
                        for j in range(4):
                            _mm(P, ps[:, :], wb[:, j, dc * 128:(dc + 1) * 128], u[:, j, tsl(tt)], j == 0, j == 3,
                                r=[("WB", g % 2), ("U", g % 2, j, tt)], w=[pk])
                        P.op("dve", lambda e, ps=ps, dc=dc, tt=tt: e.tensor_tensor(
                            out=X[:, dc, tsl(tt)], in0=ps[:, :], in1=X[:, dc, tsl(tt)], op=ALU.add),
                            r=[pk, ("X", dc, tt)], w=[("X", dc, tt)])

            first(0)
            for g in range(NG):
                if g + 1 < NG:
                    first(g + 1)
                second(g)
        if do_xout:
            for c in range(KC):
                _dma(P, "sp", xoT[c * 128:(c + 1) * 128, t0:t0 + TB], X[:, c, :],
                     r=[("X", c, 0), ("X", c, 1)], w=[], dsem="XO%d" % (c % 4))
        if proj_cols:
            norm_to_H("p")
            wv = w_p.rearrange("(k p) n -> p k n", p=128)
            c0 = 0
            while c0 < proj_cols:
                wd = min(512, proj_cols - c0)
                wa, wk = load_wa(wv, c0, wd)
                off = 0
                while off < wd:
                    cw = min(128, wd - off)
                    for tt in range(TB // TT):
                        ps, pk = nextps()
                        for k in range(KC):
                            _mm(P, ps[0:cw, :], wa[:, k, off:off + cw], H[:, k, tsl(tt)], k == 0, k == KC - 1,
                                r=[wk, ("H", k, tt)], w=[pk])
                        ei = st["ev"]
                        st["ev"] = (ei + 1) % 4
                        _act(P, EV[ei][0:cw, :], ps[0:cw, :], AF.Copy, r=[pk], w=[("EV", ei)])
                        _dma(P, "sp", projT[c0 + off:c0 + off + cw, t0 + tt * TT:t0 + (tt + 1) * TT], EV[ei][0:cw, :],
                             r=[("EV", ei)], w=[], dsem="EV%d" % ei)
                    off += cw
                c0 += wd
        if do_final:
            Gt = G["f"]
            for tt in range(TB // TT):
                def emit(c, tt=tt):
                    ei = st["ev"]
                    st["ev"] = (ei + 1) % 4
                    P.op("dve", lambda e, c=c, tt=tt, ei=ei: e.scalar_tensor_tensor(
                        out=EV[ei][:, :], in0=X[:, c, tsl(tt)], scalar=Gt[:, c:c + 1], in1=RS[:, :],
                        op0=ALU.mult, op1=ALU.mult),
                        r=[("X", c, tt), "RS", "Gf"], w=[("EV", ei)])
                    _dma(P, "sp", yT[c * 128:(c + 1) * 128, t0 + tt * TT:t0 + (tt + 1) * TT], EV[ei][:, :],
                         r=[("EV", ei)], w=[], dsem="EV%d" % ei)
                norm_tile("f", tt, emit)
    return P


def build_sb(T, NH, dbg=None):
    P = Prog()
    dbgo = {}
    if dbg:
        for nm in ("dQ", "dE", "dS", "dTP", "dA", "dV"):
            dbgo[nm] = P.dram(nm, [128, 512], F32, "ExternalOutput")
        DB = P.sbuf("DB", [128, 512], F32)

    def dump(nm, src, key):
        if not dbg:
            return
        P.op("dve", lambda e: e.tensor_copy(out=DB[:, :], in_=src), r=[key], w=["DB"])
        _dma(P, "sp", dbgo[nm][:, :], DB[:, :], r=["DB"], w=[], dsem="DB")

    QT = 512
    nq = T // QT
    nkb = T // 128
    qT = P.dram("qT", [NH, 128, T], F32, "ExternalInput")
    kT = P.dram("kT", [NH, 128, T], F32, "ExternalInput")
    vv = P.dram("v", [NH, T, 128], F32, "ExternalInput")
    cU = P.dram("cU", [128, 128], F32, "ExternalInput")
    cM = P.dram("cM", [4, 128, QT], F32, "ExternalInput")
    oT = P.dram("oT", [NH, 128, T], F32, "ExternalOutput")

    Q = P.sbuf("Q", [128, T], BF16)
    Kt = P.sbuf("Kt", [128, T], BF16)
    V = P.sbuf("V", [128, nkb, 128], BF16)
    QS = [P.sbuf("QS%d" % i, [128, 2048], F32) for i in range(2)]
    Un = P.sbuf("Un", [128, 128], BF16)
    On = P.sbuf("On", [128, 128], BF16)
    MK = P.sbuf("MK", [128, 4, QT], BF16)
    one1 = P.sbuf("one1", [128, 1], F32)
    E = [P.sbuf("E%d" % i, [128, QT], F32) for i in range(2)]
    S = [P.sbuf("S%d" % i, [128, QT], BF16) for i in range(3)]
    TS = [P.sbuf("TS%d" % i, [128, QT], F32) for i in range(2)]
    A = [P.sbuf("A%d" % i, [128, QT], BF16) for i in range(3)]
    CS = [P.sbuf("CS%d" % i, [128, QT], F32) for i in range(2)]
    OE = [P.sbuf("OE%d" % i, [128, QT], F32) for i in range(2)]
    ZP = [P.psum("zp%d" % i, [128, QT]) for i in range(2)]
    TP = [P.psum("tp%d" % i, [128, QT]) for i in range(2)]
    CP = [P.psum("cp%d" % i, [128, QT]) for i in range(2)]
    OP = [P.psum("op%d" % i, [128, QT]) for i in range(2)]

    _dma(P, "pool", Un[:, :], cU[:, :], r=[], w=["Un"], dsem="c0")
    _dma(P, "pool", MK[:, :, :], cM.rearrange("j p q -> p j q"), r=[], w=["MK"], dsem="c1")
    P.op("pool", lambda e: e.memset(On[:, :], 1.0), w=["On"])
    P.op("pool", lambda e: e.memset(one1[:, :], 1.0), w=["one1"])
    scale = 128.0 ** -0.5
    cnt = {"e": 0, "s": 0, "t": 0, "a": 0, "c": 0, "z": 0, "tp": 0}

    for h in range(NH):
        for i in range(T // 2048):
            qs = QS[i % 2]
            _dma(P, "sp", qs[:, :], qT[h, :, i * 2048:(i + 1) * 2048], r=[], w=[("QS", i % 2)], dsem="QS%d" % (i % 2))
            P.op("dve", lambda e, qs=qs, i=i: e.tensor_scalar(out=Q[:, i * 2048:(i + 1) * 2048], in0=qs[:, :], scalar1=scale,
                                                              scalar2=None, op0=ALU.mult),
                 r=[("QS", i % 2)], w=["Q"])
        _dma(P, "pool", Kt[:, :], kT[h, :, :], r=[], w=["Kt"], dsem="Kt")
        v_src = vv[h].rearrange("(b p) d -> p b d", p=128)
        for b0 in range(0, nkb, 16):
            b1 = min(nkb, b0 + 16)
            _dma(P, "pool", V[:, b0:b1, :], v_src[:, b0:b1, :], r=[], w=["V"], dsem="V")

        for qt in range(nq):
            q0 = qt * QT
            qsl = slice(q0, q0 + QT)
            kbs = list(range(4 * qt + 3, -1, -1))
            n = len(kbs)
            cp = CP[qt % 2]
            ck = ("CP", qt % 2)
            op_ = OP[qt % 2]
            ok = ("OP", qt % 2)
            info = {}

            def stageA(i):
                kb = kbs[i]
                zi = cnt["z"] % 2
                cnt["z"] += 1
                ei = cnt["e"] % 2
                cnt["e"] += 1
                si = cnt["s"] % 3
                cnt["s"] += 1
                ksl = slice(kb * 128, (kb + 1) * 128)
                _mm(P, ZP[zi][:, :], Kt[:, ksl], Q[:, qsl], True, True, r=["Kt", "Q"], w=[("ZP", zi)])
                _act(P, E[ei][:, :], ZP[zi][:, :], AF.Exp, r=[("ZP", zi)], w=[("E", ei)])
                _act(P, S[si][:, :], E[ei][:, :], AF.Ln, r=[("E", ei), "one1"], w=[("S", si)], bias=one1[:, 0:1])
                j = kb - 4 * qt
                if j >= 0:
                    P.op("dve", lambda e, si=si, j=j: e.tensor_tensor(out=S[si][:, :], in0=S[si][:, :], in1=MK[:, j, :], op=ALU.mult),
                         r=[("S", si), "MK"], w=[("S", si)])
                info[i] = (kb, si, ksl, j)
                if dbg and (h, qt, i) == dbg:
                    dump("dQ", Q[:, qsl], "Q")
                    dump("dE", E[ei][:, :], ("E", ei))
                    dump("dS", S[si][:, :], ("S", si))
                    dump("dV", V[:, kb:kb + 4, :].rearrange("p a b -> p (a b)"), "V")

            def stageB(i):
                kb, si, ksl, j = info[i]
                ti = cnt["tp"] % 2
                cnt["tp"] += 1
                _mm(P, TP[ti][:, :], Kt[:, ksl], Q[:, qsl], True, False, r=["Kt", "Q"], w=[("TP", ti)], inc=False)
                _mm(P, TP[ti][:, :], Un[:, :], S[si][:, :], False, True, r=["Un", ("S", si)], w=[("TP", ti)])
                src = TP[ti]
                srck = ("TP", ti)
                if i > 0:
                    ci = cnt["c"] % 2
                    cnt["c"] += 1
                    P.op("dve", lambda e, ci=ci, cp=cp: e.tensor_copy(out=CS[ci][:, :], in_=cp[:, :]), r=[ck], w=[("CS", ci)])
                    tsi = cnt["t"] % 2
                    cnt["t"] += 1
                    P.op("dve", lambda e, tsi=tsi, ti=ti, ci=ci: e.tensor_tensor(out=TS[tsi][:, :], in0=TP[ti][:, :], in1=CS[ci][:, :],
                                                                                  op=ALU.subtract),
                         r=[("TP", ti), ("CS", ci)], w=[("TS", tsi)])
                    src = TS[tsi]
                    srck = ("TS", tsi)
                if i < n - 1:
                    _mm(P, cp[:, :], On[:, :], S[si][:, :], i == 0, i == n - 2, r=["On", ("S", si)], w=[ck], inc=True)
                info[i] = (kb, si, ksl, j, src, srck)
                if dbg and (h, qt, i) == dbg:
                    dump("dTP", src[:, :], srck)

            def stageC(i):
                kb, si, ksl, j, src, srck = info[i]
                ai = cnt["a"] % 3
                cnt["a"] += 1
                _act(P, A[ai][:, :], src[:, :], AF.Exp, r=[srck], w=[("A", ai)])
                if j >= 0:
                    P.op("dve", lambda e, ai=ai, j=j: e.tensor_tensor(out=A[ai][:, :], in0=A[ai][:, :], in1=MK[:, j, :], op=ALU.mult),
                         r=[("A", ai), "MK"], w=[("A", ai)])
                _mm(P, op_[:, :], V[:, kb, :], A[ai][:, :], i == 0, i == n - 1, r=["V", ("A", ai)], w=[ok], inc=True)
                if dbg and (h, qt, i) == dbg:
                    dump("dA", A[ai][:, :], ("A", ai))

            for step in range(n + 2):
                if step < n:
                    stageA(step)
                if 0 <= step - 1 < n:
                    stageB(step - 1)
                if 0 <= step - 2 < n:
                    stageC(step - 2)
            oe = OE[qt % 2]
            P.op("dve", lambda e, oe=oe, op_=op_: e.tensor_copy(out=oe[:, :], in_=op_[:, :]), r=[ok], w=[("OE", qt % 2)])
            _dma(P, "sp", oT[h, :, qsl], oe[:, :], r=[("OE", qt % 2)], w=[], dsem="OE%d" % (qt % 2))
    return P


def sb_consts():
    j = np.arange(128)[:, None]
    k = np.arange(128)[None, :]
    cU = np.where(j >= k, -1.0, 0.0).astype(np.float32)
    kl = np.arange(128)[None, :, None]
    ql = np.arange(512)[None, None, :]
    jj = np.arange(4)[:, None, None]
    cM = ((128 * jj + kl) < ql).astype(np.float32)
    return {"cU": cU, "cM": np.ascontiguousarray(cM)}


CH = 64
ST = 512


def mix_consts():
    p = np.arange(128)
    same = (p[:, None] // CH) == (p[None, :] // CH)
    incl = (same & (p[:, None] <= p[None, :])).astype(np.float32)
    strict = (same & (p[:, None] < p[None, :])).astype(np.float32)
    ident = np.eye(128, dtype=np.float32)
    seg = np.ones((128, ST), np.float32)
    seg[:, ::CH] = 0.0
    sel = (p[:, None] == ((p[None, :] // CH) * CH + CH - 1)).astype(np.float32)
    return {"c_incl": incl, "c_strict": strict, "c_ident": ident, "c_ltri": incl.copy(), "c_seg": seg, "c_sel": sel}


def build_mix(T, lvl=99):
    P = Prog()
    NP = T // 128
    hin = P.dram("hin", [3, 128, T], F32, "ExternalInput")
    hi_tok = P.dram("hi_tok", [T, 128], F32, "ExternalInput")
    lbl = P.dram("lbl", [128, 2], F32, "ExternalInput")
    hnorm = P.dram("hnorm", [128, 1], F32, "ExternalInput")
    gin = P.dram("gin", [4, 128, T], F32, "ExternalInput")
    gab = P.dram("gab", [2, T], F32, "ExternalInput")
    gabc = P.dram("gabc", [2, 128, NP], F32, "ExternalInput")
    convw = P.dram("convw", [128, 12], F32, "ExternalInput")
    sc = P.dram("sc", [128, 2], F32, "ExternalInput")
    gnorm = P.dram("gnorm", [128, 1], F32, "ExternalInput")
    c_incl = P.dram("c_incl", [128, 128], F32, "ExternalInput")
    c_strict = P.dram("c_strict", [128, 128], F32, "ExternalInput")
    c_ident = P.dram("c_ident", [128, 128], F32, "ExternalInput")
    c_ltri = P.dram("c_ltri", [128, 128], F32, "ExternalInput")
    c_seg = P.dram("c_seg", [128, ST], F32, "ExternalInput")
    out = P.dram("out", [2, 128, T], F32, "ExternalOutput")

    def sb(name, shape, dt=F32):
        return P.sbuf(name, shape, dt)

    Mincl = sb("Mincl", [128, 128]); Mstr = sb("Mstr", [128, 128]); Id = sb("Id", [128, 128])
    Ltri = sb("Ltri", [128, 128]); Seg = sb("Seg", [128, ST]); ones = sb("ones", [128, 128])
    epsT = sb("epsT", [128, 1]); one1 = sb("one1", [128, 1])
    LBL = sb("LBL", [128, 2]); LB = sb("LB", [128, 1]); OML = sb("OML", [128, 1]); HN = sb("HN", [128, 1])
    CW = sb("CW", [128, 12]); SC = sb("SC", [128, 2]); NEGA = sb("NEGA", [128, 1]); GN = sb("GN", [128, 1])
    for t_, d_, k_ in ((Mincl, c_incl, "Mincl"), (Mstr, c_strict, "Mstr"), (Id, c_ident, "Id"), (Ltri, c_ltri, "Ltri"),
                       (Seg, c_seg, "Seg"), (LBL, lbl, "LBL"), (HN, hnorm, "HN"), (CW, convw, "CW"), (SC, sc, "SC"),
                       (GN, gnorm, "GN")):
        _dma(P, "sp", t_[:, :], d_[:, :], r=[], w=[k_], dsem="c_" + k_)
    P.op("pool", lambda e: e.memset(ones[:, :], 1.0), w=["ones"])
    P.op("pool", lambda e: e.memset(epsT[:, :], EPS), w=["eps"])
    P.op("pool", lambda e: e.memset(one1[:, :], 1.0), w=["one1"])

    def tt(eng, out_, a, b, op, r, w):
        P.op(eng, lambda e, out_=out_, a=a, b=b, op=op: e.tensor_tensor(out=out_, in0=a, in1=b, op=op), r=r, w=w)

    def ts(eng, out_, a, s1, s2, op0, op1, r, w):
        if s2 is None:
            P.op(eng, lambda e, out_=out_, a=a, s1=s1, op0=op0: e.tensor_scalar(out=out_, in0=a, scalar1=s1, scalar2=None, op0=op0),
                 r=r, w=w)
        else:
            P.op(eng, lambda e, out_=out_, a=a, s1=s1, s2=s2, op0=op0, op1=op1:
                 e.tensor_scalar(out=out_, in0=a, scalar1=s1, scalar2=s2, op0=op0, op1=op1), r=r, w=w)

    def stt(out_, a, s, b, op0, op1, r, w):
        P.op("dve", lambda e, out_=out_, a=a, s=s, b=b, op0=op0, op1=op1:
             e.scalar_tensor_tensor(out=out_, in0=a, scalar=s, in1=b, op0=op0, op1=op1), r=r, w=w)

    def cp(eng, out_, a, r, w):
        if eng == "act":
            _act(P, out_, a, AF.Copy, r=r, w=w)
        else:
            P.op(eng, lambda e, out_=out_, a=a: e.tensor_copy(out=out_, in_=a), r=r, w=w)

    def tr(out_, a, r, w):
        P.op("pe", lambda e, out_=out_, a=a: e.transpose(out_, a, Idb[:, :]), r=r + ["Idb"], w=w)

    PSB = [P.psum("psb%d" % i, [128, 512]) for i in range(7)]
    PST = P.psum("psbT", [128, 1024], BF16)
    pst = {"q": 0, "f": 0, "t": 0}
    Idb = sb("Idb", [128, 128], BF16)
    cp("pool", Idb[:, :], Id[:, :], r=["Id"], w=["Idb"])

    def ptq():
        return PST[:, 0:128], ("PT", 0)

    def pq():
        i = pst["q"]
        pst["q"] = (i + 1) % 5
        return PSB[i][:, 0:128], ("PQ", i)

    def pf():
        i = pst["f"]
        pst["f"] = (i + 1) % 2
        return PSB[5 + i][:, :], ("PF", i)

    tt("dve", LB[:, :], LBL[:, 0:1], LBL[:, 1:2], ALU.subtract, r=["LBL"], w=["LB"])
    _act(P, LB[:, :], LB[:, :], AF.Sigmoid, r=["LB"], w=["LB"])
    ts("dve", OML[:, :], LB[:, :], -1.0, 1.0, ALU.mult, ALU.add, r=["LB"], w=["OML"])
    _act(P, NEGA[:, :], SC[:, 0:1], AF.Exp, r=["SC"], w=["NEGA"])
    ts("dve", NEGA[:, :], NEGA[:, :], -1.0, None, ALU.mult, None, r=["NEGA"], w=["NEGA"])

    GAC = sb("GAC", [128, NP]); GBC = sb("GBC", [128, NP]); GCC = sb("GCC", [128, NP])
    BEC = sb("BEC", [128, NP]); KBEC = sb("KBEC", [128, NP]); KDC = sb("KDC", [128, NP])
    _dma(P, "sp", GAC[:, :], gabc[0], r=[], w=["GAC"], dsem="c_gac")
    _dma(P, "sp", GBC[:, :], gabc[1], r=[], w=["GBC"], dsem="c_gbc")
    _act(P, GAC[:, :], GAC[:, :], AF.Exp, r=["GAC", "SC"], w=["GAC"], bias=SC[:, 1:2])
    _act(P, GAC[:, :], GAC[:, :], AF.Ln, r=["GAC", "one1"], w=["GAC"], bias=one1[:, 0:1])
    ts("dve", GAC[:, :], GAC[:, :], NEGA[:, 0:1], None, ALU.mult, None, r=["GAC", "NEGA"], w=["GAC"])
    _act(P, BEC[:, :], GBC[:, :], AF.Sigmoid, r=["GBC"], w=["BEC"])
    for n0 in range(0, NP, 512):
        n1 = min(NP, n0 + 512)
        ps, pk = pf()
        _mm(P, ps[:, 0:n1 - n0], Ltri[:, :], GAC[:, n0:n1], True, True, r=["Ltri", "GAC"], w=[pk])
        cp("dve", GCC[:, n0:n1], ps[:, 0:n1 - n0], r=[pk], w=["GCC"])
    _act(P, KBEC[:, :], GCC[:, :], AF.Exp, r=["GCC"], w=["KBEC"])
    tt("dve", KBEC[:, :], KBEC[:, :], BEC[:, :], ALU.mult, r=["KBEC", "BEC"], w=["KBEC"])
    SEL = sb("SEL", [128, 128])
    c_sel = P.dram("c_sel", [128, 128], F32, "ExternalInput")
    _dma(P, "sp", SEL[:, :], c_sel[:, :], r=[], w=["SEL"], dsem="c_sel")
    for n0 in range(0, NP, 512):
        n1 = min(NP, n0 + 512)
        ps, pk = pf()
        _mm(P, ps[:, 0:n1 - n0], SEL[:, :], GCC[:, n0:n1], True, True, r=["SEL", "GCC"], w=[pk])
        tt("dve", KDC[:, n0:n1], ps[:, 0:n1 - n0], GCC[:, n0:n1], ALU.subtract, r=[pk, "GCC"], w=["KDC"])
    _act(P, KDC[:, :], KDC[:, :], AF.Exp, r=["KDC"], w=["KDC"])

    if lvl == 0:
        return P
    HV = sb("HV", [128, NP, 128], BF16)
    hv_src = hi_tok.rearrange("(b p) d -> p b d", p=128)
    for b0 in range(0, NP, 16):
        b1 = min(NP, b0 + 16)
        _dma(P, "pool", HV[:, b0:b1, :], hv_src[:, b0:b1, :], r=[], w=["HV"], dsem="HV")

    SAf = sb("SAf", [128, 128]); SAb = sb("SAb", [128, 128], BF16)
    SBf = sb("SBf", [128, 128]); SBb = sb("SBb", [128, 128], BF16)
    for t_, k_ in ((SAf, "SAf"), (SAb, "SAb"), (SBf, "SBf"), (SBb, "SBb")):
        P.op("pool", lambda e, t_=t_: e.memset(t_[:, :], 0.0), w=[k_])

    def T512(name, dt=F32, w=ST):
        return sb(name, [128, w], dt)
    HQ = T512("HQ"); HF = T512("HF"); HG = T512("HG")
    T1 = T512("T1"); T2 = T512("T2"); T3 = T512("T3"); T4 = T512("T4"); T5 = T512("T5")
    QBh = T512("QBh", BF16); KBh = T512("KBh", BF16)
    OA = T512("OA"); OSQ = T512("OSQ"); ORS = T512("ORS")
    UQ = T512("UQ", F32, ST + 3); UK = T512("UK", F32, ST + 3); UV = T512("UV", F32, ST + 3); GG = T512("GG")
    GAb = T512("GAb"); GBb = T512("GBb")
    CQ = T512("CQ"); CK = T512("CK"); CV = T512("CV"); RQ = T512("RQ")
    GC = T512("GC"); EGC = T512("EGC"); BETA = T512("BETA")
    QDb = T512("QDb", BF16); KNb = T512("KNb", BF16); QNb = T512("QNb", BF16); CVb = T512("CVb", BF16)
    OB = T512("OB")
    def T128(name, dt=F32):
        return sb(name, [128, 128], dt)
    AM = T128("AM", BF16); K2 = T128("K2", BF16); K2t = T128("K2t", BF16)
    EXD = T128("EXD"); DECI = T128("DECI"); DECS = T128("DECS")
    ATf = T128("ATf"); ATb = T128("ATb", BF16); Ab = T128("Ab", BF16)
    Pb = [T128("Pb%d" % i, BF16) for i in range(2)]; PTb = [T128("PTb%d" % i, BF16) for i in range(2)]
    TTf = T128("TTf"); TTb = T128("TTb", BF16)
    KBEt = T128("KBEt", BF16); KDt = T128("KDt", BF16); VBt = T128("VBt", BF16)
    Ut = T128("Ut"); WTb = T128("WTb", BF16); VN = T128("VN", BF16); QKT = T128("QKT", BF16)

    def gated_out(O, gate_src_key, G_in, W_col, wkey, oi, t0):
        ok_ = "O%d" % oi
        _act(P, OSQ[:, :], O[:, :], AF.Square, r=[ok_], w=["OSQ"])
        ps, pk = pf()
        _mm(P, ps, ones[:, :], OSQ[:, :], True, True, r=["ones", "OSQ"], w=[pk])
        _act(P, ORS[:, :], ps, AF.Sqrt, r=[pk, "eps"], w=["ORS"], scale=1.0 / 128, bias=epsT[:, 0:1])
        P.op("dve", lambda e: e.reciprocal(out=ORS[:, :], in_=ORS[:, :]), r=["ORS"], w=["ORS"])
        stt(O[:, :], O[:, :], W_col[:, 0:1], ORS[:, :], ALU.mult, ALU.mult, r=[ok_, "ORS", wkey], w=[ok_])
        _act(P, G_in[:, :], G_in[:, :], AF.Silu, r=[gate_src_key], w=[gate_src_key])
        tt("dve", O[:, :], O[:, :], G_in[:, :], ALU.mult, r=[ok_, gate_src_key], w=[ok_])
        _dma(P, "sp", out[oi, :, t0:t0 + ST], O[:, :], r=[ok_], w=[], dsem="out%d" % oi)

    for st_i in range(T // ST):
        t0 = st_i * ST
        _dma(P, "sp", HQ[:, :], hin[0, :, t0:t0 + ST], r=[], w=["HQ"], dsem="HQ")
        _dma(P, "sp", HF[:, :], hin[1, :, t0:t0 + ST], r=[], w=["HF"], dsem="HF")
        _dma(P, "sp", HG[:, :], hin[2, :, t0:t0 + ST], r=[], w=["HG"], dsem="HG")
        _act(P, T1[:, :], HF[:, :], AF.Sigmoid, r=["HF"], w=["T1"])
        ts("dve", T1[:, :], T1[:, :], OML[:, 0:1], LB[:, 0:1], ALU.mult, ALU.add, r=["T1", "OML", "LB"], w=["T1"])
        _act(P, T2[:, :], T1[:, :], AF.Ln, r=["T1"], w=["T2"])
        ts("dve", T1[:, :], T1[:, :], -1.0, 1.0, ALU.mult, ALU.add, r=["T1"], w=["T1"])
        P.op("dve", lambda e: e.tensor_tensor_scan(out=T3[:, :], data0=Seg[:, :], data1=T2[:, :], initial=0.0,
                                                   op0=ALU.mult, op1=ALU.add), r=["Seg", "T2"], w=["T3"])
        _act(P, T2[:, :], T3[:, :], AF.Exp, r=["T3"], w=["T2"])
        _act(P, T4[:, :], T3[:, :], AF.Exp, r=["T3"], w=["T4"], scale=-1.0)
        _act(P, T5[:, :], HQ[:, :], AF.Silu, r=["HQ"], w=["T5"])
        tt("dve", QBh[:, :], T5[:, :], T2[:, :], ALU.mult, r=["T5", "T2"], w=["QBh"])
        tt("dve", T1[:, :], T1[:, :], T4[:, :], ALU.mult, r=["T1", "T4"], w=["T1"])
        cp("pool", KBh[:, :], T1[:, :], r=["T1"], w=["KBh"])
        if lvl == 1:
            return P
        for U_, gi, k_ in ((UQ, 0, "UQ"), (UK, 1, "UK"), (UV, 2, "UV")):
            if t0 == 0:
                P.op("pool", lambda e, U_=U_: e.memset(U_[:, 0:3], 0.0), w=[k_])
                _dma(P, "sp", U_[:, 3:ST + 3], gin[gi, :, 0:ST], r=[], w=[k_], dsem=k_)
            else:
                _dma(P, "sp", U_[:, :], gin[gi, :, t0 - 3:t0 + ST], r=[], w=[k_], dsem=k_)
        _dma(P, "sp", GG[:, :], gin[3, :, t0:t0 + ST], r=[], w=["GG"], dsem="GG")
        _dma(P, "sp", GAb[:, :], gab[0:1, t0:t0 + ST].partition_broadcast(128), r=[], w=["GAb"], dsem="GAb")
        _dma(P, "sp", GBb[:, :], gab[1:2, t0:t0 + ST].partition_broadcast(128), r=[], w=["GBb"], dsem="GBb")
        for U_, C_, k_, ck_, wi in ((UQ, CQ, "UQ", "CQ", 0), (UK, CK, "UK", "CK", 4), (UV, CV, "UV", "CV", 8)):
            ts("dve", C_[:, :], U_[:, 0:ST], CW[:, wi:wi + 1], None, ALU.mult, None, r=[k_, "CW"], w=[ck_])
            for j in (1, 2, 3):
                stt(C_[:, :], U_[:, j:j + ST], CW[:, wi + j:wi + j + 1], C_[:, :], ALU.mult, ALU.add, r=[k_, "CW", ck_], w=[ck_])
            _act(P, C_[:, :], C_[:, :], AF.Silu, r=[ck_], w=[ck_])
        for C_, ck_, scl in ((CQ, "CQ", 128.0 ** -0.5), (CK, "CK", 1.0)):
            _act(P, OSQ[:, :], C_[:, :], AF.Square, r=[ck_], w=["OSQ"])
            ps, pk = pf()
            _mm(P, ps, ones[:, :], OSQ[:, :], True, True, r=["ones", "OSQ"], w=[pk])
            _act(P, RQ[:, :], ps, AF.Sqrt, r=[pk, "eps"], w=["RQ"], bias=epsT[:, 0:1])
            P.op("dve", lambda e: e.reciprocal(out=RQ[:, :], in_=RQ[:, :]), r=["RQ"], w=["RQ"])
            stt(C_[:, :], C_[:, :], scl, RQ[:, :], ALU.mult, ALU.mult, r=[ck_, "RQ"], w=[ck_])
        _act(P, GAb[:, :], GAb[:, :], AF.Exp, r=["GAb", "SC"], w=["GAb"], bias=SC[:, 1:2])
        _act(P, GAb[:, :], GAb[:, :], AF.Ln, r=["GAb", "one1"], w=["GAb"], bias=one1[:, 0:1])
        ts("dve", GAb[:, :], GAb[:, :], NEGA[:, 0:1], None, ALU.mult, None, r=["GAb", "NEGA"], w=["GAb"])
        P.op("dve", lambda e: e.tensor_tensor_scan(out=GC[:, :], data0=Seg[:, :], data1=GAb[:, :], initial=0.0,
                                                   op0=ALU.mult, op1=ALU.add), r=["Seg", "GAb"], w=["GC"])
        _act(P, EGC[:, :], GC[:, :], AF.Exp, r=["GC"], w=["EGC"])
        _act(P, BETA[:, :], GBb[:, :], AF.Sigmoid, r=["GBb"], w=["BETA"])
        tt("dve", QDb[:, :], CQ[:, :], EGC[:, :], ALU.mult, r=["CQ", "EGC"], w=["QDb"])
        cp("pool", KNb[:, :], CK[:, :], r=["CK"], w=["KNb"])
        cp("pool", QNb[:, :], CQ[:, :], r=["CQ"], w=["QNb"])
        cp("pool", CVb[:, :], CV[:, :], r=["CV"], w=["CVb"])

        if lvl == 2:
            return P
        for pi in range(ST // 128):
            pn = st_i * (ST // 128) + pi
            c0 = pi * 128
            csl = slice(c0, c0 + 128)
            ps, pk = pq()
            _mm(P, ps, KBh[:, csl], QBh[:, csl], True, True, r=["KBh", "QBh"], w=[pk])
            tt("dve", AM[:, :], ps, Mincl[:, :], ALU.mult, r=[pk, "Mincl"], w=["AM"])
            if lvl == 20:
                return P
            for half in (0, 1):
                hs = slice(half * 64, half * 64 + 64)
                ts("dve", K2[:, hs], T1[:, c0 + half * 64:c0 + half * 64 + 64], T2[:, c0 + half * 64 + 63:c0 + half * 64 + 64], None,
                   ALU.mult, None, r=["T1", "T2"], w=["K2"])
            if lvl == 21:
                return P
            ps2, pk2 = ptq()
            tr(ps2, K2[:, :], r=["K2"], w=[pk2])
            if lvl == 22:
                return P
            cp("act", K2t[:, :], ps2, r=[pk2], w=["K2t"])
            if lvl == 30:
                return P
            pso, pko = pq()
            for half in ((0,) if lvl == 31 else (0, 1)):
                hs = slice(half * 64, half * 64 + 64)
                tsl_ = slice(c0 + half * 64, c0 + half * 64 + 64)
                _mm(P, pso[:, hs], HV[hs, pn, :], AM[hs, hs], True, False, r=["HV", "AM"], w=[pko], inc=False)
                _mm(P, pso[:, hs], SAb[:, :], QBh[:, tsl_], False, True, r=["SAb", "QBh"], w=[pko])
                psu, pku = pq()
                _mm(P, psu, K2t[hs, :], HV[hs, pn, :], True, True, r=["K2t", "HV"], w=[pku])
                stt(SAf[:, :], SAf[:, :], T2[:, c0 + half * 64 + 63:c0 + half * 64 + 64], psu, ALU.mult, ALU.add,
                    r=["SAf", "T2", pku], w=["SAf"])
                cp("act", SAb[:, :], SAf[:, :], r=["SAf"], w=["SAb"])
            if lvl in (31, 32):
                return P
            cp("act", OA[:, csl], pso, r=[pko], w=["O0"])
            if lvl == 3:
                return P
            ts("dve", EXD[:, :], GC[:, csl], GCC[:, pn:pn + 1], 0.0, ALU.subtract, ALU.min, r=["GC", "GCC"], w=["EXD"])
            _act(P, EXD[:, :], EXD[:, :], AF.Exp, r=["EXD"], w=["EXD"])
            tt("pool", DECI[:, :], EXD[:, :], Mincl[:, :], ALU.mult, r=["EXD", "Mincl"], w=["DECI"])
            tt("pool", DECS[:, :], EXD[:, :], Mstr[:, :], ALU.mult, r=["EXD", "Mstr"], w=["DECS"])
            tt("pool", DECS[:, :], DECS[:, :], BETA[:, csl], ALU.mult, r=["DECS", "BETA"], w=["DECS"])
            psk, pkk = pq()
            _mm(P, psk, KNb[:, csl], KNb[:, csl], True, True, r=["KNb"], w=[pkk])
            stt(ATf[:, :], psk, -1.0, DECS[:, :], ALU.mult, ALU.mult, r=[pkk, "DECS"], w=["ATf"])
            cp("act", ATb[:, :], ATf[:, :], r=["ATf"], w=["ATb"])
            psq, pkq = pq()
            _mm(P, psq, KNb[:, csl], QNb[:, csl], True, True, r=["KNb", "QNb"], w=[pkq])
            tt("dve", QKT[:, :], psq, DECI[:, :], ALU.mult, r=[pkq, "DECI"], w=["QKT"])
            pst_, pkt = ptq()
            tr(pst_, ATb[:, :], r=["ATb"], w=[pkt])
            cp("act", Ab[:, :], pst_, r=[pkt], w=["Ab"])
            tt("dve", TTf[:, :], ATf[:, :], Id[:, :], ALU.add, r=["ATf", "Id"], w=["TTf"])
            cp("act", TTb[:, :], TTf[:, :], r=["TTf"], w=["TTb"])
            Xc, XTc, xk, xtk = ATb, Ab, "ATb", "Ab"
            for k in range(1, 6):
                Xn, XTn = Pb[k % 2], PTb[k % 2]
                xnk, xtnk = "Pb%d" % (k % 2), "PTb%d" % (k % 2)
                p1, k1 = pq()
                _mm(P, p1, XTc[:, :], Xc[:, :], True, True, r=[xtk, xk], w=[k1])
                cp("act", Xn[:, :], p1, r=[k1], w=[xnk])
                if k < 5:
                    p2, k2 = pq()
                    _mm(P, p2, Xc[:, :], XTc[:, :], True, True, r=[xk, xtk], w=[k2])
                    cp("dve", XTn[:, :], p2, r=[k2], w=[xtnk])
                if k == 5:
                    p2, k2 = pq()
                    _mm(P, p2, Xc[:, :], XTc[:, :], True, True, r=[xk, xtk], w=[k2])
                    cp("dve", XTn[:, :], p2, r=[k2], w=[xtnk])
                p3, k3 = pq()
                _mm(P, p3, XTn[:, :], TTb[:, :], True, True, r=[xtnk, "TTb"], w=[k3])
                tt("dve", TTf[:, :], TTf[:, :], p3, ALU.add, r=["TTf", k3], w=["TTf"])
                cp("act", TTb[:, :], TTf[:, :], r=["TTf"], w=["TTb"])
                Xc, XTc, xk, xtk = Xn, XTn, xnk, xtnk
            if lvl == 4:
                return P
            pkt_, kkt = ptq()
            tr(pkt_, KNb[:, csl], r=["KNb"], w=[kkt])
            ts("dve", KBEt[:, :], pkt_, KBEC[:, pn:pn + 1], None, ALU.mult, None, r=[kkt, "KBEC"], w=["KBEt"])
            ts("dve", KDt[:, :], pkt_, KDC[:, pn:pn + 1], None, ALU.mult, None, r=[kkt, "KDC"], w=["KDt"])
            pvt_, kvt = ptq()
            tr(pvt_, CVb[:, csl], r=["CVb"], w=[kvt])
            ts("dve", VBt[:, :], pvt_, BEC[:, pn:pn + 1], None, ALU.mult, None, r=[kvt, "BEC"], w=["VBt"])
            pu, ku = pq()
            _mm(P, pu, TTb[:, :], VBt[:, :], True, True, r=["TTb", "VBt"], w=[ku])
            cp("act", Ut[:, :], pu, r=[ku], w=["Ut"])
            pw, kw = pq()
            _mm(P, pw, KBEt[:, :], TTb[:, :], True, True, r=["KBEt", "TTb"], w=[kw])
            cp("act", WTb[:, :], pw, r=[kw], w=["WTb"])
            if lvl == 5:
                return P
            psob, pkob = pq()
            for half in (0, 1):
                hs = slice(half * 64, half * 64 + 64)
                tsl_ = slice(c0 + half * 64, c0 + half * 64 + 64)
                pws, kws = pq()
                _mm(P, pws[hs, :], WTb[:, hs], SBb[:, :], True, True, r=["WTb", "SBb"], w=[kws])
                tt("dve", VN[hs, :], Ut[hs, :], pws[hs, :], ALU.subtract, r=["Ut", kws], w=["VN"])
                _mm(P, psob[:, hs], SBb[:, :], QDb[:, tsl_], True, False, r=["SBb", "QDb"], w=[pkob], inc=False)
                _mm(P, psob[:, hs], VN[hs, :], QKT[hs, hs], False, True, r=["VN", "QKT"], w=[pkob])
                pss, kss = pq()
                _mm(P, pss, KDt[hs, :], VN[hs, :], True, True, r=["KDt", "VN"], w=[kss])
                stt(SBf[:, :], SBf[:, :], EGC[:, c0 + half * 64 + 63:c0 + half * 64 + 64], pss, ALU.mult, ALU.add,
                    r=["SBf", "EGC", kss], w=["SBf"])
                cp("act", SBb[:, :], SBf[:, :], r=["SBf"], w=["SBb"])
            cp("act", OB[:, csl], psob, r=[pkob], w=["O1"])
        if lvl == 6:
            return P
        gated_out(OA, "HG", HG, HN, "HN", 0, t0)
        gated_out(OB, "GG", GG, GN, "GN", 1, t0)
    return P


def mix_core_inputs(projT, h, conv_w, a_log, dt_bias, lb_logits, hgrn_norm, gdn_norm):
    T = projT.shape[1]
    r = lambda base: projT[base + h * 128: base + (h + 1) * 128]
    d = {}
    d["hin"] = np.ascontiguousarray(np.stack([r(0), r(1024), r(3072)]))
    d["hi_tok"] = np.ascontiguousarray(r(2048).T)
    d["lbl"] = np.ascontiguousarray(lb_logits[:, h * 128:(h + 1) * 128].T)
    d["hnorm"] = np.ascontiguousarray(hgrn_norm.reshape(128, 1))
    d["gin"] = np.ascontiguousarray(np.stack([r(4096), r(5120), r(6144), r(7184)]))
    ga = projT[7168 + h]
    gb = projT[7176 + h]
    d["gab"] = np.ascontiguousarray(np.stack([ga, gb]))
    d["gabc"] = np.ascontiguousarray(np.stack([ga.reshape(T // 128, 128).T, gb.reshape(T // 128, 128).T]))
    cw = np.concatenate([conv_w[:, h * 128:(h + 1) * 128].T, conv_w[:, 1024 + h * 128:1024 + (h + 1) * 128].T,
                         conv_w[:, 2048 + h * 128:2048 + (h + 1) * 128].T], axis=1)
    d["convw"] = np.ascontiguousarray(cw)
    d["sc"] = np.ascontiguousarray(np.stack([np.full(128, a_log[h], np.float32), np.full(128, dt_bias[h], np.float32)], axis=1))
    d["gnorm"] = np.ascontiguousarray(gdn_norm.reshape(128, 1))
    d.update(mix_consts())
    return d


NCORES = 8
SEQ = 16384
TCORE = SEQ // NCORES


def _gl(g):
    return np.ascontiguousarray(np.asarray(g, np.float32).reshape(KC, 128).T)


def _run(P, in_maps):
    nc = P.finish()
    res = run_bass_kernel_spmd(nc, in_maps, core_ids=list(range(NCORES)))
    return res.results


def kernel(x, mix_norm, a_w_in, a_conv_w, a_a_log, a_dt_bias, a_lb_logits, a_hgrn_norm,
           a_gdn_norm, a_w_out, c_w_qkv, c_w_o, mlp_norm, mlp_w1, mlp_w2, final_norm):
    f = lambda a: np.ascontiguousarray(np.asarray(a, dtype=np.float32))
    x = f(x)
    xT = np.ascontiguousarray(x[0].T)
    tsl = lambda c: slice(c * TCORE, (c + 1) * TCORE)

    P = build_dense(TCORE, False, False, 8208, False, False)
    w_in = f(a_w_in[0])
    g0 = _gl(mix_norm[0])
    res = _run(P, [{"xT": np.ascontiguousarray(xT[:, tsl(c)]), "g_p": g0, "w_p": w_in} for c in range(NCORES)])
    projT = np.concatenate([r["projT"] for r in res], axis=1)

    P = build_mix(SEQ)
    ins = [mix_core_inputs(projT, h, f(a_conv_w[0]), f(a_a_log[0]), f(a_dt_bias[0]), f(a_lb_logits),
                           f(a_hgrn_norm[0]), f(a_gdn_norm[0])) for h in range(NCORES)]
    res = _run(P, ins)
    oT = np.concatenate([r["out"][0] for r in res] + [r["out"][1] for r in res], axis=0)
    del projT

    P = build_dense(TCORE, True, True, 6144, False, True)
    cw = {"w_o": f(a_w_out[0]), "g_mlp": _gl(mlp_norm[0]), "w1": f(mlp_w1[0]), "w2": f(mlp_w2[0]),
          "g_p": _gl(mix_norm[1]), "w_p": f(c_w_qkv[0])}
    res = _run(P, [dict(cw, xT=np.ascontiguousarray(xT[:, tsl(c)]), oT=np.ascontiguousarray(oT[:, tsl(c)])) for c in range(NCORES)])
    xT = np.concatenate([r["xoT"] for r in res], axis=1)
    qkvT = np.concatenate([r["projT"] for r in res], axis=1)

    P = build_sb(SEQ, 2)
    sbc = sb_consts()
    ins = []
    for c in range(NCORES):
        hs = (2 * c, 2 * c + 1)
        d = {"qT": np.ascontiguousarray(np.stack([qkvT[h * 128:(h + 1) * 128] for h in hs])),
             "kT": np.ascontiguousarray(np.stack([qkvT[2048 + h * 128:2048 + (h + 1) * 128] for h in hs])),
             "v": np.ascontiguousarray(np.stack([qkvT[4096 + h * 128:4096 + (h + 1) * 128].T for h in hs]))}
        d.update(sbc)
        ins.append(d)
    res = _run(P, ins)
    oT = np.concatenate([r["oT"].reshape(256, SEQ) for r in res], axis=0)
    del qkvT

    P = build_dense(TCORE, True, True, 0, True, False)
    cw = {"w_o": f(c_w_o[0]), "g_mlp": _gl(mlp_norm[1]), "w1": f(mlp_w1[1]), "w2": f(mlp_w2[1]), "g_f": _gl(final_norm)}
    res = _run(P, [dict(cw, xT=np.ascontiguousarray(xT[:, tsl(c)]), oT=np.ascontiguousarray(oT[:, tsl(c)])) for c in range(NCORES)])
    yT = np.concatenate([r["yT"] for r in res], axis=1)
    return np.ascontiguousarray(yT.T)[None].astype(np.float32)
```

```python
import contextlib
import numpy as np
import concourse.bass as bass
import concourse.mybir as mybir
from concourse.bass_utils import run_bass_kernel_spmd

F32 = mybir.dt.float32
BF16 = mybir.dt.bfloat16
AF = mybir.ActivationFunctionType
ALU = mybir.AluOpType
AX = mybir.AxisListType

ENG_EPOCH = 30000
DMA_EPOCH = 1500


class Prog:
    def __init__(self, name="k"):
        self.nc = bass.Bass("TRN2", target_bir_lowering=False)
        self.stack = contextlib.ExitStack()
        self.lists = {k: [] for k in ("pe", "act", "dve", "pool", "sp")}
        self.count = {k: 0 for k in self.lists}
        self.dcount = {}
        self.dnum = {}
        self.clock = {k: {} for k in self.lists}
        self.lastw = {}
        self.readers = {}
        self.semnames = []
        self.nalloc = 0
        self.pstack = None
        self.sems = {}

    def dram(self, name, shape, dtype, kind, **kw):
        return self.nc.dram_tensor(name, list(shape), dtype, kind=kind, **kw).ap()

    def sbuf(self, name, shape, dtype):
        self.nalloc += 1
        st = self.pstack if self.pstack is not None else self.stack
        return st.enter_context(self.nc.sbuf_tensor("%s_%d" % (name, self.nalloc), list(shape), dtype))

    def psum(self, name, shape, dtype=F32):
        self.nalloc += 1
        st = self.pstack if self.pstack is not None else self.stack
        return st.enter_context(self.nc.psum_tensor("%s_%d" % (name, self.nalloc), list(shape), dtype))

    def begin_phase(self):
        self.pstack = contextlib.ExitStack()

    def end_phase(self):
        self.barrier()
        self._emit_block()
        self.pstack.close()
        self.pstack = None

    def barrier(self):
        targets = {}
        for eng, c in self.count.items():
            if c > 0:
                sn = self._esem(eng, c)
                targets[sn] = (c - ((c - 1) // ENG_EPOCH) * ENG_EPOCH, eng)
        for sn, val in self.dcount.items():
            targets[sn] = (val, None)
        for eng in self.lists:
            clk = self.clock[eng]
            for sn, (val, seng) in targets.items():
                if clk.get(sn, 0) >= val:
                    continue
                self.lists[eng].append(("w", sn, val))
                clk[sn] = val
        self.lastw = {}
        self.readers = {}

    def _emit_block(self):
        nc = self.nc
        for sn in self.semnames:
            if sn not in self.sems:
                self.sems[sn] = self.stack.enter_context(nc.semaphore(sn.replace(":", "_")))
        sems = self.sems
        lists = self.lists

        def emit(key, e):
            for it in lists[key]:
                if it[0] == "w":
                    e.wait_ge(sems[it[1]], it[2])
                else:
                    ins = it[1](e)
                    if it[2] is not None:
                        ins.then_inc(sems[it[2]], it[3])

        with nc.Block() as block:
            @block.sync
            def _(e):
                emit("sp", e)

            @block.tensor
            def _(e):
                emit("pe", e)

            @block.scalar
            def _(e):
                emit("act", e)

            @block.vector
            def _(e):
                emit("dve", e)

            @block.gpsimd
            def _(e):
                emit("pool", e)
        self.lists = {k: [] for k in lists}

    def _esem(self, eng, cnt):
        return "%s_%d" % (eng, (cnt - 1) // ENG_EPOCH)

    def _use(self, sn):
        if sn not in self.semnames:
            self.semnames.append(sn)

    PSUM_KEYS = ("PS", "ZP", "TP", "CP", "OP", "PQ", "PF", "PT")

    def op(self, eng, fn, r=(), w=(), dsem=None, inc=True, dinc=16):
        pr = [k for k in r if isinstance(k, tuple) and k[0] in self.PSUM_KEYS]
        if pr:
            r = [k for k in r if k not in pr]
            w = list(w) + [k for k in pr if k not in w]
        deps = []
        for k in r:
            e = self.lastw.get(k)
            if e is not None:
                deps.append(e)
        for k in w:
            e = self.lastw.get(k)
            if e is not None:
                deps.append(e)
            deps.extend(self.readers.get(k, ()))
        clk = self.clock[eng]
        lst = self.lists[eng]
        for (sn, val, snap, seng) in deps:
            if sn.startswith("d:"):
                val = max(val, self.dcount[sn])
            elif seng == eng and eng == "pe":
                continue
            if clk.get(sn, 0) >= val:
                continue
            lst.append(("w", sn, val))
            clk[sn] = val
            for k2, v2 in snap.items():
                if clk.get(k2, 0) < v2:
                    clk[k2] = v2
        if dsem is not None:
            dsem = "%s_%s" % (dsem, eng)
            n = self.dnum.get(dsem, 0)
            self.dnum[dsem] = n + 1
            sn = "d:%s_%d" % (dsem, n // DMA_EPOCH)
            self.dcount[sn] = self.dcount.get(sn, 0) + dinc
            val = self.dcount[sn]
            self._use(sn)
            lst.append(("i", fn, sn, dinc))
        else:
            if inc:
                self.count[eng] += 1
                c = self.count[eng]
                sn = self._esem(eng, c)
                val = c - ((c - 1) // ENG_EPOCH) * ENG_EPOCH
                self._use(sn)
                lst.append(("i", fn, sn, 1))
            else:
                c = self.count[eng] + 1
                sn = self._esem(eng, c)
                val = c - ((c - 1) // ENG_EPOCH) * ENG_EPOCH
                self._use(sn)
                lst.append(("i", fn, None, 0))
        ev = (sn, val, dict(clk), eng)
        for k in r:
            self.readers.setdefault(k, []).append(ev)
        for k in w:
            self.lastw[k] = ev
            self.readers[k] = []
        return ev

    def finish(self):
        if any(self.lists.values()):
            self.barrier()
            self._emit_block()
        if self.pstack is not None:
            self.pstack.close()
            self.pstack = None
        self.stack.close()
        return self.nc

    def stats(self):
        return {k: len(v) for k, v in self.lists.items()}


def _mm(P, out, lhsT, rhs, start, stop, r, w, inc=None):
    P.op("pe", lambda e, out=out, lhsT=lhsT, rhs=rhs, start=start, stop=stop:
         e.matmul(out, lhsT=lhsT, rhs=rhs, start=start, stop=stop),
         r=r, w=w, inc=(stop if inc is None else inc))


def _act(P, out, in_, func, r, w, **kw):
    P.op("act", lambda e, out=out, in_=in_, func=func, kw=kw:
         e.activation(out=out, in_=in_, func=func, **kw), r=r, w=w)


def _dma(P, eng, out, in_, r, w, dsem):
    P.op(eng, lambda e, out=out, in_=in_: e.dma_start(out=out, in_=in_), r=r, w=w, dsem=dsem)


def _dmaf(P, eng, out, in_fn, r, w, dsem, slow=False):
    if slow:
        P.op(eng, lambda e, out=out, in_fn=in_fn: e.dma_start(out=out, in_=in_fn(e), allow_slow_non_contiguous=True),
             r=r, w=w, dsem=dsem)
    else:
        P.op(eng, lambda e, out=out, in_fn=in_fn: e.dma_start(out=out, in_=in_fn(e)), r=r, w=w, dsem=dsem)


D_MODEL = 2048
KC = 16
EPS = 1e-6
TB = 1024
TT = 512


def dense_body(P, T, do_oproj, do_mlp, proj_cols, do_final, do_xout, io):
    D = D_MODEL
    TB = min(1024, T)
    w_o = io.get("w_o"); g_mlp = io.get("g_mlp"); w1 = io.get("w1"); w2 = io.get("w2")
    g_p = io.get("g_p"); w_p = io.get("w_p"); g_f = io.get("g_f")

    X = P.sbuf("X", [128, KC, TB], F32)
    H = P.sbuf("H", [128, KC, TB], BF16)
    WA = [P.sbuf("WA%d" % i, [128, KC, 512], BF16) for i in range(2)]
    if do_mlp:
        WB = [P.sbuf("WB%d" % i, [128, 4, D], BF16) for i in range(2)]
        U = [P.sbuf("U%d" % i, [128, 4, TB], BF16) for i in range(2)]
        TMP = [P.sbuf("TMP%d" % i, [128, TT], F32) for i in range(2)]
    SQ = [P.sbuf("SQ%d" % i, [128, TT], F32) for i in range(2)]
    RS = P.sbuf("RS", [128, TT], F32)
    EV = [P.sbuf("EV%d" % i, [128, TT], F32) for i in range(4)]
    ones = P.sbuf("ones", [128, 128], F32)
    epsT = P.sbuf("epsT", [128, 1], F32)
    G = {}
    PS = [P.psum("ps%d" % i, [128, TT], F32) for i in range(8)]
    st = {"ps": 0, "ev": 0, "tmp": 0}

    def nextps():
        i = st["ps"]
        st["ps"] = (i + 1) % 8
        return PS[i], ("PS", i)

    P.op("pool", lambda e: e.memset(ones[:, :], 1.0), w=["ones"])
    P.op("pool", lambda e: e.memset(epsT[:, :], EPS), w=["eps"])
    for nm, src in (("mlp", g_mlp if do_mlp else None), ("p", g_p if proj_cols else None),
                    ("f", g_f if do_final else None)):
        if src is not None:
            G[nm] = P.sbuf("G" + nm, [128, KC], F32)
            _dma(P, "sp", G[nm][:, :], src[:, :], r=[], w=["G" + nm], dsem="G" + nm)

    def tsl(tt):
        return slice(tt * TT, (tt + 1) * TT)

    def norm_tile(gname, tt, emit):
        ps, pk = nextps()
        for c in range(KC):
            sq = SQ[c % 2]
            _act(P, sq[:, :], X[:, c, tsl(tt)], AF.Square, r=[("X", c, tt)], w=[("SQ", c % 2)])
            _mm(P, ps[:, :], ones[:, :], sq[:, :], c == 0, c == KC - 1,
                r=["ones", ("SQ", c % 2)], w=[pk], inc=True)
        _act(P, RS[:, :], ps[:, :], AF.Sqrt, r=[pk, "eps"], w=["RS"], scale=1.0 / D_MODEL, bias=epsT[:, 0:1])
        P.op("dve", lambda e: e.reciprocal(out=RS[:, :], in_=RS[:, :]), r=["RS"], w=["RS"])
        for c in range(KC):
            emit(c)

    def norm_to_H(gname):
        Gt = G[gname]
        for tt in range(TB // TT):
            def emit(c, tt=tt):
                P.op("dve", lambda e, c=c, tt=tt: e.scalar_tensor_tensor(
                    out=H[:, c, tsl(tt)], in0=X[:, c, tsl(tt)], scalar=Gt[:, c:c + 1], in1=RS[:, :],
                    op0=ALU.mult, op1=ALU.mult),
                    r=[("X", c, tt), "RS", "G" + gname], w=[("H", c, tt)])
            norm_tile(gname, tt, emit)

    wa_i = [0]

    def load_wa(view, c0, wd):
        i = wa_i[0]
        wa_i[0] = (i + 1) % 2
        wa = WA[i]
        _dma(P, "pool", wa[:, :, 0:wd], view[:, :, c0:c0 + wd], r=[], w=[("WA", i)], dsem="WA%d" % i)
        return wa, ("WA", i)

    for blk in range(T // TB):
        t0 = blk * TB
        for c in range(KC):
            _dma(P, "sp", X[:, c, :], io["x"](c, t0, TB), r=io.get("x_keys", []),
                 w=[("X", c, 0), ("X", c, 1)], dsem="X%d" % (c % 2))
        if do_oproj:
            for c in range(KC):
                _dmaf(P, "pool", H[:, c, :], io["o"](c, t0, TB), r=io.get("o_keys", []),
                      w=[("H", c, 0), ("H", c, 1)], dsem="H%d" % (c % 2))
            wv = w_o.rearrange("(k p) n -> p k n", p=128)
            for g in range(D // 512):
                wa, wk = load_wa(wv, g * 512, 512)
                for j in range(4):
                    dc = g * 4 + j
                    for tt in range(TB // TT):
                        ps, pk = nextps()
                        for k in range(KC):
                            _mm(P, ps[:, :], wa[:, k, j * 128:(j + 1) * 128], H[:, k, tsl(tt)], k == 0, k == KC - 1,
                                r=[wk, ("H", k, tt)], w=[pk])
                        P.op("dve", lambda e, ps=ps, dc=dc, tt=tt: e.tensor_tensor(
                            out=X[:, dc, tsl(tt)], in0=ps[:, :], in1=X[:, dc, tsl(tt)], op=ALU.add),
                            r=[pk, ("X", dc, tt)], w=[("X", dc, tt)])
        if do_mlp:
            norm_to_H("mlp")
            w1v = w1.rearrange("(k p) n -> p k n", p=128)
            w2v = w2.rearrange("(j p) n -> p j n", p=128)
            NG = 4 * D // 512

            def first(g):
                wa, wk = load_wa(w1v, g * 512, 512)
                u = U[g % 2]
                for j in range(4):
                    for tt in range(TB // TT):
                        ps, pk = nextps()
                        for k in range(KC):
                            _mm(P, ps[:, :], wa[:, k, j * 128:(j + 1) * 128], H[:, k, tsl(tt)], k == 0, k == KC - 1,
                                r=[wk, ("H", k, tt)], w=[pk])
                        ti = st["tmp"]
                        st["tmp"] = (ti + 1) % 2
                        _act(P, TMP[ti][:, :], ps[:, :], AF.Relu, r=[pk], w=[("TMP", ti)])
                        _act(P, u[:, j, tsl(tt)], TMP[ti][:, :], AF.Square, r=[("TMP", ti)], w=[("U", g % 2, j, tt)])

            def second(g):
                wb = WB[g % 2]
                u = U[g % 2]
                _dma(P, "pool", wb[:, :, :], w2v[:, g * 4:(g + 1) * 4, :], r=[], w=[("WB", g % 2)], dsem="WB%d" % (g % 2))
                for dc in range(KC):
                    for tt in range(TB // TT):
                        ps, pk = nextps()
                        for j in range(4):
                            _mm(P, ps[:, :], wb[:, j, dc * 128:(dc + 1) * 128], u[:, j, tsl(tt)], j == 0, j == 3,
                                r=[("WB", g % 2), ("U", g % 2, j, tt)], w=[pk])
                        P.op("dve", lambda e, ps=ps, dc=dc, tt=tt: e.tensor_tensor(
                            out=X[:, dc, tsl(tt)], in0=ps[:, :], in1=X[:, dc, tsl(tt)], op=ALU.add),
                            r=[pk, ("X", dc, tt)], w=[("X", dc, tt)])

            first(0)
            for g in range(NG):
                if g + 1 < NG:
                    first(g + 1)
                second(g)
        if do_xout:
            for c in range(KC):
                _dma(P, "sp", io["x_dst"](c, t0, TB), X[:, c, :],
                     r=[("X", c, 0), ("X", c, 1)], w=io.get("x_dst_keys", []), dsem="XO")
        if proj_cols:
            norm_to_H("p")
            wv = w_p.rearrange("(k p) n -> p k n", p=128)
            c0 = 0
            while c0 < proj_cols:
                wd = min(512, proj_cols - c0)
                wa, wk = load_wa(wv, c0, wd)
                off = 0
                while off < wd:
                    cw = min(128, wd - off)
                    for tt in range(TB // TT):
                        ps, pk = nextps()
                        for k in range(KC):
                            _mm(P, ps[0:cw, :], wa[:, k, off:off + cw], H[:, k, tsl(tt)], k == 0, k == KC - 1,
                                r=[wk, ("H", k, tt)], w=[pk])
                        ei = st["ev"]
                        st["ev"] = (ei + 1) % 4
                        _act(P, EV[ei][0:cw, :], ps[0:cw, :], AF.Copy, r=[pk], w=[("EV", ei)])
                        _dma(P, "sp", io["proj_dst"](c0 + off, cw, t0 + tt * TT, TT), EV[ei][0:cw, :],
                             r=[("EV", ei)], w=io.get("proj_keys", []), dsem="EV%d" % ei)
                    off += cw
                c0 += wd
                if "proj_group_done" in io:
                    io["proj_group_done"](c0, blk == T // TB - 1)
        if do_final:
            Gt = G["f"]
            for tt in range(TB // TT):
                def emit(c, tt=tt):
                    ei = st["ev"]
                    st["ev"] = (ei + 1) % 4
                    P.op("dve", lambda e, c=c, tt=tt, ei=ei: e.scalar_tensor_tensor(
                        out=EV[ei][:, :], in0=X[:, c, tsl(tt)], scalar=Gt[:, c:c + 1], in1=RS[:, :],
                        op0=ALU.mult, op1=ALU.mult),
                        r=[("X", c, tt), "RS", "Gf"], w=[("EV", ei)])
                    _dma(P, "sp", io["y_dst"](c, t0 + tt * TT, TT), EV[ei][:, :],
                         r=[("EV", ei)], w=[], dsem="EV%d" % ei)
                norm_tile("f", tt, emit)


def build_dense(T, do_oproj, do_mlp, proj_cols, do_final, do_xout):
    P = Prog()
    D = D_MODEL
    io = {}
    xT = P.dram("xT", [D, T], F32, "ExternalInput")
    io["x"] = lambda c, t0, n: xT[c * 128:(c + 1) * 128, t0:t0 + n]
    if do_oproj:
        oT = P.dram("oT", [D, T], F32, "ExternalInput")
        io["w_o"] = P.dram("w_o", [D, D], F32, "ExternalInput")
        io["o"] = lambda c, t0, n: (lambda e: oT[c * 128:(c + 1) * 128, t0:t0 + n])
    if do_mlp:
        io["g_mlp"] = P.dram("g_mlp", [128, KC], F32, "ExternalInput")
        io["w1"] = P.dram("w1", [D, 4 * D], F32, "ExternalInput")
        io["w2"] = P.dram("w2", [4 * D, D], F32, "ExternalInput")
    if proj_cols:
        io["g_p"] = P.dram("g_p", [128, KC], F32, "ExternalInput")
        io["w_p"] = P.dram("w_p", [D, proj_cols], F32, "ExternalInput")
        projT = P.dram("projT", [proj_cols, T], F32, "ExternalOutput")
        io["proj_dst"] = lambda r0, cw, tc, n: projT[r0:r0 + cw, tc:tc + n]
    if do_final:
        io["g_f"] = P.dram("g_f", [128, KC], F32, "ExternalInput")
        yT = P.dram("yT", [D, T], F32, "ExternalOutput")
        io["y_dst"] = lambda c, tc, n: yT[c * 128:(c + 1) * 128, tc:tc + n]
    if do_xout:
        xoT = P.dram("xoT", [D, T], F32, "ExternalOutput")
        io["x_dst"] = lambda c, t0, n: xoT[c * 128:(c + 1) * 128, t0:t0 + n]
    P.begin_phase()
    dense_body(P, T, do_oproj, do_mlp, proj_cols, do_final, do_xout, io)
    P.end_phase()
    return P


def sb_body(P, T, NH, TR, io):
    QT = 512
    nq = T // QT
    nkb = T // 128
    NR = T // TR
    dbg = None

    def dump(nm, src, key):
        return

    Q = P.sbuf("Q", [128, T], BF16)
    Kt = P.sbuf("Kt", [128, T], BF16)
    V = P.sbuf("V", [128, nkb, 128], BF16)
    QS = [P.sbuf("QS%d" % i, [128, TR], F32) for i in range(2)]
    VS = P.sbuf("VS", [128, TR], BF16)
    Idf = P.sbuf("Idf", [128, 128], F32)
    Idb = P.sbuf("Idb", [128, 128], BF16)
    Un = P.sbuf("Un", [128, 128], BF16)
    On = P.sbuf("On", [128, 128], BF16)
    MK = P.sbuf("MK", [128, 4, QT], BF16)
    one1 = P.sbuf("one1", [128, 1], F32)
    E = [P.sbuf("E%d" % i, [128, QT], F32) for i in range(2)]
    S = [P.sbuf("S%d" % i, [128, QT], BF16) for i in range(3)]
    TS = [P.sbuf("TS%d" % i, [128, QT], F32) for i in range(2)]
    A = [P.sbuf("A%d" % i, [128, QT], BF16) for i in range(3)]
    CS = [P.sbuf("CS%d" % i, [128, QT], F32) for i in range(2)]
    OE = [P.sbuf("OE%d" % i, [128, QT], F32) for i in range(2)]
    ZP = [P.psum("zp%d" % i, [128, QT]) for i in range(2)]
    TP = [P.psum("tp%d" % i, [128, QT]) for i in range(2)]
    CP = [P.psum("cp%d" % i, [128, QT]) for i in range(1)]
    OP = [P.psum("op%d" % i, [128, QT]) for i in range(2)]
    PT = P.psum("ptb", [128, 1024], BF16)
    cU = io["cU"]; cM = io["cM"]; cI = io["cI"]

    _dma(P, "pool", Un[:, :], cU[:, :], r=[], w=["Un"], dsem="c0")
    _dma(P, "pool", MK[:, :, :], cM.rearrange("j p q -> p j q"), r=[], w=["MK"], dsem="c0")
    _dma(P, "sp", Idf[:, :], cI[:, :], r=[], w=["Idf"], dsem="c0")
    P.op("dve", lambda e: e.tensor_copy(out=Idb[:, :], in_=Idf[:, :]), r=["Idf"], w=["Idb"])
    P.op("pool", lambda e: e.memset(On[:, :], 1.0), w=["On"])
    P.op("pool", lambda e: e.memset(one1[:, :], 1.0), w=["one1"])
    scale = 128.0 ** -0.5
    cnt = {"e": 0, "s": 0, "t": 0, "a": 0, "c": 0, "z": 0, "tp": 0}

    for h in range(NH):
        si = 0
        for i in range(NR):
            qs = QS[si % 2]
            _dmaf(P, "sp", qs[:, :], io["q"](h, i), r=io["in_keys"], w=[("QS", si % 2)], dsem="QS%d" % (si % 2))
            P.op("dve", lambda e, qs=qs, i=i: e.tensor_scalar(out=Q[:, i * TR:(i + 1) * TR], in0=qs[:, :], scalar1=scale,
                                                              scalar2=None, op0=ALU.mult),
                 r=[("QS", si % 2)], w=["Q"])
            si += 1
            _dmaf(P, "pool", Kt[:, i * TR:(i + 1) * TR], io["k"](h, i), r=io["in_keys"], w=["Kt"], dsem="Kt")
            qs = QS[si % 2]
            _dmaf(P, "sp", qs[:, :], io["v"](h, i), r=io["in_keys"], w=[("QS", si % 2)], dsem="QS%d" % (si % 2))
            P.op("dve", lambda e, qs=qs: e.tensor_copy(out=VS[:, :], in_=qs[:, :]), r=[("QS", si % 2)], w=["VS"])
            si += 1
            for b in range(TR // 128):
                P.op("pe", lambda e, b=b: e.transpose(PT[:, 0:128], VS[:, b * 128:(b + 1) * 128], Idb[:, :]),
                     r=["VS", "Idb"], w=[("PT", 0)])
                _act(P, V[:, i * (TR // 128) + b, :], PT[:, 0:128], AF.Copy, r=[("PT", 0)], w=["V"])

        for qt in range(nq):
            q0 = qt * QT
            qsl = slice(q0, q0 + QT)
            kbs = list(range(4 * qt + 3, -1, -1))
            n = len(kbs)
            cp = CP[0]
            ck = ("CP", 0)
            op_ = OP[qt % 2]
            ok = ("OP", qt % 2)
            info = {}

            def stageA(i):
                kb = kbs[i]
                zi = cnt["z"] % 2
                cnt["z"] += 1
                ei = cnt["e"] % 2
                cnt["e"] += 1
                si = cnt["s"] % 3
                cnt["s"] += 1
                ksl = slice(kb * 128, (kb + 1) * 128)
                _mm(P, ZP[zi][:, :], Kt[:, ksl], Q[:, qsl], True, True, r=["Kt", "Q"], w=[("ZP", zi)])
                _act(P, E[ei][:, :], ZP[zi][:, :], AF.Exp, r=[("ZP", zi)], w=[("E", ei)])
                _act(P, S[si][:, :], E[ei][:, :], AF.Ln, r=[("E", ei), "one1"], w=[("S", si)], bias=one1[:, 0:1])
                j = kb - 4 * qt
                if j >= 0:
                    P.op("dve", lambda e, si=si, j=j: e.tensor_tensor(out=S[si][:, :], in0=S[si][:, :], in1=MK[:, j, :], op=ALU.mult),
                         r=[("S", si), "MK"], w=[("S", si)])
                info[i] = (kb, si, ksl, j)
                if dbg and (h, qt, i) == dbg:
                    dump("dQ", Q[:, qsl], "Q")
                    dump("dE", E[ei][:, :], ("E", ei))
                    dump("dS", S[si][:, :], ("S", si))
                    dump("dV", V[:, kb:kb + 4, :].rearrange("p a b -> p (a b)"), "V")

            def stageB(i):
                kb, si, ksl, j = info[i]
                ti = cnt["tp"] % 2
                cnt["tp"] += 1
                _mm(P, TP[ti][:, :], Kt[:, ksl], Q[:, qsl], True, False, r=["Kt", "Q"], w=[("TP", ti)], inc=False)
                _mm(P, TP[ti][:, :], Un[:, :], S[si][:, :], False, True, r=["Un", ("S", si)], w=[("TP", ti)])
                src = TP[ti]
                srck = ("TP", ti)
                if i > 0:
                    ci = cnt["c"] % 2
                    cnt["c"] += 1
                    P.op("dve", lambda e, ci=ci, cp=cp: e.tensor_copy(out=CS[ci][:, :], in_=cp[:, :]), r=[ck], w=[("CS", ci)])
                    tsi = cnt["t"] % 2
                    cnt["t"] += 1
                    P.op("dve", lambda e, tsi=tsi, ti=ti, ci=ci: e.tensor_tensor(out=TS[tsi][:, :], in0=TP[ti][:, :], in1=CS[ci][:, :],
                                                                                  op=ALU.subtract),
                         r=[("TP", ti), ("CS", ci)], w=[("TS", tsi)])
                    src = TS[tsi]
                    srck = ("TS", tsi)
                if i < n - 1:
                    _mm(P, cp[:, :], On[:, :], S[si][:, :], i == 0, i == n - 2, r=["On", ("S", si)], w=[ck], inc=True)
                info[i] = (kb, si, ksl, j, src, srck)
                if dbg and (h, qt, i) == dbg:
                    dump("dTP", src[:, :], srck)

            def stageC(i):
                kb, si, ksl, j, src, srck = info[i]
                ai = cnt["a"] % 3
                cnt["a"] += 1
                _act(P, A[ai][:, :], src[:, :], AF.Exp, r=[srck], w=[("A", ai)])
                if j >= 0:
                    P.op("dve", lambda e, ai=ai, j=j: e.tensor_tensor(out=A[ai][:, :], in0=A[ai][:, :], in1=MK[:, j, :], op=ALU.mult),
                         r=[("A", ai), "MK"], w=[("A", ai)])
                _mm(P, op_[:, :], V[:, kb, :], A[ai][:, :], i == 0, i == n - 1, r=["V", ("A", ai)], w=[ok], inc=True)
                if dbg and (h, qt, i) == dbg:
                    dump("dA", A[ai][:, :], ("A", ai))

            for step in range(n + 2):
                if step < n:
                    stageA(step)
                if 0 <= step - 1 < n:
                    stageB(step - 1)
                if 0 <= step - 2 < n:
                    stageC(step - 2)
            oe = OE[qt % 2]
            P.op("dve", lambda e, oe=oe, op_=op_: e.tensor_copy(out=oe[:, :], in_=op_[:, :]), r=[ok], w=[("OE", qt % 2)])
            _dma(P, "sp", io["o_dst"](h, q0, QT), oe[:, :], r=[("OE", qt % 2)],
                 w=(io["o_wkeys"](h, q0) if "o_wkeys" in io else []), dsem="OE%d" % (qt % 2))
            if "o_done" in io:
                io["o_done"](h, qt)


def build_sb(T, NH, dbg=None):
    P = Prog()
    qT = P.dram("qT", [NH, 128, T], F32, "ExternalInput")
    kT = P.dram("kT", [NH, 128, T], F32, "ExternalInput")
    vT = P.dram("vT", [NH, 128, T], F32, "ExternalInput")
    oT = P.dram("oT", [NH, 128, T], F32, "ExternalOutput")
    TR = 2048
    io = {"cU": P.dram("cU", [128, 128], F32, "ExternalInput"), "cM": P.dram("cM", [4, 128, 512], F32, "ExternalInput"),
          "cI": P.dram("cI", [128, 128], F32, "ExternalInput"), "in_keys": [],
          "q": lambda h, i: (lambda e: qT[h, :, i * TR:(i + 1) * TR]),
          "k": lambda h, i: (lambda e: kT[h, :, i * TR:(i + 1) * TR]),
          "v": lambda h, i: (lambda e: vT[h, :, i * TR:(i + 1) * TR]),
          "o_dst": lambda h, q0, n: oT[h, :, q0:q0 + n]}
    P.begin_phase()
    sb_body(P, T, NH, TR, io)
    P.end_phase()
    return P


def sb_consts():
    j = np.arange(128)[:, None]
    k = np.arange(128)[None, :]
    cU = np.where(j >= k, -1.0, 0.0).astype(np.float32)
    kl = np.arange(128)[None, :, None]
    ql = np.arange(512)[None, None, :]
    jj = np.arange(4)[:, None, None]
    cM = ((128 * jj + kl) < ql).astype(np.float32)
    return {"cU": cU, "cM": np.ascontiguousarray(cM), "cI": np.eye(128, dtype=np.float32)}


CH = 64
ST = 512


def mix_consts():
    p = np.arange(128)
    same = (p[:, None] // CH) == (p[None, :] // CH)
    incl = (same & (p[:, None] <= p[None, :])).astype(np.float32)
    strict = (same & (p[:, None] < p[None, :])).astype(np.float32)
    ident = np.eye(128, dtype=np.float32)
    seg = np.ones((128, ST), np.float32)
    seg[:, ::CH] = 0.0
    sel = (p[:, None] == ((p[None, :] // CH) * CH + CH - 1)).astype(np.float32)
    return {"c_incl": incl, "c_strict": strict, "c_ident": ident, "c_ltri": incl.copy(), "c_seg": seg, "c_sel": sel}


def mix_body(P, T, TR, io):
    lvl = 99
    NP = T // 128
    NPR = TR // 128
    lbl = io["lbl"]; hnorm = io["hnorm"]; convw = io["convw"]; sc = io["sc"]; gnorm = io["gnorm"]
    c_incl = io["c_incl"]; c_strict = io["c_strict"]; c_ident = io["c_ident"]; c_ltri = io["c_ltri"]; c_seg = io["c_seg"]
    c_sel = io["c_sel"]
    IK = io["in_keys"]

    def rows(kind, t0, n, lo=0):
        r_ = t0 // TR
        return io["rows"](kind, r_, t0 - r_ * TR + lo, n)

    def sb(name, shape, dt=F32):
        return P.sbuf(name, shape, dt)

    Mincl = sb("Mincl", [128, 128]); Mstr = sb("Mstr", [128, 128]); Id = sb("Id", [128, 128])
    Ltri = sb("Ltri", [128, 128]); Seg = sb("Seg", [128, ST]); ones = sb("ones", [128, 128])
    epsT = sb("epsT", [128, 1]); one1 = sb("one1", [128, 1])
    LBL = sb("LBL", [128, 2]); LB = sb("LB", [128, 1]); OML = sb("OML", [128, 1]); HN = sb("HN", [128, 1])
    CW = sb("CW", [128, 12]); SC = sb("SC", [128, 2]); NEGA = sb("NEGA", [128, 1]); GN = sb("GN", [128, 1])
    for t_, d_, k_ in ((Mincl, c_incl, "Mincl"), (Mstr, c_strict, "Mstr"), (Id, c_ident, "Id"), (Ltri, c_ltri, "Ltri"),
                       (Seg, c_seg, "Seg"), (LBL, lbl, "LBL"), (HN, hnorm, "HN"), (CW, convw, "CW"), (SC, sc, "SC"),
                       (GN, gnorm, "GN")):
        _dma(P, "sp", t_[:, :], d_[:, :], r=[], w=[k_], dsem="c_" + k_)
    P.op("pool", lambda e: e.memset(ones[:, :], 1.0), w=["ones"])
    P.op("pool", lambda e: e.memset(epsT[:, :], EPS), w=["eps"])
    P.op("pool", lambda e: e.memset(one1[:, :], 1.0), w=["one1"])

    def tt(eng, out_, a, b, op, r, w):
        P.op(eng, lambda e, out_=out_, a=a, b=b, op=op: e.tensor_tensor(out=out_, in0=a, in1=b, op=op), r=r, w=w)

    def ts(eng, out_, a, s1, s2, op0, op1, r, w):
        if s2 is None:
            P.op(eng, lambda e, out_=out_, a=a, s1=s1, op0=op0: e.tensor_scalar(out=out_, in0=a, scalar1=s1, scalar2=None, op0=op0),
                 r=r, w=w)
        else:
            P.op(eng, lambda e, out_=out_, a=a, s1=s1, s2=s2, op0=op0, op1=op1:
                 e.tensor_scalar(out=out_, in0=a, scalar1=s1, scalar2=s2, op0=op0, op1=op1), r=r, w=w)

    def stt(out_, a, s, b, op0, op1, r, w):
        P.op("dve", lambda e, out_=out_, a=a, s=s, b=b, op0=op0, op1=op1:
             e.scalar_tensor_tensor(out=out_, in0=a, scalar=s, in1=b, op0=op0, op1=op1), r=r, w=w)

    def cp(eng, out_, a, r, w):
        if eng == "act":
            _act(P, out_, a, AF.Copy, r=r, w=w)
        else:
            P.op(eng, lambda e, out_=out_, a=a: e.tensor_copy(out=out_, in_=a), r=r, w=w)

    def tr(out_, a, r, w):
        P.op("pe", lambda e, out_=out_, a=a: e.transpose(out_, a, Idb[:, :]), r=r + ["Idb"], w=w)

    PSB = [P.psum("psb%d" % i, [128, 512]) for i in range(7)]
    PST = P.psum("psbT", [128, 1024], BF16)
    pst = {"q": 0, "f": 0, "t": 0}
    Idb = sb("Idb", [128, 128], BF16)
    cp("pool", Idb[:, :], Id[:, :], r=["Id"], w=["Idb"])

    def ptq():
        return PST[:, 0:128], ("PT", 0)

    def pq():
        i = pst["q"]
        pst["q"] = (i + 1) % 5
        return PSB[i][:, 0:128], ("PQ", i)

    def pf():
        i = pst["f"]
        pst["f"] = (i + 1) % 2
        return PSB[5 + i][:, :], ("PF", i)

    tt("dve", LB[:, :], LBL[:, 0:1], LBL[:, 1:2], ALU.subtract, r=["LBL"], w=["LB"])
    _act(P, LB[:, :], LB[:, :], AF.Sigmoid, r=["LB"], w=["LB"])
    ts("dve", OML[:, :], LB[:, :], -1.0, 1.0, ALU.mult, ALU.add, r=["LB"], w=["OML"])
    _act(P, NEGA[:, :], SC[:, 0:1], AF.Exp, r=["SC"], w=["NEGA"])
    ts("dve", NEGA[:, :], NEGA[:, :], -1.0, None, ALU.mult, None, r=["NEGA"], w=["NEGA"])

    GAC = sb("GAC", [128, NP]); GBC = sb("GBC", [128, NP]); GCC = sb("GCC", [128, NP])
    BEC = sb("BEC", [128, NP]); KBEC = sb("KBEC", [128, NP]); KDC = sb("KDC", [128, NP])
    for r_ in range(T // TR):
        for which, dstT, dk in ((0, GAC, "GAC"), (1, GBC, "GBC")):
            _dmaf(P, "sp", dstT[:, r_ * NPR:(r_ + 1) * NPR],
                  (lambda e, f=io["ab"](which, r_, 0, TR): f(e).rearrange("o (b p) -> p (o b)", p=128)),
                  r=IK, w=[dk], dsem="c_gac", slow=True)
    _act(P, GAC[:, :], GAC[:, :], AF.Exp, r=["GAC", "SC"], w=["GAC"], bias=SC[:, 1:2])
    _act(P, GAC[:, :], GAC[:, :], AF.Ln, r=["GAC", "one1"], w=["GAC"], bias=one1[:, 0:1])
    ts("dve", GAC[:, :], GAC[:, :], NEGA[:, 0:1], None, ALU.mult, None, r=["GAC", "NEGA"], w=["GAC"])
    _act(P, BEC[:, :], GBC[:, :], AF.Sigmoid, r=["GBC"], w=["BEC"])
    for n0 in range(0, NP, 512):
        n1 = min(NP, n0 + 512)
        ps, pk = pf()
        _mm(P, ps[:, 0:n1 - n0], Ltri[:, :], GAC[:, n0:n1], True, True, r=["Ltri", "GAC"], w=[pk])
        cp("dve", GCC[:, n0:n1], ps[:, 0:n1 - n0], r=[pk], w=["GCC"])
    _act(P, KBEC[:, :], GCC[:, :], AF.Exp, r=["GCC"], w=["KBEC"])
    tt("dve", KBEC[:, :], KBEC[:, :], BEC[:, :], ALU.mult, r=["KBEC", "BEC"], w=["KBEC"])
    SEL = sb("SEL", [128, 128])
    _dma(P, "sp", SEL[:, :], c_sel[:, :], r=[], w=["SEL"], dsem="c_sel")
    for n0 in range(0, NP, 512):
        n1 = min(NP, n0 + 512)
        ps, pk = pf()
        _mm(P, ps[:, 0:n1 - n0], SEL[:, :], GCC[:, n0:n1], True, True, r=["SEL", "GCC"], w=[pk])
        tt("dve", KDC[:, n0:n1], ps[:, 0:n1 - n0], GCC[:, n0:n1], ALU.subtract, r=[pk, "GCC"], w=["KDC"])
    _act(P, KDC[:, :], KDC[:, :], AF.Exp, r=["KDC"], w=["KDC"])

    if lvl == 0:
        return P
    HI = sb("HI", [128, ST]); HIb = sb("HIb", [128, ST], BF16); HVp = sb("HVp", [128, 128], BF16)

    SAf = sb("SAf", [128, 128]); SAb = sb("SAb", [128, 128], BF16)
    SBf = sb("SBf", [128, 128]); SBb = sb("SBb", [128, 128], BF16)
    for t_, k_ in ((SAf, "SAf"), (SAb, "SAb"), (SBf, "SBf"), (SBb, "SBb")):
        P.op("pool", lambda e, t_=t_: e.memset(t_[:, :], 0.0), w=[k_])

    def T512(name, dt=F32, w=ST):
        return sb(name, [128, w], dt)
    HQ = T512("HQ"); HF = T512("HF"); HG = T512("HG")
    T1 = T512("T1"); T2 = T512("T2"); T3 = T512("T3"); T4 = T512("T4"); T5 = T512("T5")
    QBh = T512("QBh", BF16); KBh = T512("KBh", BF16)
    OA = T512("OA"); OSQ = T512("OSQ"); ORS = T512("ORS")
    UQ = T512("UQ", F32, ST + 3); UK = T512("UK", F32, ST + 3); UV = T512("UV", F32, ST + 3); GG = T512("GG")
    GAb = T512("GAb"); GBb = T512("GBb")
    CQ = T512("CQ"); CK = T512("CK"); CV = T512("CV"); RQ = T512("RQ")
    GC = T512("GC"); EGC = T512("EGC"); BETA = T512("BETA")
    QDb = T512("QDb", BF16); KNb = T512("KNb", BF16); QNb = T512("QNb", BF16); CVb = T512("CVb", BF16)
    OB = T512("OB")
    def T128(name, dt=F32):
        return sb(name, [128, 128], dt)
    AM = T128("AM", BF16); K2 = T128("K2", BF16); K2t = T128("K2t", BF16)
    EXD = T128("EXD"); DECI = T128("DECI"); DECS = T128("DECS")
    ATf = T128("ATf"); ATb = T128("ATb", BF16); Ab = T128("Ab", BF16)
    Pb = [T128("Pb%d" % i, BF16) for i in range(2)]; PTb = [T128("PTb%d" % i, BF16) for i in range(2)]
    TTf = T128("TTf"); TTb = T128("TTb", BF16)
    KBEt = T128("KBEt", BF16); KDt = T128("KDt", BF16); VBt = T128("VBt", BF16)
    Ut = T128("Ut"); WTb = T128("WTb", BF16); VN = T128("VN", BF16); QKT = T128("QKT", BF16)

    def gated_out(O, gate_src_key, G_in, W_col, wkey, oi, t0):
        ok_ = "O%d" % oi
        _act(P, OSQ[:, :], O[:, :], AF.Square, r=[ok_], w=["OSQ"])
        ps, pk = pf()
        _mm(P, ps, ones[:, :], OSQ[:, :], True, True, r=["ones", "OSQ"], w=[pk])
        _act(P, ORS[:, :], ps, AF.Sqrt, r=[pk, "eps"], w=["ORS"], scale=1.0 / 128, bias=epsT[:, 0:1])
        P.op("dve", lambda e: e.reciprocal(out=ORS[:, :], in_=ORS[:, :]), r=["ORS"], w=["ORS"])
        stt(O[:, :], O[:, :], W_col[:, 0:1], ORS[:, :], ALU.mult, ALU.mult, r=[ok_, "ORS", wkey], w=[ok_])
        _act(P, G_in[:, :], G_in[:, :], AF.Silu, r=[gate_src_key], w=[gate_src_key])
        tt("dve", O[:, :], O[:, :], G_in[:, :], ALU.mult, r=[ok_, gate_src_key], w=[ok_])
        _dma(P, "sp", io["o_dst"](oi, t0, ST), O[:, :], r=[ok_], w=(io["o_wkeys"](t0) if "o_wkeys" in io else []), dsem="out%d" % oi)

    for st_i in range(T // ST):
        t0 = st_i * ST
        _dmaf(P, "sp", HQ[:, :], rows(0, t0, ST), r=IK, w=["HQ"], dsem="HQ")
        _dmaf(P, "sp", HF[:, :], rows(1, t0, ST), r=IK, w=["HF"], dsem="HF")
        _dmaf(P, "sp", HG[:, :], rows(3, t0, ST), r=IK, w=["HG"], dsem="HG")
        _dmaf(P, "sp", HI[:, :], rows(2, t0, ST), r=IK, w=["HI"], dsem="HI")
        cp("pool", HIb[:, :], HI[:, :], r=["HI"], w=["HIb"])
        _act(P, T1[:, :], HF[:, :], AF.Sigmoid, r=["HF"], w=["T1"])
        ts("dve", T1[:, :], T1[:, :], OML[:, 0:1], LB[:, 0:1], ALU.mult, ALU.add, r=["T1", "OML", "LB"], w=["T1"])
        _act(P, T2[:, :], T1[:, :], AF.Ln, r=["T1"], w=["T2"])
        ts("dve", T1[:, :], T1[:, :], -1.0, 1.0, ALU.mult, ALU.add, r=["T1"], w=["T1"])
        P.op("dve", lambda e: e.tensor_tensor_scan(out=T3[:, :], data0=Seg[:, :], data1=T2[:, :], initial=0.0,
                                                   op0=ALU.mult, op1=ALU.add), r=["Seg", "T2"], w=["T3"])
        _act(P, T2[:, :], T3[:, :], AF.Exp, r=["T3"], w=["T2"])
        _act(P, T4[:, :], T3[:, :], AF.Exp, r=["T3"], w=["T4"], scale=-1.0)
        _act(P, T5[:, :], HQ[:, :], AF.Silu, r=["HQ"], w=["T5"])
        tt("dve", QBh[:, :], T5[:, :], T2[:, :], ALU.mult, r=["T5", "T2"], w=["QBh"])
        tt("dve", T1[:, :], T1[:, :], T4[:, :], ALU.mult, r=["T1", "T4"], w=["T1"])
        cp("pool", KBh[:, :], T1[:, :], r=["T1"], w=["KBh"])
        if lvl == 1:
            return P
        for U_, gi, k_ in ((UQ, 4, "UQ"), (UK, 5, "UK"), (UV, 6, "UV")):
            if t0 == 0:
                P.op("pool", lambda e, U_=U_: e.memset(U_[:, 0:3], 0.0), w=[k_])
                _dmaf(P, "sp", U_[:, 3:ST + 3], rows(gi, 0, ST), r=IK, w=[k_], dsem=k_)
            elif t0 % TR == 0:
                _dmaf(P, "sp", U_[:, 0:3], rows(gi, t0 - ST, 3, lo=ST - 3), r=IK, w=[k_], dsem=k_)
                _dmaf(P, "sp", U_[:, 3:ST + 3], rows(gi, t0, ST), r=IK, w=[k_], dsem=k_)
            else:
                _dmaf(P, "sp", U_[:, :], rows(gi, t0, ST + 3, lo=-3), r=IK, w=[k_], dsem=k_)
        _dmaf(P, "sp", GG[:, :], rows(7, t0, ST), r=IK, w=["GG"], dsem="GG")
        r_ = t0 // TR
        _dmaf(P, "sp", GAb[:, :], (lambda e, f=io["ab"](0, r_, t0 - r_ * TR, ST): f(e).partition_broadcast(128)), r=IK, w=["GAb"], dsem="GAb")
        _dmaf(P, "sp", GBb[:, :], (lambda e, f=io["ab"](1, r_, t0 - r_ * TR, ST): f(e).partition_broadcast(128)), r=IK, w=["GBb"], dsem="GAb")
        for U_, C_, k_, ck_, wi in ((UQ, CQ, "UQ", "CQ", 0), (UK, CK, "UK", "CK", 4), (UV, CV, "UV", "CV", 8)):
            ts("dve", C_[:, :], U_[:, 0:ST], CW[:, wi:wi + 1], None, ALU.mult, None, r=[k_, "CW"], w=[ck_])
            for j in (1, 2, 3):
                stt(C_[:, :], U_[:, j:j + ST], CW[:, wi + j:wi + j + 1], C_[:, :], ALU.mult, ALU.add, r=[k_, "CW", ck_], w=[ck_])
            _act(P, C_[:, :], C_[:, :], AF.Silu, r=[ck_], w=[ck_])
        for C_, ck_, scl in ((CQ, "CQ", 128.0 ** -0.5), (CK, "CK", 1.0)):
            _act(P, OSQ[:, :], C_[:, :], AF.Square, r=[ck_], w=["OSQ"])
            ps, pk = pf()
            _mm(P, ps, ones[:, :], OSQ[:, :], True, True, r=["ones", "OSQ"], w=[pk])
            _act(P, RQ[:, :], ps, AF.Sqrt, r=[pk, "eps"], w=["RQ"], bias=epsT[:, 0:1])
            P.op("dve", lambda e: e.reciprocal(out=RQ[:, :], in_=RQ[:, :]), r=["RQ"], w=["RQ"])
            stt(C_[:, :], C_[:, :], scl, RQ[:, :], ALU.mult, ALU.mult, r=[ck_, "RQ"], w=[ck_])
        _act(P, GAb[:, :], GAb[:, :], AF.Exp, r=["GAb", "SC"], w=["GAb"], bias=SC[:, 1:2])
        _act(P, GAb[:, :], GAb[:, :], AF.Ln, r=["GAb", "one1"], w=["GAb"], bias=one1[:, 0:1])
        ts("dve", GAb[:, :], GAb[:, :], NEGA[:, 0:1], None, ALU.mult, None, r=["GAb", "NEGA"], w=["GAb"])
        P.op("dve", lambda e: e.tensor_tensor_scan(out=GC[:, :], data0=Seg[:, :], data1=GAb[:, :], initial=0.0,
                                                   op0=ALU.mult, op1=ALU.add), r=["Seg", "GAb"], w=["GC"])
        _act(P, EGC[:, :], GC[:, :], AF.Exp, r=["GC"], w=["EGC"])
        _act(P, BETA[:, :], GBb[:, :], AF.Sigmoid, r=["GBb"], w=["BETA"])
        tt("dve", QDb[:, :], CQ[:, :], EGC[:, :], ALU.mult, r=["CQ", "EGC"], w=["QDb"])
        cp("pool", KNb[:, :], CK[:, :], r=["CK"], w=["KNb"])
        cp("pool", QNb[:, :], CQ[:, :], r=["CQ"], w=["QNb"])
        cp("pool", CVb[:, :], CV[:, :], r=["CV"], w=["CVb"])

        if lvl == 2:
            return P
        for pi in range(ST // 128):
            pn = st_i * (ST // 128) + pi
            c0 = pi * 128
            csl = slice(c0, c0 + 128)
            phv, khv = ptq()
            tr(phv, HIb[:, csl], r=["HIb"], w=[khv])
            cp("act", HVp[:, :], phv, r=[khv], w=["HVp"])
            ps, pk = pq()
            _mm(P, ps, KBh[:, csl], QBh[:, csl], True, True, r=["KBh", "QBh"], w=[pk])
            tt("dve", AM[:, :], ps, Mincl[:, :], ALU.mult, r=[pk, "Mincl"], w=["AM"])
            if lvl == 20:
                return P
            for half in (0, 1):
                hs = slice(half * 64, half * 64 + 64)
                ts("dve", K2[:, hs], T1[:, c0 + half * 64:c0 + half * 64 + 64], T2[:, c0 + half * 64 + 63:c0 + half * 64 + 64], None,
                   ALU.mult, None, r=["T1", "T2"], w=["K2"])
            if lvl == 21:
                return P
            ps2, pk2 = ptq()
            tr(ps2, K2[:, :], r=["K2"], w=[pk2])
            if lvl == 22:
                return P
            cp("act", K2t[:, :], ps2, r=[pk2], w=["K2t"])
            if lvl == 30:
                return P
            pso, pko = pq()
            for half in ((0,) if lvl == 31 else (0, 1)):
                hs = slice(half * 64, half * 64 + 64)
                tsl_ = slice(c0 + half * 64, c0 + half * 64 + 64)
                _mm(P, pso[:, hs], HVp[hs, :], AM[hs, hs], True, False, r=["HVp", "AM"], w=[pko], inc=False)
                _mm(P, pso[:, hs], SAb[:, :], QBh[:, tsl_], False, True, r=["SAb", "QBh"], w=[pko])
                psu, pku = pq()
                _mm(P, psu, K2t[hs, :], HVp[hs, :], True, True, r=["K2t", "HVp"], w=[pku])
                stt(SAf[:, :], SAf[:, :], T2[:, c0 + half * 64 + 63:c0 + half * 64 + 64], psu, ALU.mult, ALU.add,
                    r=["SAf", "T2", pku], w=["SAf"])
                cp("act", SAb[:, :], SAf[:, :], r=["SAf"], w=["SAb"])
            if lvl in (31, 32):
                return P
            cp("act", OA[:, csl], pso, r=[pko], w=["O0"])
            if lvl == 3:
                return P
            ts("dve", EXD[:, :], GC[:, csl], GCC[:, pn:pn + 1], 0.0, ALU.subtract, ALU.min, r=["GC", "GCC"], w=["EXD"])
            _act(P, EXD[:, :], EXD[:, :], AF.Exp, r=["EXD"], w=["EXD"])
            tt("pool", DECI[:, :], EXD[:, :], Mincl[:, :], ALU.mult, r=["EXD", "Mincl"], w=["DECI"])
            tt("pool", DECS[:, :], EXD[:, :], Mstr[:, :], ALU.mult, r=["EXD", "Mstr"], w=["DECS"])
            tt("pool", DECS[:, :], DECS[:, :], BETA[:, csl], ALU.mult, r=["DECS", "BETA"], w=["DECS"])
            psk, pkk = pq()
            _mm(P, psk, KNb[:, csl], KNb[:, csl], True, True, r=["KNb"], w=[pkk])
            stt(ATf[:, :], psk, -1.0, DECS[:, :], ALU.mult, ALU.mult, r=[pkk, "DECS"], w=["ATf"])
            cp("act", ATb[:, :], ATf[:, :], r=["ATf"], w=["ATb"])
            psq, pkq = pq()
            _mm(P, psq, KNb[:, csl], QNb[:, csl], True, True, r=["KNb", "QNb"], w=[pkq])
            tt("dve", QKT[:, :], psq, DECI[:, :], ALU.mult, r=[pkq, "DECI"], w=["QKT"])
            pst_, pkt = ptq()
            tr(pst_, ATb[:, :], r=["ATb"], w=[pkt])
            cp("act", Ab[:, :], pst_, r=[pkt], w=["Ab"])
            tt("dve", TTf[:, :], ATf[:, :], Id[:, :], ALU.add, r=["ATf", "Id"], w=["TTf"])
            cp("act", TTb[:, :], TTf[:, :], r=["TTf"], w=["TTb"])
            Xc, XTc, xk, xtk = ATb, Ab, "ATb", "Ab"
            for k in range(1, 6):
                Xn, XTn = Pb[k % 2], PTb[k % 2]
                xnk, xtnk = "Pb%d" % (k % 2), "PTb%d" % (k % 2)
                p1, k1 = pq()
                _mm(P, p1, XTc[:, :], Xc[:, :], True, True, r=[xtk, xk], w=[k1])
                cp("act", Xn[:, :], p1, r=[k1], w=[xnk])
                if k < 5:
                    p2, k2 = pq()
                    _mm(P, p2, Xc[:, :], XTc[:, :], True, True, r=[xk, xtk], w=[k2])
                    cp("dve", XTn[:, :], p2, r=[k2], w=[xtnk])
                if k == 5:
                    p2, k2 = pq()
                    _mm(P, p2, Xc[:, :], XTc[:, :], True, True, r=[xk, xtk], w=[k2])
                    cp("dve", XTn[:, :], p2, r=[k2], w=[xtnk])
                p3, k3 = pq()
                _mm(P, p3, XTn[:, :], TTb[:, :], True, True, r=[xtnk, "TTb"], w=[k3])
                tt("dve", TTf[:, :], TTf[:, :], p3, ALU.add, r=["TTf", k3], w=["TTf"])
                cp("act", TTb[:, :], TTf[:, :], r=["TTf"], w=["TTb"])
                Xc, XTc, xk, xtk = Xn, XTn, xnk, xtnk
            if lvl == 4:
                return P
            pkt_, kkt = ptq()
            tr(pkt_, KNb[:, csl], r=["KNb"], w=[kkt])
            ts("dve", KBEt[:, :], pkt_, KBEC[:, pn:pn + 1], None, ALU.mult, None, r=[kkt, "KBEC"], w=["KBEt"])
            ts("dve", KDt[:, :], pkt_, KDC[:, pn:pn + 1], None, ALU.mult, None, r=[kkt, "KDC"], w=["KDt"])
            pvt_, kvt = ptq()
            tr(pvt_, CVb[:, csl], r=["CVb"], w=[kvt])
            ts("dve", VBt[:, :], pvt_, BEC[:, pn:pn + 1], None, ALU.mult, None, r=[kvt, "BEC"], w=["VBt"])
            pu, ku = pq()
            _mm(P, pu, TTb[:, :], VBt[:, :], True, True, r=["TTb", "VBt"], w=[ku])
            cp("act", Ut[:, :], pu, r=[ku], w=["Ut"])
            pw, kw = pq()
            _mm(P, pw, KBEt[:, :], TTb[:, :], True, True, r=["KBEt", "TTb"], w=[kw])
            cp("act", WTb[:, :], pw, r=[kw], w=["WTb"])
            if lvl == 5:
                return P
            psob, pkob = pq()
            for half in (0, 1):
                hs = slice(half * 64, half * 64 + 64)
                tsl_ = slice(c0 + half * 64, c0 + half * 64 + 64)
                pws, kws = pq()
                _mm(P, pws[hs, :], WTb[:, hs], SBb[:, :], True, True, r=["WTb", "SBb"], w=[kws])
                tt("dve", VN[hs, :], Ut[hs, :], pws[hs, :], ALU.subtract, r=["Ut", kws], w=["VN"])
                _mm(P, psob[:, hs], SBb[:, :], QDb[:, tsl_], True, False, r=["SBb", "QDb"], w=[pkob], inc=False)
                _mm(P, psob[:, hs], VN[hs, :], QKT[hs, hs], False, True, r=["VN", "QKT"], w=[pkob])
                pss, kss = pq()
                _mm(P, pss, KDt[hs, :], VN[hs, :], True, True, r=["KDt", "VN"], w=[kss])
                stt(SBf[:, :], SBf[:, :], EGC[:, c0 + half * 64 + 63:c0 + half * 64 + 64], pss, ALU.mult, ALU.add,
                    r=["SBf", "EGC", kss], w=["SBf"])
                cp("act", SBb[:, :], SBf[:, :], r=["SBf"], w=["SBb"])
            cp("act", OB[:, csl], psob, r=[pkob], w=["O1"])
        if lvl == 6:
            return P
        gated_out(OA, "HG", HG, HN, "HN", 0, t0)
        gated_out(OB, "GG", GG, GN, "GN", 1, t0)
        if "o_done" in io:
            io["o_done"](t0)


def build_mix(T, lvl=99):
    P = Prog()
    NP = T // 128
    hin = P.dram("hin", [3, 128, T], F32, "ExternalInput")
    hiT = P.dram("hiT", [128, T], F32, "ExternalInput")
    gin = P.dram("gin", [4, 128, T], F32, "ExternalInput")
    gab = P.dram("gab", [2, T], F32, "ExternalInput")
    out = P.dram("out", [2, 128, T], F32, "ExternalOutput")
    io = {"in_keys": []}
    for nm, shp in (("lbl", [128, 2]), ("hnorm", [128, 1]), ("convw", [128, 12]), ("sc", [128, 2]), ("gnorm", [128, 1]),
                    ("c_incl", [128, 128]), ("c_strict", [128, 128]), ("c_ident", [128, 128]), ("c_ltri", [128, 128]),
                    ("c_seg", [128, ST]), ("c_sel", [128, 128])):
        io[nm] = P.dram(nm, shp, F32, "ExternalInput")
    import os
    TR = int(os.environ.get('MIX_TR', min(2048, T)))

    def rows_(kind, r_, c0, n):
        src = {0: hin[0], 1: hin[1], 2: hiT, 3: hin[2], 4: gin[0], 5: gin[1], 6: gin[2], 7: gin[3]}[kind]
        return lambda e: src[:, r_ * TR + c0:r_ * TR + c0 + n]
    io["rows"] = rows_
    io["ab"] = lambda which, r_, c0, n: (lambda e: gab[which:which + 1, r_ * TR + c0:r_ * TR + c0 + n])
    io["o_dst"] = lambda oi, t0, n: out[oi, :, t0:t0 + n]
    P.begin_phase()
    mix_body(P, T, TR, io)
    P.end_phase()
    return P


def mix_core_inputs(projT, h, conv_w, a_log, dt_bias, lb_logits, hgrn_norm, gdn_norm):
    T = projT.shape[1]
    r = lambda base: projT[base + h * 128: base + (h + 1) * 128]
    d = {}
    d["hin"] = np.ascontiguousarray(np.stack([r(0), r(1024), r(3072)]))
    d["hiT"] = np.ascontiguousarray(r(2048))
    d["lbl"] = np.ascontiguousarray(lb_logits[:, h * 128:(h + 1) * 128].T)
    d["hnorm"] = np.ascontiguousarray(hgrn_norm.reshape(128, 1))
    d["gin"] = np.ascontiguousarray(np.stack([r(4096), r(5120), r(6144), r(7184)]))
    ga = projT[7168 + h]
    gb = projT[7176 + h]
    d["gab"] = np.ascontiguousarray(np.stack([ga, gb]))
    cw = np.concatenate([conv_w[:, h * 128:(h + 1) * 128].T, conv_w[:, 1024 + h * 128:1024 + (h + 1) * 128].T,
                         conv_w[:, 2048 + h * 128:2048 + (h + 1) * 128].T], axis=1)
    d["convw"] = np.ascontiguousarray(cw)
    d["sc"] = np.ascontiguousarray(np.stack([np.full(128, a_log[h], np.float32), np.full(128, dt_bias[h], np.float32)], axis=1))
    d["gnorm"] = np.ascontiguousarray(gdn_norm.reshape(128, 1))
    d.update(mix_consts())
    return d


import concourse.bass as _bass

NCORES = 8
RG = [list(range(NCORES))]


def _allgather(P, in_ap, out_ap, r, w):
    P.op("pool", lambda e, in_ap=in_ap, out_ap=out_ap: e.collective_compute(
        "AllGather", ALU.bypass, replica_groups=RG, ins=[in_ap], outs=[out_ap]), r=r, w=w, dsem="cc", dinc=1)


def _select(P, eng, dst_ap, src_fn, r, w):
    P.op(eng, lambda e, dst_ap=dst_ap, src_fn=src_fn: e.dma_start(out=dst_ap, in_=src_fn(e.partition_id())), r=r, w=w, dsem="sel")


def build_fused(SEQ):
    P = Prog()
    P.nc.cache_partition_id()
    D = D_MODEL
    TR = SEQ // NCORES
    ds = _bass.ds
    ext = lambda n, shp: P.dram(n, shp, F32, "ExternalInput")
    xT = ext("xT", [D, TR])
    w_in = ext("w_in", [D, 8208]); g0 = ext("g0", [128, KC])
    w_out0 = ext("w_out0", [D, D]); gm0 = ext("gm0", [128, KC]); w1_0 = ext("w1_0", [D, 4 * D]); w2_0 = ext("w2_0", [4 * D, D])
    g1 = ext("g1", [128, KC]); w_qkv = ext("w_qkv", [D, 6144])
    w_o1 = ext("w_o1", [D, D]); gm1 = ext("gm1", [128, KC]); w1_1 = ext("w1_1", [D, 4 * D]); w2_1 = ext("w2_1", [4 * D, D])
    gf = ext("gf", [128, KC])
    mio = {"in_keys": []}
    for nm, shp in (("lbl", [128, 2]), ("hnorm", [128, 1]), ("convw", [128, 12]), ("sc", [128, 2]), ("gnorm", [128, 1]),
                    ("c_incl", [128, 128]), ("c_strict", [128, 128]), ("c_ident", [128, 128]), ("c_ltri", [128, 128]),
                    ("c_seg", [128, ST]), ("c_sel", [128, 128])):
        mio[nm] = ext(nm, shp)
    sio = {"cU": ext("cU", [128, 128]), "cM": ext("cM", [4, 128, 512]), "cI": mio["c_ident"], "in_keys": []}
    yT = P.dram("yT", [D, TR], F32, "ExternalOutput")
    internal = lambda n, shp: P.dram(n, shp, F32, "Internal")
    shared = lambda n, shp: P.dram(n, shp, F32, "Internal", addr_space="Shared")
    ag1_in = internal("ag1_in", [8208, TR])
    ag1_out = [shared("ag1_out%d" % g, [NCORES * 1024, TR]) for g in range(8)] + [shared("ag1_out8", [NCORES * 16, TR])]
    sel1 = [internal("sel1_%d" % g, [NCORES * 128, TR]) for g in range(8)]
    sel1ab = internal("sel1ab", [2 * NCORES, TR])
    ag2_in = internal("ag2_in", [NCORES * 256, TR]); ag2_out = shared("ag2_out", [NCORES * NCORES * 256, TR])
    sel2 = internal("sel2", [NCORES * 256, TR])
    x1 = internal("x1", [D, TR])
    ag3_in = internal("ag3_in", [6144, TR])
    ag3_out = [shared("ag3_out%d" % g, [NCORES * 2048, TR]) for g in range(3)]
    sel3 = [internal("sel3_%d" % g, [NCORES * 256, TR]) for g in range(3)]
    ag4_in = internal("ag4_in", [16 * 128, TR]); ag4_out = shared("ag4_out", [16 * NCORES * 128, TR])
    sel4 = internal("sel4", [16 * 128, TR])

    P.begin_phase()

    def p1_done(c_end, last):
        if not last:
            return
        if c_end % 1024 == 0 and c_end <= 8192:
            g = c_end // 1024 - 1
            _allgather(P, ag1_in[g * 1024:(g + 1) * 1024, :], ag1_out[g][:, :], r=["ag1_in"], w=[("ag1_out", g)])
            _select(P, "sp", sel1[g].rearrange("(r p) t -> r p t", p=128),
                    lambda pid, g=g: ag1_out[g].rearrange("(r q) t -> r q t", q=1024)[:, ds(pid * 128, 128), :],
                    r=[("ag1_out", g)], w=["sel1"])
        elif c_end == 8208:
            _allgather(P, ag1_in[8192:8208, :], ag1_out[8][:, :], r=["ag1_in"], w=[("ag1_out", 8)])
            for which in (0, 1):
                _select(P, "sp", sel1ab[which * NCORES:(which + 1) * NCORES, :].rearrange("(r o) t -> r o t", o=1),
                        lambda pid, which=which: ag1_out[8].rearrange("(r q) t -> r q t", q=16)[:, which * 8:which * 8 + 8, :][:, ds(pid, 1), :],
                        r=[("ag1_out", 8)], w=["sel1"])
    dense_body(P, TR, False, False, 8208, False, False, {
        "x": lambda c, t0, n: xT[c * 128:(c + 1) * 128, t0:t0 + n], "g_p": g0, "w_p": w_in,
        "proj_dst": lambda r0, cw, tc, n: ag1_in[r0:r0 + cw, tc:tc + n], "proj_keys": ["ag1_in"], "proj_group_done": p1_done})
    P.end_phase()

    P.begin_phase()
    mio["rows"] = lambda kind, r_, c0, n: (lambda e: sel1[kind][r_ * 128:(r_ + 1) * 128, c0:c0 + n])
    mio["ab"] = lambda which, r_, c0, n: (lambda e: sel1ab[which * NCORES + r_:which * NCORES + r_ + 1, c0:c0 + n])
    mio["abscr"] = sel1ab
    mio["o_dst"] = lambda oi, t0, n: ag2_in[(t0 // TR) * 256 + oi * 128:(t0 // TR) * 256 + (oi + 1) * 128, t0 % TR:t0 % TR + n]

    def p2_done(t0):
        if (t0 + ST) % TR == 0:
            j = t0 // TR
            _allgather(P, ag2_in[j * 256:(j + 1) * 256, :], ag2_out[j * 2048:(j + 1) * 2048, :], r=[("ag2_in", j)], w=["ag2_out"])
            if j == NCORES - 1:
                _select(P, "pool", sel2[:, :], lambda pid: ag2_out[ds(pid * 2048, 2048), :], r=["ag2_out"], w=["sel2"])
    mio["o_done"] = p2_done
    mio["o_wkeys"] = lambda t0: [("ag2_in", t0 // TR)]
    mix_body(P, SEQ, TR, mio)
    P.end_phase()

    P.begin_phase()

    def p3_done(c_end, last):
        if last and c_end % 2048 == 0:
            g = c_end // 2048 - 1
            _allgather(P, ag3_in[g * 2048:(g + 1) * 2048, :], ag3_out[g][:, :], r=["ag3_in"], w=[("ag3_out", g)])
            _select(P, "pool", sel3[g].rearrange("(r p) t -> r p t", p=256),
                    lambda pid, g=g: ag3_out[g].rearrange("(r q) t -> r q t", q=2048)[:, ds(pid * 256, 256), :],
                    r=[("ag3_out", g)], w=["sel3"])
    dense_body(P, TR, True, True, 6144, False, True, {
        "x": lambda c, t0, n: xT[c * 128:(c + 1) * 128, t0:t0 + n],
        "o": lambda c, t0, n: (lambda e: sel2[(c % 8) * 256 + (c // 8) * 128:(c % 8) * 256 + (c // 8) * 128 + 128, t0:t0 + n]),
        "w_o": w_out0, "g_mlp": gm0, "w1": w1_0, "w2": w2_0, "g_p": g1, "w_p": w_qkv,
        "x_dst": lambda c, t0, n: x1[c * 128:(c + 1) * 128, t0:t0 + n],
        "proj_dst": lambda r0, cw, tc, n: ag3_in[r0:r0 + cw, tc:tc + n], "proj_keys": ["ag3_in"], "proj_group_done": p3_done})
    P.end_phase()

    P.begin_phase()
    sio["q"] = lambda h, r_: (lambda e: sel3[0][r_ * 256 + h * 128:r_ * 256 + (h + 1) * 128, :])
    sio["k"] = lambda h, r_: (lambda e: sel3[1][r_ * 256 + h * 128:r_ * 256 + (h + 1) * 128, :])
    sio["v"] = lambda h, r_: (lambda e: sel3[2][r_ * 256 + h * 128:r_ * 256 + (h + 1) * 128, :])
    sio["o_dst"] = lambda h, q0, n: ag4_in[(h * 8 + q0 // TR) * 128:(h * 8 + q0 // TR + 1) * 128, q0 % TR:q0 % TR + n]

    def p4_done(h, qt):
        q0 = qt * 512
        if (q0 + 512) % TR == 0:
            g = h * 8 + q0 // TR
            _allgather(P, ag4_in[g * 128:(g + 1) * 128, :], ag4_out[g * 1024:(g + 1) * 1024, :],
                       r=[("ag4_in", g)], w=["ag4_out"])
            if g == 15:
                _select(P, "pool", sel4.rearrange("(h q) t -> h q t", h=2),
                        lambda pid: ag4_out.rearrange("(h j q) t -> h j q t", h=2, j=8)[:, ds(pid, 1), :, :]
                        .rearrange("h o q t -> h (o q) t"),
                        r=["ag4_out"], w=["sel4"])
    sio["o_done"] = p4_done
    sio["o_wkeys"] = lambda h, q0: [("ag4_in", h * 8 + q0 // TR)]
    sb_body(P, SEQ, 2, TR, sio)
    P.end_phase()

    P.begin_phase()
    dense_body(P, TR, True, True, 0, True, False, {
        "x": lambda c, t0, n: x1[c * 128:(c + 1) * 128, t0:t0 + n],
        "o": lambda c, t0, n: (lambda e: sel4[(c % 2) * 1024 + (c // 2) * 128:(c % 2) * 1024 + (c // 2) * 128 + 128, t0:t0 + n]),
        "w_o": w_o1, "g_mlp": gm1, "w1": w1_1, "w2": w2_1, "g_f": gf,
        "y_dst": lambda c, tc, n: yT[c * 128:(c + 1) * 128, tc:tc + n]})
    P.end_phase()
    return P


def _gl(g):
    return np.ascontiguousarray(np.asarray(g, np.float32).reshape(KC, 128).T)


def fused_inputs(SEQ, x, mix_norm, a_w_in, a_conv_w, a_a_log, a_dt_bias, a_lb_logits, a_hgrn_norm,
                 a_gdn_norm, a_w_out, c_w_qkv, c_w_o, mlp_norm, mlp_w1, mlp_w2, final_norm):
    f = lambda a: np.ascontiguousarray(np.asarray(a, dtype=np.float32))
    TR = SEQ // NCORES
    xT = f(x)[0].T
    w_in = f(a_w_in[0])
    w_in = np.ascontiguousarray(np.concatenate([w_in[:, 0:7168], w_in[:, 7184:8208], w_in[:, 7168:7184]], axis=1))
    common = {"w_in": w_in, "g0": _gl(mix_norm[0]), "w_out0": f(a_w_out[0]), "gm0": _gl(mlp_norm[0]),
              "w1_0": f(mlp_w1[0]), "w2_0": f(mlp_w2[0]), "g1": _gl(mix_norm[1]), "w_qkv": f(c_w_qkv[0]),
              "w_o1": f(c_w_o[0]), "gm1": _gl(mlp_norm[1]), "w1_1": f(mlp_w1[1]), "w2_1": f(mlp_w2[1]), "gf": _gl(final_norm)}
    common.update(mix_consts())
    sbc = sb_consts()
    common["cU"] = sbc["cU"]; common["cM"] = sbc["cM"]
    conv_w = f(a_conv_w[0]); a_log = f(a_a_log[0]); dt_bias = f(a_dt_bias[0]); lbl = f(a_lb_logits)
    ins = []
    for h in range(NCORES):
        d = dict(common)
        d["xT"] = np.ascontiguousarray(xT[:, h * TR:(h + 1) * TR])
        d["lbl"] = np.ascontiguousarray(lbl[:, h * 128:(h + 1) * 128].T)
        d["hnorm"] = f(a_hgrn_norm[0]).reshape(128, 1)
        d["gnorm"] = f(a_gdn_norm[0]).reshape(128, 1)
        d["convw"] = np.ascontiguousarray(np.concatenate(
            [conv_w[:, h * 128:(h + 1) * 128].T, conv_w[:, 1024 + h * 128:1024 + (h + 1) * 128].T,
             conv_w[:, 2048 + h * 128:2048 + (h + 1) * 128].T], axis=1))
        d["sc"] = np.ascontiguousarray(np.stack([np.full(128, a_log[h], np.float32), np.full(128, dt_bias[h], np.float32)], axis=1))
        ins.append(d)
    return ins


def kernel(x, mix_norm, a_w_in, a_conv_w, a_a_log, a_dt_bias, a_lb_logits, a_hgrn_norm,
           a_gdn_norm, a_w_out, c_w_qkv, c_w_o, mlp_norm, mlp_w1, mlp_w2, final_norm):
    SEQ = np.asarray(x).shape[1]
    P = build_fused(SEQ)
    nc = P.finish()
    ins = fused_inputs(SEQ, x, mix_norm, a_w_in, a_conv_w, a_a_log, a_dt_bias, a_lb_logits, a_hgrn_norm,
                       a_gdn_norm, a_w_out, c_w_qkv, c_w_o, mlp_norm, mlp_w1, mlp_w2, final_norm)
    res = run_bass_kernel_spmd(nc, ins, core_ids=list(range(NCORES)))
    yT = np.concatenate([r["yT"] for r in res.results], axis=1)
    return np.ascontiguousarray(yT.T)[None].astype(np.float32)
```

```python
import contextlib
import numpy as np
import concourse.bass as bass
import concourse.mybir as mybir
from concourse.bass_utils import run_bass_kernel_spmd

F32 = mybir.dt.float32
BF16 = mybir.dt.bfloat16
AF = mybir.ActivationFunctionType
ALU = mybir.AluOpType
AX = mybir.AxisListType

ENG_EPOCH = 30000
DMA_EPOCH = 1500


class Prog:
    def __init__(self, name="k"):
        self.nc = bass.Bass("TRN2", target_bir_lowering=False)
        self.stack = contextlib.ExitStack()
        self.lists = {k: [] for k in ("pe", "act", "dve", "pool", "sp")}
        self.count = {k: 0 for k in self.lists}
        self.dcount = {}
        self.dnum = {}
        self.clock = {k: {} for k in self.lists}
        self.lastw = {}
        self.readers = {}
        self.semnames = []
        self.nalloc = 0
        self.pstack = None
        self.sems = {}

    def dram(self, name, shape, dtype, kind, **kw):
        return self.nc.dram_tensor(name, list(shape), dtype, kind=kind, **kw).ap()

    def sbuf(self, name, shape, dtype):
        self.nalloc += 1
        st = self.pstack if self.pstack is not None else self.stack
        return st.enter_context(self.nc.sbuf_tensor("%s_%d" % (name, self.nalloc), list(shape), dtype))

    def psum(self, name, shape, dtype=F32):
        self.nalloc += 1
        st = self.pstack if self.pstack is not None else self.stack
        return st.enter_context(self.nc.psum_tensor("%s_%d" % (name, self.nalloc), list(shape), dtype))

    def begin_phase(self):
        self.pstack = contextlib.ExitStack()

    def end_phase(self):
        self.barrier()
        self._emit_block()
        self.pstack.close()
        self.pstack = None

    def barrier(self):
        targets = {}
        for eng, c in self.count.items():
            if c > 0:
                sn = self._esem(eng, c)
                targets[sn] = (c - ((c - 1) // ENG_EPOCH) * ENG_EPOCH, eng)
        for sn, val in self.dcount.items():
            targets[sn] = (val, None)
        for eng in self.lists:
            clk = self.clock[eng]
            for sn, (val, seng) in targets.items():
                if clk.get(sn, 0) >= val:
                    continue
                self.lists[eng].append(("w", sn, val))
                clk[sn] = val
        self.lastw = {}
        self.readers = {}

    def _emit_block(self):
        nc = self.nc
        for sn in self.semnames:
            if sn not in self.sems:
                self.sems[sn] = self.stack.enter_context(nc.semaphore(sn.replace(":", "_")))
        sems = self.sems
        lists = self.lists

        def emit(key, e):
            for it in lists[key]:
                if it[0] == "w":
                    e.wait_ge(sems[it[1]], it[2])
                else:
                    ins = it[1](e)
                    if it[2] is not None:
                        ins.then_inc(sems[it[2]], it[3])

        with nc.Block() as block:
            @block.sync
            def _(e):
                emit("sp", e)

            @block.tensor
            def _(e):
                emit("pe", e)

            @block.scalar
            def _(e):
                emit("act", e)

            @block.vector
            def _(e):
                emit("dve", e)

            @block.gpsimd
            def _(e):
                emit("pool", e)
        self.lists = {k: [] for k in lists}

    def _esem(self, eng, cnt):
        return "%s_%d" % (eng, (cnt - 1) // ENG_EPOCH)

    def _use(self, sn):
        if sn not in self.semnames:
            self.semnames.append(sn)

    PSUM_KEYS = ("PS", "ZP", "TP", "CP", "OP", "PQ", "PF", "PT")

    def op(self, eng, fn, r=(), w=(), dsem=None, inc=True, dinc=16):
        pr = [k for k in r if isinstance(k, tuple) and k[0] in self.PSUM_KEYS]
        if pr:
            r = [k for k in r if k not in pr]
            w = list(w) + [k for k in pr if k not in w]
        deps = []
        for k in r:
            e = self.lastw.get(k)
            if e is not None:
                deps.append(e)
        for k in w:
            e = self.lastw.get(k)
            if e is not None:
                deps.append(e)
            deps.extend(self.readers.get(k, ()))
        clk = self.clock[eng]
        lst = self.lists[eng]
        for (sn, val, snap, seng) in deps:
            if sn.startswith("d:"):
                val = max(val, self.dcount[sn])
            elif seng == eng and eng == "pe":
                continue
            if clk.get(sn, 0) >= val:
                continue
            lst.append(("w", sn, val))
            clk[sn] = val
            for k2, v2 in snap.items():
                if clk.get(k2, 0) < v2:
                    clk[k2] = v2
        if dsem is not None:
            dsem = "%s_%s" % (dsem, eng)
            n = self.dnum.get(dsem, 0)
            self.dnum[dsem] = n + 1
            sn = "d:%s_%d" % (dsem, n // DMA_EPOCH)
            self.dcount[sn] = self.dcount.get(sn, 0) + dinc
            val = self.dcount[sn]
            self._use(sn)
            lst.append(("i", fn, sn, dinc))
        else:
            if inc:
                self.count[eng] += 1
                c = self.count[eng]
                sn = self._esem(eng, c)
                val = c - ((c - 1) // ENG_EPOCH) * ENG_EPOCH
                self._use(sn)
                lst.append(("i", fn, sn, 1))
            else:
                c = self.count[eng] + 1
                sn = self._esem(eng, c)
                val = c - ((c - 1) // ENG_EPOCH) * ENG_EPOCH
                self._use(sn)
                lst.append(("i", fn, None, 0))
        ev = (sn, val, dict(clk), eng)
        for k in r:
            self.readers.setdefault(k, []).append(ev)
        for k in w:
            self.lastw[k] = ev
            self.readers[k] = []
        return ev

    def finish(self):
        if any(self.lists.values()):
            self.barrier()
            self._emit_block()
        if self.pstack is not None:
            self.pstack.close()
            self.pstack = None
        self.stack.close()
        return self.nc

    def stats(self):
        return {k: len(v) for k, v in self.lists.items()}


def _mm(P, out, lhsT, rhs, start, stop, r, w, inc=None):
    P.op("pe", lambda e, out=out, lhsT=lhsT, rhs=rhs, start=start, stop=stop:
         e.matmul(out, lhsT=lhsT, rhs=rhs, start=start, stop=stop),
         r=r, w=w, inc=(stop if inc is None else inc))


def _act(P, out, in_, func, r, w, **kw):
    P.op("act", lambda e, out=out, in_=in_, func=func, kw=kw:
         e.activation(out=out, in_=in_, func=func, **kw), r=r, w=w)


def _dma(P, eng, out, in_, r, w, dsem):
    P.op(eng, lambda e, out=out, in_=in_: e.dma_start(out=out, in_=in_), r=r, w=w, dsem=dsem)


def _dmaf(P, eng, out, in_fn, r, w, dsem, slow=False):
    if slow:
        P.op(eng, lambda e, out=out, in_fn=in_fn: e.dma_start(out=out, in_=in_fn(e), allow_slow_non_contiguous=True),
             r=r, w=w, dsem=dsem)
    else:
        P.op(eng, lambda e, out=out, in_fn=in_fn: e.dma_start(out=out, in_=in_fn(e)), r=r, w=w, dsem=dsem)


D_MODEL = 2048
KC = 16
EPS = 1e-6
TB = 1024
TT = 512


def dense_body(P, T, do_oproj, do_mlp, proj_cols, do_final, do_xout, io):
    D = D_MODEL
    TB = min(1024, T)
    w_o = io.get("w_o"); g_mlp = io.get("g_mlp"); w1 = io.get("w1"); w2 = io.get("w2")
    g_p = io.get("g_p"); w_p = io.get("w_p"); g_f = io.get("g_f")

    X = P.sbuf("X", [128, KC, TB], F32)
    H = P.sbuf("H", [128, KC, TB], BF16)
    WA = [P.sbuf("WA%d" % i, [128, KC, 512], BF16) for i in range(2)]
    if do_mlp:
        WB = [P.sbuf("WB%d" % i, [128, 4, D], BF16) for i in range(2)]
        U = [P.sbuf("U%d" % i, [128, 4, TB], BF16) for i in range(2)]
        TMP = [P.sbuf("TMP%d" % i, [128, TT], F32) for i in range(2)]
    SQ = [P.sbuf("SQ%d" % i, [128, TT], F32) for i in range(2)]
    RS = P.sbuf("RS", [128, TT], F32)
    EV = [P.sbuf("EV%d" % i, [128, TT], F32) for i in range(4)]
    ones = P.sbuf("ones", [128, 128], F32)
    epsT = P.sbuf("epsT", [128, 1], F32)
    G = {}
    PS = [P.psum("ps%d" % i, [128, TT], F32) for i in range(8)]
    st = {"ps": 0, "ev": 0, "tmp": 0}

    def nextps():
        i = st["ps"]
        st["ps"] = (i + 1) % 8
        return PS[i], ("PS", i)

    P.op("pool", lambda e: e.memset(ones[:, :], 1.0), w=["ones"])
    P.op("pool", lambda e: e.memset(epsT[:, :], EPS), w=["eps"])
    for nm, src in (("mlp", g_mlp if do_mlp else None), ("p", g_p if proj_cols else None),
                    ("f", g_f if do_final else None)):
        if src is not None:
            G[nm] = P.sbuf("G" + nm, [128, KC], F32)
            _dma(P, "sp", G[nm][:, :], src[:, :], r=[], w=["G" + nm], dsem="G" + nm)

    def tsl(tt):
        return slice(tt * TT, (tt + 1) * TT)

    def norm_tile(gname, tt, emit):
        ps, pk = nextps()
        for c in range(KC):
            sq = SQ[c % 2]
            _act(P, sq[:, :], X[:, c, tsl(tt)], AF.Square, r=[("X", c, tt)], w=[("SQ", c % 2)])
            _mm(P, ps[:, :], ones[:, :], sq[:, :], c == 0, c == KC - 1,
                r=["ones", ("SQ", c % 2)], w=[pk], inc=True)
        _act(P, RS[:, :], ps[:, :], AF.Sqrt, r=[pk, "eps"], w=["RS"], scale=1.0 / D_MODEL, bias=epsT[:, 0:1])
        P.op("dve", lambda e: e.reciprocal(out=RS[:, :], in_=RS[:, :]), r=["RS"], w=["RS"])
        for c in range(KC):
            emit(c)

    def norm_to_H(gname):
        Gt = G[gname]
        for tt in range(TB // TT):
            def emit(c, tt=tt):
                P.op("dve", lambda e, c=c, tt=tt: e.scalar_tensor_tensor(
                    out=H[:, c, tsl(tt)], in0=X[:, c, tsl(tt)], scalar=Gt[:, c:c + 1], in1=RS[:, :],
                    op0=ALU.mult, op1=ALU.mult),
                    r=[("X", c, tt), "RS", "G" + gname], w=[("H", c, tt)])
            norm_tile(gname, tt, emit)

    wa_i = [0]

    def load_wa(view, c0, wd):
        i = wa_i[0]
        wa_i[0] = (i + 1) % 2
        wa = WA[i]
        _dma(P, "pool", wa[:, :, 0:wd], view[:, :, c0:c0 + wd], r=[], w=[("WA", i)], dsem="WA%d" % i)
        return wa, ("WA", i)

    for blk in range(T // TB):
        t0 = blk * TB
        for c in range(KC):
            _dma(P, "sp", X[:, c, :], io["x"](c, t0, TB), r=io.get("x_keys", []),
                 w=[("X", c, 0), ("X", c, 1)], dsem="X%d" % (c % 2))
        if do_oproj:
            for c in range(KC):
                _dmaf(P, "pool", H[:, c, :], io["o"](c, t0, TB), r=io.get("o_keys", []),
                      w=[("H", c, 0), ("H", c, 1)], dsem="H%d" % (c % 2))
            wv = w_o.rearrange("(k p) n -> p k n", p=128)
            for g in range(D // 512):
                wa, wk = load_wa(wv, g * 512, 512)
                for j in range(4):
                    dc = g * 4 + j
                    for tt in range(TB // TT):
                        ps, pk = nextps()
                        for k in range(KC):
                            _mm(P, ps[:, :], wa[:, k, j * 128:(j + 1) * 128], H[:, k, tsl(tt)], k == 0, k == KC - 1,
                                r=[wk, ("H", k, tt)], w=[pk])
                        P.op("dve", lambda e, ps=ps, dc=dc, tt=tt: e.tensor_tensor(
                            out=X[:, dc, tsl(tt)], in0=ps[:, :], in1=X[:, dc, tsl(tt)], op=ALU.add),
                            r=[pk, ("X", dc, tt)], w=[("X", dc, tt)])
        if do_mlp:
            norm_to_H("mlp")
            w1v = w1.rearrange("(k p) n -> p k n", p=128)
            w2v = w2.rearrange("(j p) n -> p j n", p=128)
            NG = 4 * D // 512

            def first(g):
                wa, wk = load_wa(w1v, g * 512, 512)
                u = U[g % 2]
                for j in range(4):
                    for tt in range(TB // TT):
                        ps, pk = nextps()
                        for k in range(KC):
                            _mm(P, ps[:, :], wa[:, k, j * 128:(j + 1) * 128], H[:, k, tsl(tt)], k == 0, k == KC - 1,
                                r=[wk, ("H", k, tt)], w=[pk])
                        ti = st["tmp"]
                        st["tmp"] = (ti + 1) % 2
                        _act(P, TMP[ti][:, :], ps[:, :], AF.Relu, r=[pk], w=[("TMP", ti)])
                        _act(P, u[:, j, tsl(tt)], TMP[ti][:, :], AF.Square, r=[("TMP", ti)], w=[("U", g % 2, j, tt)])

            def second(g):
                wb = WB[g % 2]
                u = U[g % 2]
                _dma(P, "pool", wb[:, :, :], w2v[:, g * 4:(g + 1) * 4, :], r=[], w=[("WB", g % 2)], dsem="WB%d" % (g % 2))
                for dc in range(KC):
                    for tt in range(TB // TT):
                        ps, pk = nextps()
                        for j in range(4):
                            _mm(P, ps[:, :], wb[:, j, dc * 128:(dc + 1) * 128], u[:, j, tsl(tt)], j == 0, j == 3,
                                r=[("WB", g % 2), ("U", g % 2, j, tt)], w=[pk])
                        P.op("dve", lambda e, ps=ps, dc=dc, tt=tt: e.tensor_tensor(
                            out=X[:, dc, tsl(tt)], in0=ps[:, :], in1=X[:, dc, tsl(tt)], op=ALU.add),
                            r=[pk, ("X", dc, tt)], w=[("X", dc, tt)])

            first(0)
            for g in range(NG):
                if g + 1 < NG:
                    first(g + 1)
                second(g)
        if do_xout:
            for c in range(KC):
                _dma(P, "sp", io["x_dst"](c, t0, TB), X[:, c, :],
                     r=[("X", c, 0), ("X", c, 1)], w=io.get("x_dst_keys", []), dsem="XO")
        if proj_cols:
            norm_to_H("p")
            wv = w_p.rearrange("(k p) n -> p k n", p=128)
            c0 = 0
            pend = None
            while c0 < proj_cols:
                wd = min(512, proj_cols - c0)
                wa, wk = load_wa(wv, c0, wd)
                if pend is not None:
                    io["proj_group_done"](*pend)
                    pend = None
                off = 0
                while off < wd:
                    cw = min(128, wd - off)
                    for tt in range(TB // TT):
                        ps, pk = nextps()
                        for k in range(KC):
                            _mm(P, ps[0:cw, :], wa[:, k, off:off + cw], H[:, k, tsl(tt)], k == 0, k == KC - 1,
                                r=[wk, ("H", k, tt)], w=[pk])
                        ei = st["ev"]
                        st["ev"] = (ei + 1) % 4
                        _act(P, EV[ei][0:cw, :], ps[0:cw, :], AF.Copy, r=[pk], w=[("EV", ei)])
                        _dma(P, "sp", io["proj_dst"](c0 + off, cw, t0 + tt * TT, TT), EV[ei][0:cw, :],
                             r=[("EV", ei)], w=io.get("proj_keys", []), dsem="EV%d" % ei)
                    off += cw
                c0 += wd
                if "proj_group_done" in io:
                    pend = (c0, blk == T // TB - 1)
            if pend is not None:
                io["proj_group_done"](*pend)
        if do_final:
            Gt = G["f"]
            for tt in range(TB // TT):
                def emit(c, tt=tt):
                    ei = st["ev"]
                    st["ev"] = (ei + 1) % 4
                    P.op("dve", lambda e, c=c, tt=tt, ei=ei: e.scalar_tensor_tensor(
                        out=EV[ei][:, :], in0=X[:, c, tsl(tt)], scalar=Gt[:, c:c + 1], in1=RS[:, :],
                        op0=ALU.mult, op1=ALU.mult),
                        r=[("X", c, tt), "RS", "Gf"], w=[("EV", ei)])
                    _dma(P, "sp", io["y_dst"](c, t0 + tt * TT, TT), EV[ei][:, :],
                         r=[("EV", ei)], w=[], dsem="EV%d" % ei)
                norm_tile("f", tt, emit)


def proj_body(P, T, proj_cols, io):
    D = D_MODEL
    TB = min(1024, T)
    NTT = T // TT
    g_p = io["g_p"]; w_p = io["w_p"]
    X = P.sbuf("X", [128, KC, TB], F32)
    H = P.sbuf("H", [128, KC, T], BF16)
    WA = [P.sbuf("WA%d" % i, [128, KC, 512], BF16) for i in range(2)]
    SQ = [P.sbuf("SQ%d" % i, [128, TT], F32) for i in range(2)]
    RS = P.sbuf("RS", [128, TT], F32)
    EV = [P.sbuf("EV%d" % i, [128, TT], F32) for i in range(4)]
    ones = P.sbuf("ones", [128, 128], F32)
    epsT = P.sbuf("epsT", [128, 1], F32)
    Gt = P.sbuf("Gp", [128, KC], F32)
    PS = [P.psum("ps%d" % i, [128, TT], F32) for i in range(8)]
    st = {"ps": 0, "ev": 0, "wa": 0}

    def nextps():
        i = st["ps"]
        st["ps"] = (i + 1) % 8
        return PS[i], ("PS", i)

    P.op("pool", lambda e: e.memset(ones[:, :], 1.0), w=["ones"])
    P.op("pool", lambda e: e.memset(epsT[:, :], EPS), w=["eps"])
    _dma(P, "sp", Gt[:, :], g_p[:, :], r=[], w=["Gp"], dsem="Gp")
    for blk in range(T // TB):
        t0 = blk * TB
        for c in range(KC):
            _dma(P, "sp", X[:, c, :], io["x"](c, t0, TB), r=[], w=[("X", c, 0), ("X", c, 1)], dsem="X%d" % (c % 2))
        for tt in range(TB // TT):
            xs = slice(tt * TT, (tt + 1) * TT)
            gt = blk * (TB // TT) + tt
            hs = slice(gt * TT, (gt + 1) * TT)
            ps, pk = nextps()
            for c in range(KC):
                sq = SQ[c % 2]
                _act(P, sq[:, :], X[:, c, xs], AF.Square, r=[("X", c, tt)], w=[("SQ", c % 2)])
                _mm(P, ps[:, :], ones[:, :], sq[:, :], c == 0, c == KC - 1, r=["ones", ("SQ", c % 2)], w=[pk], inc=True)
            _act(P, RS[:, :], ps[:, :], AF.Sqrt, r=[pk, "eps"], w=["RS"], scale=1.0 / D_MODEL, bias=epsT[:, 0:1])
            P.op("dve", lambda e: e.reciprocal(out=RS[:, :], in_=RS[:, :]), r=["RS"], w=["RS"])
            for c in range(KC):
                P.op("dve", lambda e, c=c, xs=xs, hs=hs: e.scalar_tensor_tensor(
                    out=H[:, c, hs], in0=X[:, c, xs], scalar=Gt[:, c:c + 1], in1=RS[:, :], op0=ALU.mult, op1=ALU.mult),
                    r=[("X", c, tt), "RS", "Gp"], w=[("H", c, gt)])
    wv = w_p.rearrange("(k p) n -> p k n", p=128)
    c0 = 0
    pend = None
    while c0 < proj_cols:
        wd = min(512, proj_cols - c0)
        i = st["wa"]
        st["wa"] = (i + 1) % 2
        wa = WA[i]
        wk = ("WA", i)
        _dma(P, "pool", wa[:, :, 0:wd], wv[:, :, c0:c0 + wd], r=[], w=[wk], dsem="WA%d" % i)
        if pend is not None:
            io["proj_group_done"](pend, True)
            pend = None
        off = 0
        while off < wd:
            cw = min(128, wd - off)
            for gt in range(NTT):
                hs = slice(gt * TT, (gt + 1) * TT)
                ps, pk = nextps()
                for k in range(KC):
                    _mm(P, ps[0:cw, :], wa[:, k, off:off + cw], H[:, k, hs], k == 0, k == KC - 1, r=[wk, ("H", k, gt)], w=[pk])
                ei = st["ev"]
                st["ev"] = (ei + 1) % 4
                _act(P, EV[ei][0:cw, :], ps[0:cw, :], AF.Copy, r=[pk], w=[("EV", ei)])
                _dma(P, "sp", io["proj_dst"](c0 + off, cw, gt * TT, TT), EV[ei][0:cw, :],
                     r=[("EV", ei)], w=io.get("proj_keys", []), dsem="EV%d" % ei)
            off += cw
        c0 += wd
        if "proj_group_done" in io:
            pend = c0
    if pend is not None:
        io["proj_group_done"](pend, True)


def build_dense(T, do_oproj, do_mlp, proj_cols, do_final, do_xout):
    P = Prog()
    D = D_MODEL
    io = {}
    xT = P.dram("xT", [D, T], F32, "ExternalInput")
    io["x"] = lambda c, t0, n: xT[c * 128:(c + 1) * 128, t0:t0 + n]
    if do_oproj:
        oT = P.dram("oT", [D, T], F32, "ExternalInput")
        io["w_o"] = P.dram("w_o", [D, D], F32, "ExternalInput")
        io["o"] = lambda c, t0, n: (lambda e: oT[c * 128:(c + 1) * 128, t0:t0 + n])
    if do_mlp:
        io["g_mlp"] = P.dram("g_mlp", [128, KC], F32, "ExternalInput")
        io["w1"] = P.dram("w1", [D, 4 * D], F32, "ExternalInput")
        io["w2"] = P.dram("w2", [4 * D, D], F32, "ExternalInput")
    if proj_cols:
        io["g_p"] = P.dram("g_p", [128, KC], F32, "ExternalInput")
        io["w_p"] = P.dram("w_p", [D, proj_cols], F32, "ExternalInput")
        projT = P.dram("projT", [proj_cols, T], F32, "ExternalOutput")
        io["proj_dst"] = lambda r0, cw, tc, n: projT[r0:r0 + cw, tc:tc + n]
    if do_final:
        io["g_f"] = P.dram("g_f", [128, KC], F32, "ExternalInput")
        yT = P.dram("yT", [D, T], F32, "ExternalOutput")
        io["y_dst"] = lambda c, tc, n: yT[c * 128:(c + 1) * 128, tc:tc + n]
    if do_xout:
        xoT = P.dram("xoT", [D, T], F32, "ExternalOutput")
        io["x_dst"] = lambda c, t0, n: xoT[c * 128:(c + 1) * 128, t0:t0 + n]
    P.begin_phase()
    dense_body(P, T, do_oproj, do_mlp, proj_cols, do_final, do_xout, io)
    P.end_phase()
    return P


def sb_body(P, T, NH, TR, io):
    QT = 512
    nq = T // QT
    nkb = T // 128
    NR = T // TR
    dbg = None

    def dump(nm, src, key):
        return

    Q = P.sbuf("Q", [128, T], BF16)
    Kt = P.sbuf("Kt", [128, T], BF16)
    V = P.sbuf("V", [128, nkb, 128], BF16)
    QS = [P.sbuf("QS%d" % i, [128, TR], F32) for i in range(2)]
    VS = P.sbuf("VS", [128, TR], BF16)
    Idf = P.sbuf("Idf", [128, 128], F32)
    Idb = P.sbuf("Idb", [128, 128], BF16)
    Un = P.sbuf("Un", [128, 128], BF16)
    On = P.sbuf("On", [128, 128], BF16)
    MK = P.sbuf("MK", [128, 4, QT], BF16)
    one1 = P.sbuf("one1", [128, 1], F32)
    E = [P.sbuf("E%d" % i, [128, QT], F32) for i in range(2)]
    S = [P.sbuf("S%d" % i, [128, QT], BF16) for i in range(3)]
    TS = [P.sbuf("TS%d" % i, [128, QT], F32) for i in range(2)]
    A = [P.sbuf("A%d" % i, [128, QT], BF16) for i in range(3)]
    CS = [P.sbuf("CS%d" % i, [128, QT], F32) for i in range(2)]
    OE = [P.sbuf("OE%d" % i, [128, QT], F32) for i in range(2)]
    ZP = [P.psum("zp%d" % i, [128, QT]) for i in range(2)]
    TP = [P.psum("tp%d" % i, [128, QT]) for i in range(2)]
    CP = [P.psum("cp%d" % i, [128, QT]) for i in range(2)]
    OP = [P.psum("op%d" % i, [128, QT]) for i in range(1)]
    PT = P.psum("ptb", [128, 1024], BF16)
    cU = io["cU"]; cM = io["cM"]; cI = io["cI"]

    _dma(P, "pool", Un[:, :], cU[:, :], r=[], w=["Un"], dsem="c0")
    _dma(P, "pool", MK[:, :, :], cM.rearrange("j p q -> p j q"), r=[], w=["MK"], dsem="c0")
    _dma(P, "sp", Idf[:, :], cI[:, :], r=[], w=["Idf"], dsem="c0")
    P.op("dve", lambda e: e.tensor_copy(out=Idb[:, :], in_=Idf[:, :]), r=["Idf"], w=["Idb"])
    P.op("pool", lambda e: e.memset(On[:, :], 1.0), w=["On"])
    P.op("pool", lambda e: e.memset(one1[:, :], 1.0), w=["one1"])
    scale = 128.0 ** -0.5
    cnt = {"e": 0, "s": 0, "t": 0, "a": 0, "c": 0, "z": 0, "tp": 0}

    for h in range(NH):
        si = 0
        for i in range(NR):
            qs = QS[si % 2]
            _dmaf(P, "sp", qs[:, :], io["q"](h, i), r=io["in_keys"], w=[("QS", si % 2)], dsem="QS%d" % (si % 2))
            P.op("dve", lambda e, qs=qs, i=i: e.tensor_scalar(out=Q[:, i * TR:(i + 1) * TR], in0=qs[:, :], scalar1=scale,
                                                              scalar2=None, op0=ALU.mult),
                 r=[("QS", si % 2)], w=["Q"])
            si += 1
            _dmaf(P, "pool", Kt[:, i * TR:(i + 1) * TR], io["k"](h, i), r=io["in_keys"], w=["Kt"], dsem="Kt")
            qs = QS[si % 2]
            _dmaf(P, "sp", qs[:, :], io["v"](h, i), r=io["in_keys"], w=[("QS", si % 2)], dsem="QS%d" % (si % 2))
            P.op("dve", lambda e, qs=qs: e.tensor_copy(out=VS[:, :], in_=qs[:, :]), r=[("QS", si % 2)], w=["VS"])
            si += 1
            for b in range(TR // 128):
                P.op("pe", lambda e, b=b: e.transpose(PT[:, 0:128], VS[:, b * 128:(b + 1) * 128], Idb[:, :]),
                     r=["VS", "Idb"], w=[("PT", 0)])
                _act(P, V[:, i * (TR // 128) + b, :], PT[:, 0:128], AF.Copy, r=[("PT", 0)], w=["V"])

        for qt in range(nq):
            q0 = qt * QT
            qsl = slice(q0, q0 + QT)
            kbs = list(range(4 * qt + 3, -1, -1))
            n = len(kbs)
            op_ = OP[0]
            ok = ("OP", 0)
            info = {}

            def stageA1(i):
                kb = kbs[i]
                zi = cnt["z"] % 2
                cnt["z"] += 1
                ksl = slice(kb * 128, (kb + 1) * 128)
                _mm(P, ZP[zi][:, :], Kt[:, ksl], Q[:, qsl], True, True, r=["Kt", "Q"], w=[("ZP", zi)])
                info[i] = (kb, zi, ksl)

            def stageA2(i):
                kb, zi, ksl = info[i]
                ei = cnt["e"] % 2
                cnt["e"] += 1
                si = cnt["s"] % 3
                cnt["s"] += 1
                _act(P, E[ei][:, :], ZP[zi][:, :], AF.Exp, r=[("ZP", zi)], w=[("E", ei)])
                _act(P, S[si][:, :], E[ei][:, :], AF.Ln, r=[("E", ei), "one1"], w=[("S", si)], bias=one1[:, 0:1])
                j = kb - 4 * qt
                if j >= 0:
                    P.op("dve", lambda e, si=si, j=j: e.tensor_tensor(out=S[si][:, :], in0=S[si][:, :], in1=MK[:, j, :], op=ALU.mult),
                         r=[("S", si), "MK"], w=[("S", si)])
                info[i] = (kb, si, ksl, j)

            def stageB(i):
                kb, si, ksl, j = info[i]
                ti = cnt["tp"] % 2
                cnt["tp"] += 1
                _mm(P, TP[ti][:, :], Kt[:, ksl], Q[:, qsl], True, False, r=["Kt", "Q"], w=[("TP", ti)], inc=False)
                _mm(P, TP[ti][:, :], Un[:, :], S[si][:, :], False, True, r=["Un", ("S", si)], w=[("TP", ti)])
                src = TP[ti]
                srck = ("TP", ti)
                if i > 0:
                    tsi = cnt["t"] % 2
                    cnt["t"] += 1
                    P.op("dve", lambda e, tsi=tsi, ti=ti, ci=(i - 1) % 2: e.tensor_tensor(
                        out=TS[tsi][:, :], in0=TP[ti][:, :], in1=CS[ci][:, :], op=ALU.subtract),
                        r=[("TP", ti), ("CS", (i - 1) % 2)], w=[("TS", tsi)])
                    src = TS[tsi]
                    srck = ("TS", tsi)
                if i < n - 1:
                    cpi = cnt["c"] % 2
                    cnt["c"] += 1
                    _mm(P, CP[cpi][:, :], On[:, :], S[si][:, :], True, True, r=["On", ("S", si)], w=[("CP", cpi)])
                    if i == 0:
                        P.op("dve", lambda e, cpi=cpi: e.tensor_copy(out=CS[0][:, :], in_=CP[cpi][:, :]),
                             r=[("CP", cpi)], w=[("CS", 0)])
                    else:
                        P.op("dve", lambda e, cpi=cpi, a=i % 2, b=(i - 1) % 2: e.tensor_tensor(
                            out=CS[a][:, :], in0=CP[cpi][:, :], in1=CS[b][:, :], op=ALU.add),
                            r=[("CP", cpi), ("CS", (i - 1) % 2)], w=[("CS", i % 2)])
                info[i] = (kb, si, ksl, j, src, srck)

            def stageC(i):
                kb, si, ksl, j, src, srck = info[i]
                ai = cnt["a"] % 3
                cnt["a"] += 1
                _act(P, A[ai][:, :], src[:, :], AF.Exp, r=[srck], w=[("A", ai)])
                if j >= 0:
                    P.op("dve", lambda e, ai=ai, j=j: e.tensor_tensor(out=A[ai][:, :], in0=A[ai][:, :], in1=MK[:, j, :], op=ALU.mult),
                         r=[("A", ai), "MK"], w=[("A", ai)])
                _mm(P, op_[:, :], V[:, kb, :], A[ai][:, :], i == 0, i == n - 1, r=["V", ("A", ai)], w=[ok], inc=True)

            for step in range(-1, n + 2):
                if 0 <= step + 1 < n:
                    stageA1(step + 1)
                if 0 <= step < n:
                    stageA2(step)
                if 0 <= step - 1 < n:
                    stageB(step - 1)
                if 0 <= step - 2 < n:
                    stageC(step - 2)
            oe = OE[qt % 2]
            P.op("dve", lambda e, oe=oe, op_=op_: e.tensor_copy(out=oe[:, :], in_=op_[:, :]), r=[ok], w=[("OE", qt % 2)])
            _dma(P, "sp", io["o_dst"](h, q0, QT), oe[:, :], r=[("OE", qt % 2)],
                 w=(io["o_wkeys"](h, q0) if "o_wkeys" in io else []), dsem="OE%d" % (qt % 2))
            if "o_done" in io:
                io["o_done"](h, qt)


def build_sb(T, NH, dbg=None):
    P = Prog()
    qT = P.dram("qT", [NH, 128, T], F32, "ExternalInput")
    kT = P.dram("kT", [NH, 128, T], F32, "ExternalInput")
    vT = P.dram("vT", [NH, 128, T], F32, "ExternalInput")
    oT = P.dram("oT", [NH, 128, T], F32, "ExternalOutput")
    TR = 2048
    io = {"cU": P.dram("cU", [128, 128], F32, "ExternalInput"), "cM": P.dram("cM", [4, 128, 512], F32, "ExternalInput"),
          "cI": P.dram("cI", [128, 128], F32, "ExternalInput"), "in_keys": [],
          "q": lambda h, i: (lambda e: qT[h, :, i * TR:(i + 1) * TR]),
          "k": lambda h, i: (lambda e: kT[h, :, i * TR:(i + 1) * TR]),
          "v": lambda h, i: (lambda e: vT[h, :, i * TR:(i + 1) * TR]),
          "o_dst": lambda h, q0, n: oT[h, :, q0:q0 + n]}
    P.begin_phase()
    sb_body(P, T, NH, TR, io)
    P.end_phase()
    return P


def sb_consts():
    j = np.arange(128)[:, None]
    k = np.arange(128)[None, :]
    cU = np.where(j >= k, -1.0, 0.0).astype(np.float32)
    kl = np.arange(128)[None, :, None]
    ql = np.arange(512)[None, None, :]
    jj = np.arange(4)[:, None, None]
    cM = ((128 * jj + kl) < ql).astype(np.float32)
    return {"cU": cU, "cM": np.ascontiguousarray(cM), "cI": np.eye(128, dtype=np.float32)}


CH = 64
ST = 512


def mix_consts():
    p = np.arange(128)
    same = (p[:, None] // CH) == (p[None, :] // CH)
    incl = (same & (p[:, None] <= p[None, :])).astype(np.float32)
    strict = (same & (p[:, None] < p[None, :])).astype(np.float32)
    ident = np.eye(128, dtype=np.float32)
    seg = np.ones((128, ST), np.float32)
    seg[:, ::CH] = 0.0
    sel = (p[:, None] == ((p[None, :] // CH) * CH + CH - 1)).astype(np.float32)
    return {"c_incl": incl, "c_strict": strict, "c_ident": ident, "c_ltri": incl.copy(), "c_seg": seg, "c_sel": sel}


def mix_body(P, T, TR, io):
    lvl = 99
    NP = T // 128
    NPR = TR // 128
    lbl = io["lbl"]; hnorm = io["hnorm"]; convw = io["convw"]; sc = io["sc"]; gnorm = io["gnorm"]
    c_incl = io["c_incl"]; c_strict = io["c_strict"]; c_ident = io["c_ident"]; c_ltri = io["c_ltri"]; c_seg = io["c_seg"]
    c_sel = io["c_sel"]
    IK = io["in_keys"]

    def rows(kind, t0, n, lo=0):
        r_ = t0 // TR
        return io["rows"](kind, r_, t0 - r_ * TR + lo, n)

    def sb(name, shape, dt=F32):
        return P.sbuf(name, shape, dt)

    Mincl = sb("Mincl", [128, 128]); Mstr = sb("Mstr", [128, 128]); Id = sb("Id", [128, 128])
    Ltri = sb("Ltri", [128, 128]); Seg = sb("Seg", [128, ST]); ones = sb("ones", [128, 128])
    epsT = sb("epsT", [128, 1]); one1 = sb("one1", [128, 1])
    LBL = sb("LBL", [128, 2]); LB = sb("LB", [128, 1]); OML = sb("OML", [128, 1]); HN = sb("HN", [128, 1])
    CW = sb("CW", [128, 12]); SC = sb("SC", [128, 2]); NEGA = sb("NEGA", [128, 1]); GN = sb("GN", [128, 1])
    for t_, d_, k_ in ((Mincl, c_incl, "Mincl"), (Mstr, c_strict, "Mstr"), (Id, c_ident, "Id"), (Ltri, c_ltri, "Ltri"),
                       (Seg, c_seg, "Seg"), (LBL, lbl, "LBL"), (HN, hnorm, "HN"), (CW, convw, "CW"), (SC, sc, "SC"),
                       (GN, gnorm, "GN")):
        _dma(P, "sp", t_[:, :], d_[:, :], r=[], w=[k_], dsem="c_" + k_)
    P.op("pool", lambda e: e.memset(ones[:, :], 1.0), w=["ones"])
    P.op("pool", lambda e: e.memset(epsT[:, :], EPS), w=["eps"])
    P.op("pool", lambda e: e.memset(one1[:, :], 1.0), w=["one1"])

    def tt(eng, out_, a, b, op, r, w):
        P.op(eng, lambda e, out_=out_, a=a, b=b, op=op: e.tensor_tensor(out=out_, in0=a, in1=b, op=op), r=r, w=w)

    def ts(eng, out_, a, s1, s2, op0, op1, r, w):
        if s2 is None:
            P.op(eng, lambda e, out_=out_, a=a, s1=s1, op0=op0: e.tensor_scalar(out=out_, in0=a, scalar1=s1, scalar2=None, op0=op0),
                 r=r, w=w)
        else:
            P.op(eng, lambda e, out_=out_, a=a, s1=s1, s2=s2, op0=op0, op1=op1:
                 e.tensor_scalar(out=out_, in0=a, scalar1=s1, scalar2=s2, op0=op0, op1=op1), r=r, w=w)

    def stt(out_, a, s, b, op0, op1, r, w):
        P.op("dve", lambda e, out_=out_, a=a, s=s, b=b, op0=op0, op1=op1:
             e.scalar_tensor_tensor(out=out_, in0=a, scalar=s, in1=b, op0=op0, op1=op1), r=r, w=w)

    def cp(eng, out_, a, r, w):
        if eng == "act":
            _act(P, out_, a, AF.Copy, r=r, w=w)
        else:
            P.op(eng, lambda e, out_=out_, a=a: e.tensor_copy(out=out_, in_=a), r=r, w=w)

    def tr(out_, a, r, w):
        P.op("pe", lambda e, out_=out_, a=a: e.transpose(out_, a, Idb[:, :]), r=r + ["Idb"], w=w)

    PSB = [P.psum("psb%d" % i, [128, 512]) for i in range(7)]
    PST = P.psum("psbT", [128, 1024], BF16)
    pst = {"q": 0, "f": 0, "t": 0}
    Idb = sb("Idb", [128, 128], BF16)
    cp("pool", Idb[:, :], Id[:, :], r=["Id"], w=["Idb"])

    def ptq():
        return PST[:, 0:128], ("PT", 0)

    def pq():
        i = pst["q"]
        pst["q"] = (i + 1) % 5
        return PSB[i][:, 0:128], ("PQ", i)

    def pf():
        i = pst["f"]
        pst["f"] = (i + 1) % 2
        return PSB[5 + i][:, :], ("PF", i)

    tt("dve", LB[:, :], LBL[:, 0:1], LBL[:, 1:2], ALU.subtract, r=["LBL"], w=["LB"])
    _act(P, LB[:, :], LB[:, :], AF.Sigmoid, r=["LB"], w=["LB"])
    ts("dve", OML[:, :], LB[:, :], -1.0, 1.0, ALU.mult, ALU.add, r=["LB"], w=["OML"])
    _act(P, NEGA[:, :], SC[:, 0:1], AF.Exp, r=["SC"], w=["NEGA"])
    ts("dve", NEGA[:, :], NEGA[:, :], -1.0, None, ALU.mult, None, r=["NEGA"], w=["NEGA"])

    GAC = sb("GAC", [128, NP]); GBC = sb("GBC", [128, NP]); GCC = sb("GCC", [128, NP])
    BEC = sb("BEC", [128, NP]); KBEC = sb("KBEC", [128, NP]); KDC = sb("KDC", [128, NP])
    for r_ in range(T // TR):
        for which, dstT, dk in ((0, GAC, "GAC"), (1, GBC, "GBC")):
            _dmaf(P, "sp", dstT[:, r_ * NPR:(r_ + 1) * NPR],
                  (lambda e, f=io["ab"](which, r_, 0, TR): f(e).rearrange("o (b p) -> p (o b)", p=128)),
                  r=IK, w=[dk], dsem="c_gac", slow=True)
    _act(P, GAC[:, :], GAC[:, :], AF.Exp, r=["GAC", "SC"], w=["GAC"], bias=SC[:, 1:2])
    _act(P, GAC[:, :], GAC[:, :], AF.Ln, r=["GAC", "one1"], w=["GAC"], bias=one1[:, 0:1])
    ts("dve", GAC[:, :], GAC[:, :], NEGA[:, 0:1], None, ALU.mult, None, r=["GAC", "NEGA"], w=["GAC"])
    _act(P, BEC[:, :], GBC[:, :], AF.Sigmoid, r=["GBC"], w=["BEC"])
    for n0 in range(0, NP, 512):
        n1 = min(NP, n0 + 512)
        ps, pk = pf()
        _mm(P, ps[:, 0:n1 - n0], Ltri[:, :], GAC[:, n0:n1], True, True, r=["Ltri", "GAC"], w=[pk])
        cp("dve", GCC[:, n0:n1], ps[:, 0:n1 - n0], r=[pk], w=["GCC"])
    _act(P, KBEC[:, :], GCC[:, :], AF.Exp, r=["GCC"], w=["KBEC"])
    tt("dve", KBEC[:, :], KBEC[:, :], BEC[:, :], ALU.mult, r=["KBEC", "BEC"], w=["KBEC"])
    SEL = sb("SEL", [128, 128])
    _dma(P, "sp", SEL[:, :], c_sel[:, :], r=[], w=["SEL"], dsem="c_sel")
    for n0 in range(0, NP, 512):
        n1 = min(NP, n0 + 512)
        ps, pk = pf()
        _mm(P, ps[:, 0:n1 - n0], SEL[:, :], GCC[:, n0:n1], True, True, r=["SEL", "GCC"], w=[pk])
        tt("dve", KDC[:, n0:n1], ps[:, 0:n1 - n0], GCC[:, n0:n1], ALU.subtract, r=[pk, "GCC"], w=["KDC"])
    _act(P, KDC[:, :], KDC[:, :], AF.Exp, r=["KDC"], w=["KDC"])

    if lvl == 0:
        return P
    HI = sb("HI", [128, ST]); HIb = sb("HIb", [128, ST], BF16); HVp = sb("HVp", [128, 128], BF16)

    SAf = sb("SAf", [128, 128]); SAb = sb("SAb", [128, 128], BF16)
    SBf = sb("SBf", [128, 128]); SBb = sb("SBb", [128, 128], BF16)
    for t_, k_ in ((SAf, "SAf"), (SAb, "SAb"), (SBf, "SBf"), (SBb, "SBb")):
        P.op("pool", lambda e, t_=t_: e.memset(t_[:, :], 0.0), w=[k_])

    def T512(name, dt=F32, w=ST):
        return sb(name, [128, w], dt)
    HQ = T512("HQ"); HF = T512("HF"); HG = T512("HG")
    T1 = T512("T1"); T2 = T512("T2"); T3 = T512("T3"); T4 = T512("T4"); T5 = T512("T5")
    QBh = T512("QBh", BF16); KBh = T512("KBh", BF16)
    OA = T512("OA"); OSQ = T512("OSQ"); ORS = T512("ORS")
    UQ = T512("UQ", F32, ST + 3); UK = T512("UK", F32, ST + 3); UV = T512("UV", F32, ST + 3); GG = T512("GG")
    GAb = T512("GAb"); GBb = T512("GBb")
    CQ = T512("CQ"); CK = T512("CK"); CV = T512("CV"); RQ = T512("RQ")
    GC = T512("GC"); EGC = T512("EGC"); BETA = T512("BETA")
    QDb = T512("QDb", BF16); KNb = T512("KNb", BF16); QNb = T512("QNb", BF16); CVb = T512("CVb", BF16)
    OB = T512("OB")
    def T128(name, dt=F32):
        return sb(name, [128, 128], dt)
    AM = T128("AM", BF16); K2 = T128("K2", BF16); K2t = T128("K2t", BF16)
    EXD = T128("EXD"); DECI = T128("DECI"); DECS = T128("DECS")
    ATf = T128("ATf"); ATb = T128("ATb", BF16); Ab = T128("Ab", BF16)
    Pb = [T128("Pb%d" % i, BF16) for i in range(2)]; PTb = [T128("PTb%d" % i, BF16) for i in range(2)]
    TTf = T128("TTf"); TTb = T128("TTb", BF16)
    KBEt = T128("KBEt", BF16); KDt = T128("KDt", BF16); VBt = T128("VBt", BF16)
    Ut = T128("Ut"); WTb = T128("WTb", BF16); VN = T128("VN", BF16); QKT = T128("QKT", BF16)

    def gated_out(O, gate_src_key, G_in, W_col, wkey, oi, t0):
        ok_ = "O%d" % oi
        _act(P, OSQ[:, :], O[:, :], AF.Square, r=[ok_], w=["OSQ"])
        ps, pk = pf()
        _mm(P, ps, ones[:, :], OSQ[:, :], True, True, r=["ones", "OSQ"], w=[pk])
        _act(P, ORS[:, :], ps, AF.Sqrt, r=[pk, "eps"], w=["ORS"], scale=1.0 / 128, bias=epsT[:, 0:1])
        P.op("dve", lambda e: e.reciprocal(out=ORS[:, :], in_=ORS[:, :]), r=["ORS"], w=["ORS"])
        stt(O[:, :], O[:, :], W_col[:, 0:1], ORS[:, :], ALU.mult, ALU.mult, r=[ok_, "ORS", wkey], w=[ok_])
        _act(P, G_in[:, :], G_in[:, :], AF.Silu, r=[gate_src_key], w=[gate_src_key])
        tt("dve", O[:, :], O[:, :], G_in[:, :], ALU.mult, r=[ok_, gate_src_key], w=[ok_])
        _dma(P, "sp", io["o_dst"](oi, t0, ST), O[:, :], r=[ok_], w=(io["o_wkeys"](t0) if "o_wkeys" in io else []), dsem="out%d" % oi)

    for st_i in range(T // ST):
        t0 = st_i * ST
        _dmaf(P, "sp", HQ[:, :], rows(0, t0, ST), r=IK, w=["HQ"], dsem="HQ")
        _dmaf(P, "sp", HF[:, :], rows(1, t0, ST), r=IK, w=["HF"], dsem="HF")
        _dmaf(P, "sp", HG[:, :], rows(3, t0, ST), r=IK, w=["HG"], dsem="HG")
        _dmaf(P, "sp", HI[:, :], rows(2, t0, ST), r=IK, w=["HI"], dsem="HI")
        cp("pool", HIb[:, :], HI[:, :], r=["HI"], w=["HIb"])
        _act(P, T1[:, :], HF[:, :], AF.Sigmoid, r=["HF"], w=["T1"])
        ts("dve", T1[:, :], T1[:, :], OML[:, 0:1], LB[:, 0:1], ALU.mult, ALU.add, r=["T1", "OML", "LB"], w=["T1"])
        _act(P, T2[:, :], T1[:, :], AF.Ln, r=["T1"], w=["T2"])
        ts("dve", T1[:, :], T1[:, :], -1.0, 1.0, ALU.mult, ALU.add, r=["T1"], w=["T1"])
        P.op("dve", lambda e: e.tensor_tensor_scan(out=T3[:, :], data0=Seg[:, :], data1=T2[:, :], initial=0.0,
                                                   op0=ALU.mult, op1=ALU.add), r=["Seg", "T2"], w=["T3"])
        _act(P, T2[:, :], T3[:, :], AF.Exp, r=["T3"], w=["T2"])
        _act(P, T4[:, :], T3[:, :], AF.Exp, r=["T3"], w=["T4"], scale=-1.0)
        _act(P, T5[:, :], HQ[:, :], AF.Silu, r=["HQ"], w=["T5"])
        tt("dve", QBh[:, :], T5[:, :], T2[:, :], ALU.mult, r=["T5", "T2"], w=["QBh"])
        tt("dve", T1[:, :], T1[:, :], T4[:, :], ALU.mult, r=["T1", "T4"], w=["T1"])
        cp("pool", KBh[:, :], T1[:, :], r=["T1"], w=["KBh"])
        if lvl == 1:
            return P
        for U_, gi, k_ in ((UQ, 4, "UQ"), (UK, 5, "UK"), (UV, 6, "UV")):
            if t0 == 0:
                P.op("pool", lambda e, U_=U_: e.memset(U_[:, 0:3], 0.0), w=[k_])
                _dmaf(P, "sp", U_[:, 3:ST + 3], rows(gi, 0, ST), r=IK, w=[k_], dsem=k_)
            elif t0 % TR == 0:
                _dmaf(P, "sp", U_[:, 0:3], rows(gi, t0 - ST, 3, lo=ST - 3), r=IK, w=[k_], dsem=k_)
                _dmaf(P, "sp", U_[:, 3:ST + 3], rows(gi, t0, ST), r=IK, w=[k_], dsem=k_)
            else:
                _dmaf(P, "sp", U_[:, :], rows(gi, t0, ST + 3, lo=-3), r=IK, w=[k_], dsem=k_)
        _dmaf(P, "sp", GG[:, :], rows(7, t0, ST), r=IK, w=["GG"], dsem="GG")
        r_ = t0 // TR
        _dmaf(P, "sp", GAb[:, :], (lambda e, f=io["ab"](0, r_, t0 - r_ * TR, ST): f(e).partition_broadcast(128)), r=IK, w=["GAb"], dsem="GAb")
        _dmaf(P, "sp", GBb[:, :], (lambda e, f=io["ab"](1, r_, t0 - r_ * TR, ST): f(e).partition_broadcast(128)), r=IK, w=["GBb"], dsem="GAb")
        for U_, C_, k_, ck_, wi in ((UQ, CQ, "UQ", "CQ", 0), (UK, CK, "UK", "CK", 4), (UV, CV, "UV", "CV", 8)):
            ts("dve", C_[:, :], U_[:, 0:ST], CW[:, wi:wi + 1], None, ALU.mult, None, r=[k_, "CW"], w=[ck_])
            for j in (1, 2, 3):
                stt(C_[:, :], U_[:, j:j + ST], CW[:, wi + j:wi + j + 1], C_[:, :], ALU.mult, ALU.add, r=[k_, "CW", ck_], w=[ck_])
            _act(P, C_[:, :], C_[:, :], AF.Silu, r=[ck_], w=[ck_])
        for C_, ck_, scl in ((CQ, "CQ", 128.0 ** -0.5), (CK, "CK", 1.0)):
            _act(P, OSQ[:, :], C_[:, :], AF.Square, r=[ck_], w=["OSQ"])
            ps, pk = pf()
            _mm(P, ps, ones[:, :], OSQ[:, :], True, True, r=["ones", "OSQ"], w=[pk])
            _act(P, RQ[:, :], ps, AF.Sqrt, r=[pk, "eps"], w=["RQ"], bias=epsT[:, 0:1])
            P.op("dve", lambda e: e.reciprocal(out=RQ[:, :], in_=RQ[:, :]), r=["RQ"], w=["RQ"])
            stt(C_[:, :], C_[:, :], scl, RQ[:, :], ALU.mult, ALU.mult, r=[ck_, "RQ"], w=[ck_])
        _act(P, GAb[:, :], GAb[:, :], AF.Exp, r=["GAb", "SC"], w=["GAb"], bias=SC[:, 1:2])
        _act(P, GAb[:, :], GAb[:, :], AF.Ln, r=["GAb", "one1"], w=["GAb"], bias=one1[:, 0:1])
        ts("dve", GAb[:, :], GAb[:, :], NEGA[:, 0:1], None, ALU.mult, None, r=["GAb", "NEGA"], w=["GAb"])
        P.op("dve", lambda e: e.tensor_tensor_scan(out=GC[:, :], data0=Seg[:, :], data1=GAb[:, :], initial=0.0,
                                                   op0=ALU.mult, op1=ALU.add), r=["Seg", "GAb"], w=["GC"])
        _act(P, EGC[:, :], GC[:, :], AF.Exp, r=["GC"], w=["EGC"])
        _act(P, BETA[:, :], GBb[:, :], AF.Sigmoid, r=["GBb"], w=["BETA"])
        tt("dve", QDb[:, :], CQ[:, :], EGC[:, :], ALU.mult, r=["CQ", "EGC"], w=["QDb"])
        cp("pool", KNb[:, :], CK[:, :], r=["CK"], w=["KNb"])
        cp("pool", QNb[:, :], CQ[:, :], r=["CQ"], w=["QNb"])
        cp("pool", CVb[:, :], CV[:, :], r=["CV"], w=["CVb"])

        if lvl == 2:
            return P
        for pi in range(ST // 128):
            pn = st_i * (ST // 128) + pi
            c0 = pi * 128
            csl = slice(c0, c0 + 128)
            if lvl == 3:
                return P
            ts("dve", EXD[:, :], GC[:, csl], GCC[:, pn:pn + 1], 0.0, ALU.subtract, ALU.min, r=["GC", "GCC"], w=["EXD"])
            _act(P, EXD[:, :], EXD[:, :], AF.Exp, r=["EXD"], w=["EXD"])
            tt("pool", DECI[:, :], EXD[:, :], Mincl[:, :], ALU.mult, r=["EXD", "Mincl"], w=["DECI"])
            tt("pool", DECS[:, :], EXD[:, :], Mstr[:, :], ALU.mult, r=["EXD", "Mstr"], w=["DECS"])
            tt("pool", DECS[:, :], DECS[:, :], BETA[:, csl], ALU.mult, r=["DECS", "BETA"], w=["DECS"])
            psk, pkk = pq()
            _mm(P, psk, KNb[:, csl], KNb[:, csl], True, True, r=["KNb"], w=[pkk])
            stt(ATf[:, :], psk, -1.0, DECS[:, :], ALU.mult, ALU.mult, r=[pkk, "DECS"], w=["ATf"])
            cp("act", ATb[:, :], ATf[:, :], r=["ATf"], w=["ATb"])
            psq, pkq = pq()
            _mm(P, psq, KNb[:, csl], QNb[:, csl], True, True, r=["KNb", "QNb"], w=[pkq])
            tt("dve", QKT[:, :], psq, DECI[:, :], ALU.mult, r=[pkq, "DECI"], w=["QKT"])
            pst_, pkt = ptq()
            tr(pst_, ATb[:, :], r=["ATb"], w=[pkt])
            cp("act", Ab[:, :], pst_, r=[pkt], w=["Ab"])
            tt("dve", TTf[:, :], ATf[:, :], Id[:, :], ALU.add, r=["ATf", "Id"], w=["TTf"])
            cp("act", TTb[:, :], TTf[:, :], r=["TTf"], w=["TTb"])
            Xc, XTc, xk, xtk = ATb, Ab, "ATb", "Ab"
            for k in range(1, 6):
                Xn, XTn = Pb[k % 2], PTb[k % 2]
                xnk, xtnk = "Pb%d" % (k % 2), "PTb%d" % (k % 2)
                p1, k1 = pq()
                _mm(P, p1, XTc[:, :], Xc[:, :], True, True, r=[xtk, xk], w=[k1])
                cp("act", Xn[:, :], p1, r=[k1], w=[xnk])
                if k < 5:
                    p2, k2 = pq()
                    _mm(P, p2, Xc[:, :], XTc[:, :], True, True, r=[xk, xtk], w=[k2])
                    cp("dve", XTn[:, :], p2, r=[k2], w=[xtnk])
                if k == 5:
                    p2, k2 = pq()
                    _mm(P, p2, Xc[:, :], XTc[:, :], True, True, r=[xk, xtk], w=[k2])
                    cp("dve", XTn[:, :], p2, r=[k2], w=[xtnk])
                p3, k3 = pq()
                _mm(P, p3, XTn[:, :], TTb[:, :], True, True, r=[xtnk, "TTb"], w=[k3])
                tt("dve", TTf[:, :], TTf[:, :], p3, ALU.add, r=["TTf", k3], w=["TTf"])
                cp("act", TTb[:, :], TTf[:, :], r=["TTf"], w=["TTb"])
                Xc, XTc, xk, xtk = Xn, XTn, xnk, xtnk
            if lvl == 4:
                return P
            pkt_, kkt = ptq()
            tr(pkt_, KNb[:, csl], r=["KNb"], w=[kkt])
            ts("dve", KBEt[:, :], pkt_, KBEC[:, pn:pn + 1], None, ALU.mult, None, r=[kkt, "KBEC"], w=["KBEt"])
            ts("dve", KDt[:, :], pkt_, KDC[:, pn:pn + 1], None, ALU.mult, None, r=[kkt, "KDC"], w=["KDt"])
            pvt_, kvt = ptq()
            tr(pvt_, CVb[:, csl], r=["CVb"], w=[kvt])
            ts("dve", VBt[:, :], pvt_, BEC[:, pn:pn + 1], None, ALU.mult, None, r=[kvt, "BEC"], w=["VBt"])
            pu, ku = pq()
            _mm(P, pu, TTb[:, :], VBt[:, :], True, True, r=["TTb", "VBt"], w=[ku])
            cp("act", Ut[:, :], pu, r=[ku], w=["Ut"])
            pw, kw = pq()
            _mm(P, pw, KBEt[:, :], TTb[:, :], True, True, r=["KBEt", "TTb"], w=[kw])
            cp("act", WTb[:, :], pw, r=[kw], w=["WTb"])
            phv, khv = ptq()
            tr(phv, HIb[:, csl], r=["HIb"], w=[khv])
            cp("act", HVp[:, :], phv, r=[khv], w=["HVp"])
            ps, pk = pq()
            _mm(P, ps, KBh[:, csl], QBh[:, csl], True, True, r=["KBh", "QBh"], w=[pk])
            tt("dve", AM[:, :], ps, Mincl[:, :], ALU.mult, r=[pk, "Mincl"], w=["AM"])
            if lvl == 20:
                return P
            for half in (0, 1):
                hs = slice(half * 64, half * 64 + 64)
                ts("dve", K2[:, hs], T1[:, c0 + half * 64:c0 + half * 64 + 64], T2[:, c0 + half * 64 + 63:c0 + half * 64 + 64], None,
                   ALU.mult, None, r=["T1", "T2"], w=["K2"])
            if lvl == 21:
                return P
            ps2, pk2 = ptq()
            tr(ps2, K2[:, :], r=["K2"], w=[pk2])
            if lvl == 22:
                return P
            cp("act", K2t[:, :], ps2, r=[pk2], w=["K2t"])
            if lvl == 30:
                return P
            pso, pko = pq()
            for half in ((0,) if lvl == 31 else (0, 1)):
                hs = slice(half * 64, half * 64 + 64)
                tsl_ = slice(c0 + half * 64, c0 + half * 64 + 64)
                _mm(P, pso[:, hs], HVp[hs, :], AM[hs, hs], True, False, r=["HVp", "AM"], w=[pko], inc=False)
                _mm(P, pso[:, hs], SAb[:, :], QBh[:, tsl_], False, True, r=["SAb", "QBh"], w=[pko])
                psu, pku = pq()
                _mm(P, psu, K2t[hs, :], HVp[hs, :], True, True, r=["K2t", "HVp"], w=[pku])
                stt(SAf[:, :], SAf[:, :], T2[:, c0 + half * 64 + 63:c0 + half * 64 + 64], psu, ALU.mult, ALU.add,
                    r=["SAf", "T2", pku], w=["SAf"])
                cp("act", SAb[:, :], SAf[:, :], r=["SAf"], w=["SAb"])
            if lvl in (31, 32):
                return P
            cp("act", OA[:, csl], pso, r=[pko], w=["O0"])
            if lvl == 5:
                return P
            psob, pkob = pq()
            for half in (0, 1):
                hs = slice(half * 64, half * 64 + 64)
                tsl_ = slice(c0 + half * 64, c0 + half * 64 + 64)
                pws, kws = pq()
                _mm(P, pws[hs, :], WTb[:, hs], SBb[:, :], True, True, r=["WTb", "SBb"], w=[kws])
                tt("dve", VN[hs, :], Ut[hs, :], pws[hs, :], ALU.subtract, r=["Ut", kws], w=["VN"])
                _mm(P, psob[:, hs], SBb[:, :], QDb[:, tsl_], True, False, r=["SBb", "QDb"], w=[pkob], inc=False)
                _mm(P, psob[:, hs], VN[hs, :], QKT[hs, hs], False, True, r=["VN", "QKT"], w=[pkob])
                pss, kss = pq()
                _mm(P, pss, KDt[hs, :], VN[hs, :], True, True, r=["KDt", "VN"], w=[kss])
                stt(SBf[:, :], SBf[:, :], EGC[:, c0 + half * 64 + 63:c0 + half * 64 + 64], pss, ALU.mult, ALU.add,
                    r=["SBf", "EGC", kss], w=["SBf"])
                cp("act", SBb[:, :], SBf[:, :], r=["SBf"], w=["SBb"])
            cp("act", OB[:, csl], psob, r=[pkob], w=["O1"])
        if lvl == 6:
            return P
        gated_out(OA, "HG", HG, HN, "HN", 0, t0)
        gated_out(OB, "GG", GG, GN, "GN", 1, t0)
        if "o_done" in io:
            io["o_done"](t0)


def build_mix(T, lvl=99):
    P = Prog()
    NP = T // 128
    hin = P.dram("hin", [3, 128, T], F32, "ExternalInput")
    hiT = P.dram("hiT", [128, T], F32, "ExternalInput")
    gin = P.dram("gin", [4, 128, T], F32, "ExternalInput")
    gab = P.dram("gab", [2, T], F32, "ExternalInput")
    out = P.dram("out", [2, 128, T], F32, "ExternalOutput")
    io = {"in_keys": []}
    for nm, shp in (("lbl", [128, 2]), ("hnorm", [128, 1]), ("convw", [128, 12]), ("sc", [128, 2]), ("gnorm", [128, 1]),
                    ("c_incl", [128, 128]), ("c_strict", [128, 128]), ("c_ident", [128, 128]), ("c_ltri", [128, 128]),
                    ("c_seg", [128, ST]), ("c_sel", [128, 128])):
        io[nm] = P.dram(nm, shp, F32, "ExternalInput")
    import os
    TR = int(os.environ.get('MIX_TR', min(2048, T)))

    def rows_(kind, r_, c0, n):
        src = {0: hin[0], 1: hin[1], 2: hiT, 3: hin[2], 4: gin[0], 5: gin[1], 6: gin[2], 7: gin[3]}[kind]
        return lambda e: src[:, r_ * TR + c0:r_ * TR + c0 + n]
    io["rows"] = rows_
    io["ab"] = lambda which, r_, c0, n: (lambda e: gab[which:which + 1, r_ * TR + c0:r_ * TR + c0 + n])
    io["o_dst"] = lambda oi, t0, n: out[oi, :, t0:t0 + n]
    P.begin_phase()
    mix_body(P, T, TR, io)
    P.end_phase()
    return P


def mix_core_inputs(projT, h, conv_w, a_log, dt_bias, lb_logits, hgrn_norm, gdn_norm):
    T = projT.shape[1]
    r = lambda base: projT[base + h * 128: base + (h + 1) * 128]
    d = {}
    d["hin"] = np.ascontiguousarray(np.stack([r(0), r(1024), r(3072)]))
    d["hiT"] = np.ascontiguousarray(r(2048))
    d["lbl"] = np.ascontiguousarray(lb_logits[:, h * 128:(h + 1) * 128].T)
    d["hnorm"] = np.ascontiguousarray(hgrn_norm.reshape(128, 1))
    d["gin"] = np.ascontiguousarray(np.stack([r(4096), r(5120), r(6144), r(7184)]))
    ga = projT[7168 + h]
    gb = projT[7176 + h]
    d["gab"] = np.ascontiguousarray(np.stack([ga, gb]))
    cw = np.concatenate([conv_w[:, h * 128:(h + 1) * 128].T, conv_w[:, 1024 + h * 128:1024 + (h + 1) * 128].T,
                         conv_w[:, 2048 + h * 128:2048 + (h + 1) * 128].T], axis=1)
    d["convw"] = np.ascontiguousarray(cw)
    d["sc"] = np.ascontiguousarray(np.stack([np.full(128, a_log[h], np.float32), np.full(128, dt_bias[h], np.float32)], axis=1))
    d["gnorm"] = np.ascontiguousarray(gdn_norm.reshape(128, 1))
    d.update(mix_consts())
    return d


import concourse.bass as _bass

NCORES = 8
RG = [list(range(NCORES))]


def _allgather(P, in_ap, out_ap, r, w):
    P.op("pool", lambda e, in_ap=in_ap, out_ap=out_ap: e.collective_compute(
        "AllGather", ALU.bypass, replica_groups=RG, ins=[in_ap], outs=[out_ap]), r=r, w=w, dsem="cc", dinc=1)


def _select(P, eng, dst_ap, src_fn, r, w):
    P.op(eng, lambda e, dst_ap=dst_ap, src_fn=src_fn: e.dma_start(out=dst_ap, in_=src_fn(e.partition_id())), r=r, w=w, dsem="sel")


def build_fused(SEQ):
    P = Prog()
    P.nc.cache_partition_id()
    D = D_MODEL
    TR = SEQ // NCORES
    ds = _bass.ds
    ext = lambda n, shp: P.dram(n, shp, F32, "ExternalInput")
    xT = ext("xT", [D, TR])
    w_in = ext("w_in", [D, 8208]); g0 = ext("g0", [128, KC])
    w_out0 = ext("w_out0", [D, D]); gm0 = ext("gm0", [128, KC]); w1_0 = ext("w1_0", [D, 4 * D]); w2_0 = ext("w2_0", [4 * D, D])
    g1 = ext("g1", [128, KC]); w_qkv = ext("w_qkv", [D, 6144])
    w_o1 = ext("w_o1", [D, D]); gm1 = ext("gm1", [128, KC]); w1_1 = ext("w1_1", [D, 4 * D]); w2_1 = ext("w2_1", [4 * D, D])
    gf = ext("gf", [128, KC])
    mio = {"in_keys": []}
    for nm, shp in (("lbl", [128, 2]), ("hnorm", [128, 1]), ("convw", [128, 12]), ("sc", [128, 2]), ("gnorm", [128, 1]),
                    ("c_incl", [128, 128]), ("c_strict", [128, 128]), ("c_ident", [128, 128]), ("c_ltri", [128, 128]),
                    ("c_seg", [128, ST]), ("c_sel", [128, 128])):
        mio[nm] = ext(nm, shp)
    sio = {"cU": ext("cU", [128, 128]), "cM": ext("cM", [4, 128, 512]), "cI": mio["c_ident"], "in_keys": []}
    yT = P.dram("yT", [D, TR], F32, "ExternalOutput")
    internal = lambda n, shp: P.dram(n, shp, F32, "Internal")
    shared = lambda n, shp: P.dram(n, shp, F32, "Internal", addr_space="Shared")
    ag1_in = internal("ag1_in", [8208, TR])
    ag1_out = [shared("ag1_out%d" % g, [NCORES * 1024, TR]) for g in range(8)] + [shared("ag1_out8", [NCORES * 16, TR])]
    sel1 = [internal("sel1_%d" % g, [NCORES * 128, TR]) for g in range(8)]
    sel1ab = internal("sel1ab", [2 * NCORES, TR])
    ag2_in = internal("ag2_in", [NCORES * 256, TR]); ag2_out = shared("ag2_out", [NCORES * NCORES * 256, TR])
    sel2 = internal("sel2", [NCORES * 256, TR])
    x1 = internal("x1", [D, TR])
    ag3_in = internal("ag3_in", [6144, TR])
    ag3_out = [shared("ag3_out%d" % g, [NCORES * 2048, TR]) for g in range(3)]
    sel3 = [internal("sel3_%d" % g, [NCORES * 256, TR]) for g in range(3)]
    ag4_in = internal("ag4_in", [16 * 128, TR]); ag4_out = shared("ag4_out", [16 * NCORES * 128, TR])
    sel4 = internal("sel4", [16 * 128, TR])

    P.begin_phase()
    deferred = []

    def p1_done(c_end, last):
        if not last:
            return
        if c_end % 1024 == 0 and c_end <= 8192:
            g = c_end // 1024 - 1
            _allgather(P, ag1_in[g * 1024:(g + 1) * 1024, :], ag1_out[g][:, :], r=["ag1_in"], w=[("ag1_out", g)])
            deferred.append(lambda g=g: _select(P, "sp", sel1[g].rearrange("(r p) t -> r p t", p=128),
                            lambda pid, g=g: ag1_out[g].rearrange("(r q) t -> r q t", q=1024)[:, ds(pid * 128, 128), :],
                            r=[("ag1_out", g)], w=["sel1"]))
        elif c_end == 8208:
            _allgather(P, ag1_in[8192:8208, :], ag1_out[8][:, :], r=["ag1_in"], w=[("ag1_out", 8)])
            for which in (0, 1):
                deferred.append(lambda which=which: _select(
                    P, "sp", sel1ab[which * NCORES:(which + 1) * NCORES, :].rearrange("(r o) t -> r o t", o=1),
                    lambda pid, which=which: ag1_out[8].rearrange("(r q) t -> r q t", q=16)[:, which * 8:which * 8 + 8, :][:, ds(pid, 1), :],
                    r=[("ag1_out", 8)], w=["sel1"]))
    proj_body(P, TR, 8208, {
        "x": lambda c, t0, n: xT[c * 128:(c + 1) * 128, t0:t0 + n], "g_p": g0, "w_p": w_in,
        "proj_dst": lambda r0, cw, tc, n: ag1_in[r0:r0 + cw, tc:tc + n], "proj_keys": ["ag1_in"], "proj_group_done": p1_done})
    for fn in deferred:
        fn()
    P.end_phase()

    P.begin_phase()
    mio["rows"] = lambda kind, r_, c0, n: (lambda e: sel1[kind][r_ * 128:(r_ + 1) * 128, c0:c0 + n])
    mio["ab"] = lambda which, r_, c0, n: (lambda e: sel1ab[which * NCORES + r_:which * NCORES + r_ + 1, c0:c0 + n])
    mio["abscr"] = sel1ab
    mio["o_dst"] = lambda oi, t0, n: ag2_in[(t0 // TR) * 256 + oi * 128:(t0 // TR) * 256 + (oi + 1) * 128, t0 % TR:t0 % TR + n]

    def p2_done(t0):
        if (t0 + ST) % TR == 0:
            j = t0 // TR
            _allgather(P, ag2_in[j * 256:(j + 1) * 256, :], ag2_out[j * 2048:(j + 1) * 2048, :], r=[("ag2_in", j)], w=["ag2_out"])
            if j == NCORES - 1:
                _select(P, "pool", sel2[:, :], lambda pid: ag2_out[ds(pid * 2048, 2048), :], r=["ag2_out"], w=["sel2"])
    mio["o_done"] = p2_done
    mio["o_wkeys"] = lambda t0: [("ag2_in", t0 // TR)]
    mix_body(P, SEQ, TR, mio)
    P.end_phase()

    P.begin_phase()
    deferred3 = []

    def p3_done(c_end, last):
        if last and c_end % 2048 == 0:
            g = c_end // 2048 - 1
            _allgather(P, ag3_in[g * 2048:(g + 1) * 2048, :], ag3_out[g][:, :], r=["ag3_in"], w=[("ag3_out", g)])
            deferred3.append(lambda g=g: _select(P, "pool", sel3[g].rearrange("(r p) t -> r p t", p=256),
                             lambda pid, g=g: ag3_out[g].rearrange("(r q) t -> r q t", q=2048)[:, ds(pid * 256, 256), :],
                             r=[("ag3_out", g)], w=["sel3"]))
    dense_body(P, TR, True, True, 6144, False, True, {
        "x": lambda c, t0, n: xT[c * 128:(c + 1) * 128, t0:t0 + n],
        "o": lambda c, t0, n: (lambda e: sel2[(c % 8) * 256 + (c // 8) * 128:(c % 8) * 256 + (c // 8) * 128 + 128, t0:t0 + n]),
        "w_o": w_out0, "g_mlp": gm0, "w1": w1_0, "w2": w2_0, "g_p": g1, "w_p": w_qkv,
        "x_dst": lambda c, t0, n: x1[c * 128:(c + 1) * 128, t0:t0 + n],
        "proj_dst": lambda r0, cw, tc, n: ag3_in[r0:r0 + cw, tc:tc + n], "proj_keys": ["ag3_in"], "proj_group_done": p3_done})
    for fn in deferred3:
        fn()
    P.end_phase()

    P.begin_phase()
    sio["q"] = lambda h, r_: (lambda e: sel3[0][r_ * 256 + h * 128:r_ * 256 + (h + 1) * 128, :])
    sio["k"] = lambda h, r_: (lambda e: sel3[1][r_ * 256 + h * 128:r_ * 256 + (h + 1) * 128, :])
    sio["v"] = lambda h, r_: (lambda e: sel3[2][r_ * 256 + h * 128:r_ * 256 + (h + 1) * 128, :])
    sio["o_dst"] = lambda h, q0, n: ag4_in[(h * 8 + q0 // TR) * 128:(h * 8 + q0 // TR + 1) * 128, q0 % TR:q0 % TR + n]

    def p4_done(h, qt):
        q0 = qt * 512
        if (q0 + 512) % TR == 0:
            g = h * 8 + q0 // TR
            _allgather(P, ag4_in[g * 128:(g + 1) * 128, :], ag4_out[g * 1024:(g + 1) * 1024, :],
                       r=[("ag4_in", g)], w=["ag4_out"])
            if g == 15:
                _select(P, "pool", sel4.rearrange("(h q) t -> h q t", h=2),
                        lambda pid: ag4_out.rearrange("(h j q) t -> h j q t", h=2, j=8)[:, ds(pid, 1), :, :]
                        .rearrange("h o q t -> h (o q) t"),
                        r=["ag4_out"], w=["sel4"])
    sio["o_done"] = p4_done
    sio["o_wkeys"] = lambda h, q0: [("ag4_in", h * 8 + q0 // TR)]
    sb_body(P, SEQ, 2, TR, sio)
    P.end_phase()

    P.begin_phase()
    dense_body(P, TR, True, True, 0, True, False, {
        "x": lambda c, t0, n: x1[c * 128:(c + 1) * 128, t0:t0 + n],
        "o": lambda c, t0, n: (lambda e: sel4[(c % 2) * 1024 + (c // 2) * 128:(c % 2) * 1024 + (c // 2) * 128 + 128, t0:t0 + n]),
        "w_o": w_o1, "g_mlp": gm1, "w1": w1_1, "w2": w2_1, "g_f": gf,
        "y_dst": lambda c, tc, n: yT[c * 128:(c + 1) * 128, tc:tc + n]})
    P.end_phase()
    return P


def _gl(g):
    return np.ascontiguousarray(np.asarray(g, np.float32).reshape(KC, 128).T)


def fused_inputs(SEQ, x, mix_norm, a_w_in, a_conv_w, a_a_log, a_dt_bias, a_lb_logits, a_hgrn_norm,
                 a_gdn_norm, a_w_out, c_w_qkv, c_w_o, mlp_norm, mlp_w1, mlp_w2, final_norm):
    f = lambda a: np.ascontiguousarray(np.asarray(a, dtype=np.float32))
    TR = SEQ // NCORES
    xT = f(x)[0].T
    w_in = f(a_w_in[0])
    w_in = np.ascontiguousarray(np.concatenate([w_in[:, 0:7168], w_in[:, 7184:8208], w_in[:, 7168:7184]], axis=1))
    common = {"w_in": w_in, "g0": _gl(mix_norm[0]), "w_out0": f(a_w_out[0]), "gm0": _gl(mlp_norm[0]),
              "w1_0": f(mlp_w1[0]), "w2_0": f(mlp_w2[0]), "g1": _gl(mix_norm[1]), "w_qkv": f(c_w_qkv[0]),
              "w_o1": f(c_w_o[0]), "gm1": _gl(mlp_norm[1]), "w1_1": f(mlp_w1[1]), "w2_1": f(mlp_w2[1]), "gf": _gl(final_norm)}
    common.update(mix_consts())
    sbc = sb_consts()
    common["cU"] = sbc["cU"]; common["cM"] = sbc["cM"]
    conv_w = f(a_conv_w[0]); a_log = f(a_a_log[0]); dt_bias = f(a_dt_bias[0]); lbl = f(a_lb_logits)
    ins = []
    for h in range(NCORES):
        d = dict(common)
        d["xT"] = np.ascontiguousarray(xT[:, h * TR:(h + 1) * TR])
        d["lbl"] = np.ascontiguousarray(lbl[:, h * 128:(h + 1) * 128].T)
        d["hnorm"] = f(a_hgrn_norm[0]).reshape(128, 1)
        d["gnorm"] = f(a_gdn_norm[0]).reshape(128, 1)
        d["convw"] = np.ascontiguousarray(np.concatenate(
            [conv_w[:, h * 128:(h + 1) * 128].T, conv_w[:, 1024 + h * 128:1024 + (h + 1) * 128].T,
             conv_w[:, 2048 + h * 128:2048 + (h + 1) * 128].T], axis=1))
        d["sc"] = np.ascontiguousarray(np.stack([np.full(128, a_log[h], np.float32), np.full(128, dt_bias[h], np.float32)], axis=1))
        ins.append(d)
    return ins


def kernel(x, mix_norm, a_w_in, a_conv_w, a_a_log, a_dt_bias, a_lb_logits, a_hgrn_norm,
           a_gdn_norm, a_w_out, c_w_qkv, c_w_o, mlp_norm, mlp_w1, mlp_w2, final_norm):
    SEQ = np.asarray(x).shape[1]
    P = build_fused(SEQ)
    nc = P.finish()
    ins = fused_inputs(SEQ, x, mix_norm, a_w_in, a_conv_w, a_a_log, a_dt_bias, a_lb_logits, a_hgrn_norm,
                       a_gdn_norm, a_w_out, c_w_qkv, c_w_o, mlp_norm, mlp_w1, mlp_w2, final_norm)
    res = run_bass_kernel_spmd(nc, ins, core_ids=list(range(NCORES)))
    yT = np.concatenate([r["yT"] for r in res.results], axis=1)
    return np.ascontiguousarray(yT.T)[None].astype(np.float32)
```
